# Optimizing a Trainium2 kernel written in Bass

```python
import math
import jax
import jax.numpy as jnp
from jax import lax
import numpy as np

D_MODEL = 2048
BATCH = 4
SEQ = 2048
DEPTH = 2

CTX_LEN = 256
GRID_W = 64
D_MIX = D_MODEL
ATT_WIDTH = D_MIX // 2
POOL_WIDTH = D_MIX // 4
CONV_WIDTH = D_MIX - ATT_WIDTH - POOL_WIDTH
ATT_HEADS = 8
ATT_HD = ATT_WIDTH // (2 * ATT_HEADS)
ATT_VD = 2 * ATT_HD
POOL_WINDOWS = (2, 4, 8, 16)
POOL_GROUPS = len(POOL_WINDOWS)
POOL_GD = POOL_WIDTH // POOL_GROUPS
CONV_K = 31
Q_BLOCK = 128
ROPE_BASE = 10000.0
EPS = 1e-6
SPLITS = (ATT_WIDTH, 2 * ATT_WIDTH, 3 * ATT_WIDTH, 4 * ATT_WIDTH,
          4 * ATT_WIDTH + POOL_WIDTH, 4 * ATT_WIDTH + 2 * POOL_WIDTH,
          4 * ATT_WIDTH + 2 * POOL_WIDTH + CONV_WIDTH,
          4 * ATT_WIDTH + 2 * POOL_WIDTH + 2 * CONV_WIDTH)
N_IN = 4 * ATT_WIDTH + 2 * POOL_WIDTH + 3 * CONV_WIDTH

kernel_name = "hybrid_pool_diffattn_conformer_dit_block"


def _rmsnorm(x, g):
    xf = x.astype(jnp.float32)
    y = xf * lax.rsqrt(jnp.mean(xf * xf, axis=-1, keepdims=True) + EPS)
    return (y * g.astype(jnp.float32)).astype(x.dtype)


def _layernorm(x, g, b):
    xf = x.astype(jnp.float32)
    mu = jnp.mean(xf, axis=-1, keepdims=True)
    var = jnp.mean(jnp.square(xf - mu), axis=-1, keepdims=True)
    y = (xf - mu) * lax.rsqrt(var + EPS) * g.astype(jnp.float32) + b.astype(jnp.float32)
    return y.astype(x.dtype)


def _heads_qk(t):
    return t.reshape(t.shape[0], t.shape[1], ATT_HEADS, 2, ATT_HD)


def _heads_v(t):
    return t.reshape(t.shape[0], t.shape[1], ATT_HEADS, ATT_VD)


def _axial_rope(t, row, col):
    n_freq = ATT_HD // 4
    inv_freq = ROPE_BASE ** (-jnp.arange(n_freq, dtype=jnp.float32) / n_freq)

    def rot(u, pos):
        ang = pos.astype(jnp.float32)[:, None] * inv_freq[None, :]
        cos = jnp.concatenate([jnp.cos(ang)] * 2, axis=-1)[None, :, None, None, :]
        sin = jnp.concatenate([jnp.sin(ang)] * 2, axis=-1)[None, :, None, None, :]
        u1, u2 = jnp.split(u, 2, axis=-1)
        rh = jnp.concatenate([-u2, u1], axis=-1)
        return (u.astype(jnp.float32) * cos + rh.astype(jnp.float32) * sin).astype(u.dtype)

    t_row, t_col = jnp.split(t, 2, axis=-1)
    return jnp.concatenate([rot(t_row, row), rot(t_col, col)], axis=-1)


def _diff_attention(q, k, v, lam):
    B, Lq = q.shape[0], q.shape[1]
    nb = Lq // Q_BLOCK
    qb = jnp.moveaxis(q.reshape(B, nb, Q_BLOCK, ATT_HEADS, 2, ATT_HD), 1, 0)
    scale = ATT_HD ** -0.5

    def block(qq):
        s = jnp.einsum('bqhcd,bkhcd->bhcqk', qq, k).astype(jnp.float32) * scale
        p = jax.nn.softmax(s, axis=-1)
        a = p[:, :, 0] - lam * p[:, :, 1]
        return jnp.einsum('bhqk,bkhe->bqhe', a.astype(v.dtype), v)

    o = lax.map(block, qb)
    return jnp.moveaxis(o, 0, 1).reshape(B, Lq, ATT_HEADS, ATT_VD)


def _multiscale_pool(u, w_pool, pool_scale):
    B, L, _ = u.shape
    ug = u.reshape(B, L, POOL_GROUPS, POOL_GD)
    cs = jnp.concatenate([jnp.zeros((B, 1, POOL_GROUPS, POOL_GD), jnp.float32),
                          jnp.cumsum(ug.astype(jnp.float32), axis=1)], axis=1)
    t = jnp.arange(L, dtype=jnp.int32)[:, None]
    halfw = jnp.array(POOL_WINDOWS, dtype=jnp.int32)[None, :] // 2
    lo = jnp.clip(t - halfw, 0, L)
    hi = jnp.clip(t + halfw, 0, L)
    gi = jnp.arange(POOL_GROUPS, dtype=jnp.int32)[None, :]
    win_sum = cs[:, hi, gi] - cs[:, lo, gi]
    mean = win_sum / (hi - lo).astype(jnp.float32)[None, :, :, None]
    d = (mean - ug.astype(jnp.float32)).astype(u.dtype)
    y = jnp.einsum('blgc,gcd->blgd', d, w_pool) * pool_scale.reshape(POOL_GROUPS, POOL_GD)
    return y.reshape(B, L, POOL_WIDTH)


def _conformer_conv(a, b, w_dw, b_dw, ln_g, ln_b, w_pw2):
    u = a * jax.nn.sigmoid(b)
    y = lax.conv_general_dilated(u, w_dw[:, None, :].astype(u.dtype), window_strides=(1,),
                                 padding=[(CONV_K // 2, CONV_K // 2)],
                                 dimension_numbers=('NWC', 'WIO', 'NWC'),
                                 feature_group_count=CONV_WIDTH) + b_dw
    y = jax.nn.silu(_layernorm(y, ln_g, ln_b))
    return y @ w_pw2


def _mix(q, k, v, g_att, u_pool, g_pool, a_conv, b_conv, g_conv, lam, lam_init,
         subln_g, w_pool, pool_scale, w_dw, b_dw, ln_g, ln_b, w_pw2, w_out):
    B, L = q.shape[0], q.shape[1]
    o = _diff_attention(q, k, v, lam)
    y_att = (_rmsnorm(o, subln_g) * (1.0 - lam_init)).reshape(B, L, ATT_WIDTH)
    y_pool = _multiscale_pool(u_pool, w_pool, pool_scale)
    y_conv = _conformer_conv(a_conv, b_conv, w_dw, b_dw, ln_g, ln_b, w_pw2)
    y = jnp.concatenate([y_att * jax.nn.silu(g_att),
                         y_pool * jax.nn.silu(g_pool),
                         y_conv * jax.nn.silu(g_conv)], axis=-1)
    return y @ w_out


def setup_inputs(seed: int = 0) -> dict:
    key = jax.random.key(seed)
    ks = jax.random.split(key, 24)
    f32 = jnp.float32
    n = lambda k, s, sc: jax.random.normal(k, s, f32) * sc
    return {
        "x": n(ks[0], (BATCH, SEQ, D_MODEL), 1.0),
        "c": n(ks[1], (BATCH, D_MODEL), 1.0),
        "ctx": n(ks[2], (BATCH, CTX_LEN, D_MODEL), 1.0),
        "c_ctx": n(ks[3], (D_MODEL,), 1.0),
        "w_mod": n(ks[4], (DEPTH, D_MODEL, 3 * D_MODEL), 0.5 * D_MODEL ** -0.5),
        "b_mod": n(ks[5], (DEPTH, 3 * D_MODEL), 0.02),
        "norm_g": 1.0 + n(ks[6], (DEPTH, D_MODEL), 0.05),
        "w_in": n(ks[7], (DEPTH, D_MODEL, N_IN), D_MODEL ** -0.5),
        "lambda_q1": n(ks[8], (DEPTH, ATT_HD), 0.1),
        "lambda_k1": n(ks[9], (DEPTH, ATT_HD), 0.1),
        "lambda_q2": n(ks[10], (DEPTH, ATT_HD), 0.1),
        "lambda_k2": n(ks[11], (DEPTH, ATT_HD), 0.1),
        "subln_g": 1.0 + n(ks[12], (DEPTH, ATT_VD), 0.05),
        "w_pool": n(ks[13], (DEPTH, POOL_GROUPS, POOL_GD, POOL_GD), POOL_GD ** -0.5),
        "pool_scale": 1.0 + n(ks[14], (DEPTH, POOL_WIDTH), 0.05),
        "w_dw": n(ks[15], (DEPTH, CONV_K, CONV_WIDTH), CONV_K ** -0.5),
        "b_dw": n(ks[16], (DEPTH, CONV_WIDTH), 0.02),
        "conv_ln_g": 1.0 + n(ks[17], (DEPTH, CONV_WIDTH), 0.05),
        "conv_ln_b": n(ks[18], (DEPTH, CONV_WIDTH), 0.02),
        "w_pw2": n(ks[19], (DEPTH, CONV_WIDTH, CONV_WIDTH), CONV_WIDTH ** -0.5),
        "w_out": n(ks[20], (DEPTH, D_MIX, D_MODEL), D_MIX ** -0.5),
        "final_g": 1.0 + n(ks[21], (D_MODEL,), 0.05),
    }


def reference(x, c, ctx, c_ctx, w_mod, b_mod, norm_g, w_in, lambda_q1, lambda_k1, lambda_q2, lambda_k2,
              subln_g, w_pool, pool_scale, w_dw, b_dw, conv_ln_g, conv_ln_b, w_pw2, w_out, final_g):
    B, L, _ = x.shape
    rows = L // GRID_W
    row = jnp.repeat(jnp.arange(rows, dtype=jnp.int32), GRID_W)
    col = jnp.tile(jnp.arange(GRID_W, dtype=jnp.int32), rows)
    s_lat = jax.nn.silu(c)
    s_ctx = jax.nn.silu(c_ctx)
    for l in range(DEPTH):
        last = l == DEPTH - 1
        lam_init = 0.8 - 0.6 * math.exp(-0.3 * l)
        lam = (jnp.exp(jnp.sum(lambda_q1[l].astype(jnp.float32) * lambda_k1[l].astype(jnp.float32)))
               - jnp.exp(jnp.sum(lambda_q2[l].astype(jnp.float32) * lambda_k2[l].astype(jnp.float32)))
               + lam_init)
        shift, scale, gate = jnp.split((s_lat @ w_mod[l] + b_mod[l])[:, None, :], 3, axis=-1)
        shift_c, scale_c, gate_c = jnp.split(s_ctx @ w_mod[l] + b_mod[l], 3, axis=-1)
        hx = _rmsnorm(x, norm_g[l]) * (1.0 + scale) + shift
        hc = _rmsnorm(ctx, norm_g[l]) * (1.0 + scale_c) + shift_c

        q, k, v, g_att, u_pool, g_pool, a_conv, b_conv, g_conv = jnp.split(hx @ w_in[l], SPLITS, axis=-1)
        q = _axial_rope(_heads_qk(q), row, col)
        k = _axial_rope(_heads_qk(k), row, col)
        v = _heads_v(v)
        if last:
            k_c, v_c = jnp.split(hc @ w_in[l][:, ATT_WIDTH:3 * ATT_WIDTH], 2, axis=-1)
        else:
            (q_c, k_c, v_c, g_att_c, u_pool_c, g_pool_c,
             a_conv_c, b_conv_c, g_conv_c) = jnp.split(hc @ w_in[l], SPLITS, axis=-1)
        k_c = _heads_qk(k_c)
        v_c = _heads_v(v_c)

        lw = (subln_g[l], w_pool[l], pool_scale[l], w_dw[l], b_dw[l],
              conv_ln_g[l], conv_ln_b[l], w_pw2[l], w_out[l])
        y = _mix(q, jnp.concatenate([k_c, k], axis=1), jnp.concatenate([v_c, v], axis=1),
                 g_att, u_pool, g_pool, a_conv, b_conv, g_conv, lam, lam_init, *lw)
        if not last:
            y_c = _mix(_heads_qk(q_c), k_c, v_c, g_att_c, u_pool_c, g_pool_c,
                       a_conv_c, b_conv_c, g_conv_c, lam, lam_init, *lw)
            ctx = ctx + gate_c * y_c
        x = x + gate * y
    return _rmsnorm(x, final_g)
```

```python
import math
import os
import numpy as np
import ml_dtypes
from contextlib import ExitStack
import concourse.bass as bass
import concourse.mybir as mybir
from concourse.bass_utils import run_bass_kernel_spmd

F32 = mybir.dt.float32
BF16 = mybir.dt.bfloat16
AF = mybir.ActivationFunctionType
ALU = mybir.AluOpType
AX = mybir.AxisListType

ENGS = ("pe", "act", "dve", "pool", "sp")
D = 2048
NIN = 6656
EPS = 1e-6
NCORES = 8


class Op:
    __slots__ = ("eng", "fn", "dma", "deps_hard", "deps_war", "needs_inc", "inc_idx",
                 "dma_cnt", "waits", "idx")

    def __init__(self, eng, fn, dma):
        self.eng = eng
        self.fn = fn
        self.dma = dma
        self.deps_hard = set()
        self.deps_war = set()
        self.needs_inc = False
        self.inc_idx = 0
        self.dma_cnt = 0
        self.waits = []


class Prog:
    def __init__(self, nc, es):
        self.nc = nc
        self.es = es
        self.ops = []
        self.last_w = {}
        self.readers = {}
        self.dma_count = {}
        self.nt = 0

    def add(self, eng, fn, r=(), w=(), dma=None):
        op = Op(eng, fn, dma)
        op.idx = len(self.ops)
        for k in r:
            lw = self.last_w.get(k)
            if lw is not None:
                op.deps_hard.add(lw)
        for k in w:
            lw = self.last_w.get(k)
            if lw is not None:
                op.deps_hard.add(lw)
            rd = self.readers.get(k)
            if rd:
                for o in rd.values():
                    op.deps_war.add(o)
        ent = ("dma", dma) if dma else ("eng", eng)
        for k in r:
            if isinstance(k, tuple) and k and k[0] == "ps":
                rd = self.readers.get(k)
                if rd:
                    for ent2, o in rd.items():
                        if ent2 != ent:
                            op.deps_hard.add(o)
        if fn is not None:
            for k in r:
                self.readers.setdefault(k, {})[ent] = op.idx
            for k in w:
                self.last_w[k] = op.idx
                self.readers[k] = {}
        if dma:
            self.dma_count[dma] = self.dma_count.get(dma, 0) + 1
            op.dma_cnt = self.dma_count[dma]
        self.ops.append(op)
        return op

    def barrier(self):
        allk = list(dict.fromkeys(list(self.last_w.keys()) + list(self.readers.keys())))
        for e in ENGS:
            self.add(e, None, r=allk, w=allk)

    def resolve(self):
        ops = self.ops
        for op in ops:
            need = set()
            for d in op.deps_hard:
                po = ops[d]
                if po.dma:
                    need.add(d)
                elif po.eng == op.eng and not op.dma and op.eng == "pe":
                    continue
                else:
                    need.add(d)
            for d in op.deps_war:
                po = ops[d]
                if po.dma:
                    need.add(d)
                elif po.eng == op.eng and op.eng == "pe" and not op.dma:
                    continue
                else:
                    need.add(d)
            op.waits = need
            for d in need:
                if not ops[d].dma:
                    ops[d].needs_inc = True
        cnt = {e: 0 for e in ENGS}
        for op in ops:
            if op.needs_inc:
                assert op.fn is not None
                cnt[op.eng] += 1
                op.inc_idx = cnt[op.eng]
        self.eng_total = cnt
        waited = {e: {} for e in ENGS}
        for op in ops:
            wl = {}
            for d in op.waits:
                po = ops[d]
                if po.dma:
                    key = ("dma", po.dma)
                    val = 16 * po.dma_cnt
                else:
                    key = ("eng", po.eng)
                    val = po.inc_idx
                if val > wl.get(key, 0):
                    wl[key] = val
            out = []
            for key, val in wl.items():
                if waited[op.eng].get(key, 0) >= val:
                    continue
                waited[op.eng][key] = val
                out.append((key, val))
            op.waits = out

    def emit(self):
        nc = self.nc
        self.resolve()
        sems = {}
        for e in ENGS:
            sems[("eng", e)] = self.es.enter_context(nc.semaphore(f"s_{e}"))
        for name in self.dma_count:
            sems[("dma", name)] = self.es.enter_context(nc.semaphore(f"d_{name}"))
        per = {e: [op for op in self.ops if op.eng == e] for e in ENGS}

        def run(engname, eng):
            for op in per[engname]:
                for key, val in op.waits:
                    eng.wait_ge(sems[key], val)
                if op.fn is None:
                    continue
                ins = op.fn(eng)
                if op.dma:
                    ins.then_inc(sems[("dma", op.dma)], 16)
                elif op.needs_inc:
                    ins.then_inc(sems[("eng", engname)], 1)

        with nc.Block() as block:
            @block.tensor
            def _(e):
                run("pe", e)

            @block.scalar
            def _(e):
                run("act", e)

            @block.vector
            def _(e):
                run("dve", e)

            @block.gpsimd
            def _(e):
                run("pool", e)

            @block.sync
            def _(e):
                run("sp", e)


class Builder:
    def __init__(self, layers, fused, debug=False, stop=None):
        self.stop = stop
        self.layers = layers
        self.fused = fused
        self.debug = debug
        self.nc = bass.Bass("TRN2", target_bir_lowering=False)
        self.din = {}
        self.dout = {}

    def inp(self, name, shape, dt=F32):
        if name in self.din:
            return self.din[name]
        t = self.nc.dram_tensor(name, list(shape), dt, kind="ExternalInput").ap()
        self.din[name] = t
        return t

    def outp(self, name, shape, dt=F32):
        t = self.nc.dram_tensor(name, list(shape), dt, kind="ExternalOutput").ap()
        self.dout[name] = t
        return t

    def scratch(self, name, shape, dt):
        if self.debug:
            return self.outp(name, shape, dt)
        return self.nc.dram_tensor(name, list(shape), dt, kind="Internal").ap()

    def scope(self):
        b = self

        class _S:
            def __enter__(self_):
                self_.mark = b.off
                return self_

            def __exit__(self_, *a):
                b.off = self_.mark
                return False
        return _S()

    def sb(self, es, shape, dt, name=None):
        esz = 4 if dt == F32 else 2
        n = 1
        for d_ in shape[1:]:
            n *= d_
        nbytes = (n * esz + 255) // 256 * 256
        off = self.off
        self.off += nbytes
        self.peak = max(self.peak, self.off)
        assert self.off <= self.BIGBYTES, f"SBUF overflow {self.off}"
        v = self.big[:, off // 2:(off + n * esz) // 2]
        if dt == F32:
            v = v.bitcast(F32)
        if len(shape) == 3:
            v = v.rearrange("p (a b) -> p a b", a=shape[1])
        elif len(shape) == 4:
            v = v.rearrange("p (a b c) -> p a b c", a=shape[1], b=shape[2])
        return v

    def dma(self, q, out, in_, r, w, sem):
        self.P.add(q, lambda e, o=out, i=in_: e.dma_start(out=o, in_=i), r=r, w=w, dma=sem)

    def mm(self, ps, lhsT, rhs, start, stop, r, w):
        self.P.add("pe", lambda e, a=ps, b=lhsT, c=rhs, s0=start, s1=stop:
                   e.matmul(a, lhsT=b, rhs=c, start=s0, stop=s1), r=r, w=w)

    def tr(self, ps, in_, ident, r, w):
        self.P.add("pe", lambda e, a=ps, b=in_, c=ident: e.transpose(a, b, c), r=r, w=w)

    def act(self, out, in_, func, r, w, **kw):
        self.P.add("act", lambda e, o=out, i=in_, f=func, k=kw: e.activation(out=o, in_=i, func=f, **k), r=r, w=w)

    def tt(self, eng, out, in0, in1, op, r, w):
        self.P.add(eng, lambda e, o=out, a=in0, b=in1, p=op: e.tensor_tensor(out=o, in0=a, in1=b, op=p), r=r, w=w)

    def ts(self, eng, out, in0, s1, s2, op0, op1, r, w):
        self.P.add(eng, lambda e, o=out, a=in0, x=s1, y=s2, p=op0, q=op1:
                   e.tensor_scalar(out=o, in0=a, scalar1=x, scalar2=y, op0=p, op1=q), r=r, w=w)

    def tsm(self, eng, out, in0, s1, r, w):
        self.P.add(eng, lambda e, o=out, a=in0, x=s1: e.tensor_scalar_mul(out=o, in0=a, scalar1=x), r=r, w=w)

    def stt(self, eng, out, in0, scalar, in1, op0, op1, r, w, accum_out=None):
        if accum_out is None:
            self.P.add(eng, lambda e, o=out, a=in0, s=scalar, b=in1, p=op0, q=op1:
                       e.scalar_tensor_tensor(out=o, in0=a, scalar=s, in1=b, op0=p, op1=q), r=r, w=w)
        else:
            self.P.add(eng, lambda e, o=out, a=in0, s=scalar, b=in1, p=op0, q=op1, ac=accum_out:
                       e.scalar_tensor_tensor(out=o, in0=a, scalar=s, in1=b, op0=p, op1=q, accum_out=ac), r=r, w=w)

    def cp(self, eng, out, in_, r, w):
        if eng == "act":
            self.P.add("act", lambda e, o=out, i=in_: e.copy(out=o, in_=i), r=r, w=w)
        else:
            self.P.add(eng, lambda e, o=out, i=in_: e.tensor_copy(out=o, in_=i), r=r, w=w)

    def memset(self, eng, ap, val, w):
        self.P.add(eng, lambda e, a=ap, v=val: e.memset(a, v), w=w)

    def recip(self, out, in_, r, w):
        self.P.add("dve", lambda e, o=out, i=in_: e.reciprocal(out=o, in_=i), r=r, w=w)

    def wload(self, view, c0, ncols=512):
        i = self.wcur
        self.wcur ^= 1
        self.dma("pool", self.wb[i][:, :, 0:ncols], view[:, :, c0:c0 + ncols], r=[], w=[("wb", i)], sem=f"wb{i}")
        return i

    def bank(self):
        i = self.gb[self.gbi % len(self.gb)]
        self.gbi += 1
        return i

    def mm16(self, bi, n, lhs_fn, rhs_fn, r):
        for kc in range(16):
            self.mm(self.B[bi][:, 0:n], lhs_fn(kc), rhs_fn(kc), kc == 0, kc == 15, r=r, w=[("ps", bi)])

    def hxkeys(self, tok0, n):
        return [("hx", a) for a in range(tok0 // 128, (tok0 + n + 127) // 128)]

    def build(self):
        nc = self.nc
        with ExitStack() as es:
            self.P = Prog(nc, es)
            P = self.P
            ident_d = self.inp("ident", [128, 128])
            rotm_d = self.inp("rotm", [128, 128])
            cosk_d = self.inp("cosk", [128, 2048])
            sink_d = self.inp("sink", [128, 2048])
            msk_d = self.inp("msk", [128, 4])
            cT_d = self.inp("cT", [128, 32])
            self.cosk_d, self.sink_d = cosk_d, sink_d

            self.BIGBYTES = 206 * 1024
            self.off = 128
            self.peak = 0
            self.big = es.enter_context(nc.sbuf_tensor("big", [128, self.BIGBYTES // 2], BF16))
            self.ident_f = self.sb(es, [128, 128], F32)
            if True:
                self.rotm_b = self.sb(es, [128, 128], BF16)
                self.ident_b = self.sb(es, [128, 128], BF16)
            else:
                self.ident_b = self.sb(es, [128, 128], BF16)
                self.rotm_b = self.sb(es, [128, 128], BF16)
            self.ones_b = self.sb(es, [128, 128], BF16)
            self.ones_f = self.sb(es, [128, 128], F32)
            self.msk = self.sb(es, [128, 4], F32)
            self.cT = self.sb(es, [128, 32], F32)
            self.hxA = self.sb(es, [128, 16, 2304], BF16)
            self.wb = [self.sb(es, [128, 16, 512], BF16) for _ in range(2)]
            self.wcur = 0
            self.B = [es.enter_context(nc.psum_tensor(f"bank{i}", [128, 512], F32)) for i in range(8)]
            self.gb = [0, 1, 2, 3]
            self.gbi = 0

            self.dma("sp", self.ident_f[:], ident_d, r=[], w=["ident_f"], sem="c0")
            self.dma("pool", self.ident_b[:], ident_d, r=[], w=["ident_b"], sem="c1")
            self.dma("pool", self.rotm_b[:], rotm_d, r=[], w=["rotm_b"], sem="c2")
            self.dma("sp", self.msk[:], msk_d, r=[], w=["msk"], sem="c3")
            self.dma("sp", self.cT[:], cT_d, r=[], w=["cT"], sem="c4")
            self.memset("dve", self.ones_b[:], 1.0, w=["ones_b"])
            self.memset("dve", self.ones_f[:], 1.0, w=["ones_f"])

            self.modsave = {l_: ([self.sb(es, [128, 32], F32) for _ in range(2)],
                                 [self.sb(es, [128, 16], F32) for _ in range(2)]) for l_ in self.layers}
            self.kT_d = self.scratch("kT_d", [8, 128, 2304], BF16)
            self.v_d = self.scratch("v_d", [8, 128, 18, 132], BF16)

            if not self.fused:
                l = self.layers[0]
                last = (l == 1)
                src = {"own": self.inp("x_own", [1024, D]), "oth": self.inp("x_oth", [1024, D]),
                       "ctx": self.inp("ctx_in", [256, D]), "blend": None}
                if last:
                    xdst = self.scratch("x2_own", [1024, D], F32)
                    cdst = None
                    ydst = self.outp("y", [1024, D])
                else:
                    xdst = self.outp("x1_own", [1024, D])
                    cdst = self.outp("ctx1", [256, D])
                    ydst = None
                self.layer(l, last, src, xdst, cdst, ydst)
            else:
                x_own = self.inp("x_own", [1024, D])
                x_oth = self.inp("x_oth", [1024, D])
                ctx_in = self.inp("ctx_in", [256, D])
                x1_own = nc.dram_tensor("x1_own", [1024, D], F32, kind="Internal").ap()
                x1_oth = nc.dram_tensor("x1_oth", [1024, D], F32, kind="Internal").ap()
                ctx1 = nc.dram_tensor("ctx1", [256, D], F32, kind="Internal").ap()
                x2_own = nc.dram_tensor("x2_own", [1024, D], F32, kind="Internal").ap()
                ydst = self.outp("y", [1024, D])
                self.layer(0, False, {"own": x_own, "oth": x_oth, "ctx": ctx_in, "blend": None}, x1_own, ctx1, None,
                           mode="full", L="L0", xkey="x1own")
                self.layer(0, False, {"own": x_oth, "oth": x_own, "ctx": ctx_in, "blend": None}, x1_oth, None, None,
                           mode="B", L="L0B", xkey="x1oth")
                self.layer(1, True, {"own": x1_own, "oth": x1_oth, "ctx": ctx1, "blend": None, "own_key": "x1own",
                                     "oth_key": "x1oth"}, x2_own, None, ydst, mode="full", L="L1", xkey="x2own")
            P.add("sp", None, r=list(self.final_keys))
            P.emit()
        return nc

    def layer(self, l, last, src, xdst, cdst, ydst, mode="full", L=None, xkey="xdst"):
        nc, P, B = self.nc, self.P, self.B
        hxA = self.hxA
        lam_init = 0.8 - 0.6 * math.exp(-0.3 * l)
        has_ctx = (not last) and mode == "full"
        passB = mode == "B"
        mL, mR = (1, 0) if passB else (0, 1)
        qoff = 1024 if passB else 0
        with self.scope() as les:
            wmod = self.inp(f"wmod{l}", [D, 3 * D]).rearrange("(kc p) n -> p kc n", p=128)
            bmod = self.inp(f"bmod{l}", [1, 3 * D])
            ng_d = self.inp(f"ng{l}", [128, 16])
            win = self.inp(f"win{l}", [D, NIN]).rearrange("(kc p) n -> p kc n", p=128)
            lam_d = self.inp(f"lam{l}", [1, 256])
            subg_d = self.inp(f"subg{l}", [128, 1])
            wpool_d = self.inp(f"wpool{l}", [4, 128, 128])
            pscale_d = self.inp(f"pscale{l}", [128, 4])
            wdw_d = self.inp(f"wdw{l}", [128, 4, 31])
            cvec_d = self.inp(f"cvec{l}", [128, 12])
            wpw2_d = self.inp(f"wpw2{l}", [512, 512]).rearrange("(c p) n -> p c n", p=128)
            wout = self.inp(f"wout{l}", [D, D]).rearrange("(kc p) n -> p kc n", p=128)
            fg_d = self.inp("fg", [1, D]) if last else None

            sm = self.sb(les, [128, 64], F32)
            ng = self.sb(les, [128, 16], F32)
            lamb = self.sb(les, [128, 256], F32)
            subg = self.sb(les, [128, 1], F32)
            pscale = self.sb(les, [128, 4], F32)
            wdw = self.sb(les, [128, 4, 31], F32)
            cvec = self.sb(les, [128, 12], F32)
            wpool_b = self.sb(les, [128, 4, 128], BF16)
            wpw2_b = self.sb(les, [128, 4, 512], BF16)
            modcol, gs = self.modsave[l]
            ML = f"M{l}"
            junk128 = self.sb(les, [128, 128], F32)
            L = L or f"L{l}"
            self.dma("sp", ng[:], ng_d, r=[], w=[L + "ng"], sem="p0")
            self.dma("sp", lamb[:], lam_d.partition_broadcast(128), r=[], w=[L + "lamb"], sem="p1")
            self.dma("sp", subg[:], subg_d, r=[], w=[L + "subg"], sem="p2")
            self.dma("sp", pscale[:], pscale_d, r=[], w=[L + "pscale"], sem="p3")
            self.dma("sp", wdw[:], wdw_d, r=[], w=[L + "wdw"], sem="p4")
            self.dma("sp", cvec[:], cvec_d, r=[], w=[L + "cvec"], sem="p5")
            self.dma("pool", wpool_b[:], wpool_d.rearrange("g c d -> c g d"), r=[], w=[L + "wpool"], sem="p6")
            self.dma("pool", wpw2_b[:], wpw2_d, r=[], w=[L + "wpw2"], sem="p7")
            self.memset("dve", sm[:], 0.0, w=[L + "sm"])
            self.stt("dve", junk128[:, 0:64], lamb[:, 0:64], 1.0, lamb[:, 64:128], ALU.mult, ALU.mult,
                     r=[L + "lamb", L + "sm"], w=[L + "junk128", L + "sm0"], accum_out=sm[:, 0:1])
            self.stt("dve", junk128[:, 64:128], lamb[:, 128:192], 1.0, lamb[:, 192:256], ALU.mult, ALU.mult,
                     r=[L + "lamb", L + "sm"], w=[L + "junk128b", L + "sm1"], accum_out=sm[:, 1:2])
            self.act(sm[:, 2:4], sm[:, 0:2], AF.Exp, r=[L + "sm0", L + "sm1"], w=[L + "sm23"])
            self.tt("dve", sm[:, 4:5], sm[:, 2:3], sm[:, 3:4], ALU.subtract, r=[L + "sm23"], w=[L + "sm4"])
            self.ts("dve", sm[:, 5:6], sm[:, 4:5], lam_init, -1.0, ALU.add, ALU.mult, r=[L + "sm4"], w=[L + "neglam"])
            self.tsm("dve", sm[:, 6:7], subg[:], 1.0 - lam_init, r=[L + "subg", L + "sm"], w=[L + "subg2"])
            neglam = sm[:, 5:6]
            subg2 = sm[:, 6:7]

            pes_mod = self.scope()
            pes_mod.__enter__()
            pes = pes_mod
            if not passB:
                s_f = self.sb(pes, [128, 32], F32)
                srep = self.sb(pes, [128, 32, 128], BF16)
                bm = [self.sb(pes, [128, 512], F32) for _ in range(2)]
                rowb = [self.sb(pes, [128, 512], F32) for _ in range(2)]
                self.act(s_f[:], self.cT[:], AF.Silu, r=["cT"], w=[L + "s_f"])
                for j in range(32):
                    self.tsm("dve", srep[:, j, :], self.ones_b[:], s_f[:, j:j + 1], r=["ones_b", L + "s_f"],
                             w=[(L + "srep", j)])
                for r_ in range(2):
                    self.memset("dve", modcol[r_][:], 0.0, w=[(ML + "modcol", r_)])
                for g in range(8):
                    i = self.wload(wmod, g * 512)
                    self.dma("sp", bm[g % 2][:], bmod[0:1, g * 512:(g + 1) * 512].partition_broadcast(128),
                             r=[], w=[(L + "bm", g % 2)], sem=f"bm{g%2}")
                    for r_ in range(2):
                        bi = self.bank()
                        self.mm16(bi, 512, lambda kc: srep[:, r_ * 16 + kc, :], lambda kc: self.wb[i][:, kc, :],
                                  r=[("wb", i)] + [(L + "srep", r_ * 16 + kc) for kc in range(16)])
                        self.tt("dve", rowb[r_][:], B[bi][:], bm[g % 2][:], ALU.add,
                                r=[("ps", bi), (L + "bm", g % 2)], w=[(L + "rowb", r_)])
                        for j in range(4):
                            c = g * 4 + j
                            self.stt("dve", junk128[:], rowb[r_][:, j * 128:(j + 1) * 128], 1.0, self.ident_f[:],
                                     ALU.mult, ALU.mult, r=[(L + "rowb", r_), "ident_f", (ML + "modcol", r_)],
                                     w=[L + "junk128", (ML + "modcolc", r_, c)], accum_out=modcol[r_][:, c:c + 1])
                for r_ in range(2):
                    rk = [(ML + "modcolc", r_, c) for c in range(16, 32)]
                    self.ts("dve", gs[r_][:], modcol[r_][:, 16:32], 1.0, 1.0, ALU.add, ALU.mult,
                            r=rk, w=[(ML + "gs0", r_)])
                    self.tt("dve", gs[r_][:], gs[r_][:], ng[:], ALU.mult, r=[(ML + "gs0", r_), L + "ng"],
                            w=[(ML + "gs", r_)])
            shiftk = lambda r_: [(ML + "modcolc", r_, c) for c in range(16)]
            self.final_keys = set()
            if self.debug and not passB:
                dm = self.outp(f"dbg_mod{l}", [128, 96], F32)
                allmk = [(ML + "modcolc", r_, c) for r_ in range(2) for c in range(32)] + [(ML + "gs", 0), (ML + "gs", 1)]
                self.dma("sp", dm[:, 0:32], modcol[0][:], r=allmk, w=["dm0"], sem="dbgm0")
                self.dma("sp", dm[:, 32:64], modcol[1][:], r=allmk, w=["dm1"], sem="dbgm1")
                self.dma("sp", dm[:, 64:80], gs[0][:], r=allmk, w=["dm2"], sem="dbgm2")
                self.dma("sp", dm[:, 80:96], gs[1][:], r=allmk, w=["dm3"], sem="dbgm3")
                self.final_keys |= {"dm0", "dm1", "dm2", "dm3"}
                P.barrier()
            if self.stop == "MOD1":
                return

            with self.scope() as pes:
                xt = [self.sb(pes, [128, D], F32) for _ in range(2)]
                xt2 = [self.sb(pes, [128, D], F32) for _ in range(2)] if src["blend"] is not None else None
                xn = [self.sb(pes, [128, D], BF16) for _ in range(2)]
                junkb = self.sb(pes, [128, D], BF16)
                st = self.sb(pes, [128, 4, 18], F32)
                self.memset("dve", st[:], 0.0, w=[L + "st"])
                hx_tiles = [2, 3, 4, 5, 6, 7, 8, 9, 10, 17] if passB else list(range(18))

                def hx_s1(a, do_sq):
                        r_ = 1 if a < 2 else 0
                        x = xt[a % 2]
                        xk = (L + "xt", a % 2)
                        if a < 2:
                            sap = src["ctx"][a * 128:(a + 1) * 128, :]
                            self.dma("sp", x[:], sap, r=[("cdst", a)], w=[xk], sem=f"xt{a%2}")
                        elif a < 10:
                            t = a - 2
                            self.dma("sp", x[:], src["own"][t * 128:(t + 1) * 128, :], r=[(src.get("own_key", "none"), t)], w=[xk],
                                     sem=f"xt{a%2}")
                        else:
                            t = a - 10
                            if src["blend"] is None:
                                self.dma("sp", x[:], src["oth"][t * 128:(t + 1) * 128, :], r=[(src.get("oth_key", "none"), t)], w=[xk], sem=f"xt{a%2}")
                            else:
                                x2 = xt2[a % 2]
                                x2k = (L + "xt2", a % 2)
                                self.dma("sp", x[:], src["blend"][t * 128:(t + 1) * 128, :], r=["recv"], w=[xk],
                                         sem=f"xt{a%2}")
                                self.dma("sp", x2[:], src["blend"][1024 + t * 128:1024 + (t + 1) * 128, :], r=["recv"],
                                         w=[x2k], sem=f"xtb{a%2}")
                                self.tsm("dve", x[:], x[:], self.msk[:, 2:3], r=[xk, "msk"], w=[xk])
                                self.stt("dve", x[:], x2[:], self.msk[:, 3:4], x[:], ALU.mult, ALU.add,
                                         r=[xk, x2k, "msk"], w=[xk])
                        if do_sq:
                            self.act(junkb[:], x[:], AF.Square, r=[xk, L + "st"], w=[L + "junkb", (L + "ss", a)],
                                     accum_out=st[:, 0, a:a + 1])

                def hx_s2(a):
                        xk = (L + "xt", a % 2)
                        x = xt[a % 2]
                        xnk = (L + "xn", a % 2)
                        self.act(xn[a % 2][:], x[:], AF.Identity, r=[xk, L + "rstd"], w=[xnk],
                                 scale=st[:, 3, a:a + 1])

                def hx_s3(a):
                        r_ = 1 if a < 2 else 0
                        xnk = (L + "xn", a % 2)
                        b0 = (a % 2) * 2
                        for kc in range(16):
                            bi = b0 + kc // 8
                            pv = B[bi][:].bitcast(BF16)
                            self.tr(pv[:, (kc % 8) * 128:(kc % 8 + 1) * 128], xn[a % 2][:, kc * 128:(kc + 1) * 128],
                                    self.ident_b[:], r=[xnk, "ident_b"], w=[("ps", bi)])
                        for kc in range(16):
                            bi = b0 + kc // 8
                            pv = B[bi][:].bitcast(BF16)
                            o = hxA[:, kc, a * 128:(a + 1) * 128]
                            i_ = pv[:, (kc % 8) * 128:(kc % 8 + 1) * 128]
                            rr = [("ps", bi), (ML + "gs", r_)] + shiftk(r_)
                            if True:
                                self.ts("dve", o, i_, gs[r_][:, kc:kc + 1], modcol[r_][:, kc:kc + 1], ALU.mult, ALU.add,
                                        r=rr, w=[("hx", a)])
                            else:
                                self.act(o, i_, AF.Identity, r=rr, w=[("hx", a)], scale=gs[r_][:, kc:kc + 1],
                                         bias=modcol[r_][:, kc:kc + 1])

                for a in hx_tiles:
                    hx_s1(a, True)
                ssk = [(L + "ss", a) for a in hx_tiles]
                self.ts("dve", st[:, 1, :], st[:, 0, :], 1.0 / D, EPS, ALU.mult, ALU.add, r=ssk + [L + "st"], w=[L + "ms"])
                self.act(st[:, 2, :], st[:, 1, :], AF.Sqrt, r=[L + "ms"], w=[L + "sd"])
                self.recip(st[:, 3, :], st[:, 2, :], r=[L + "sd"], w=[L + "rstd"])
                for i_t, a in enumerate(hx_tiles):
                    hx_s1(a, False)
                    hx_s2(a)
                    if i_t >= 1:
                        hx_s3(hx_tiles[i_t - 1])
                hx_s3(hx_tiles[-1])
                P.barrier()
            pes_mod.__exit__(None, None, None)
            if self.debug and not passB:
                dh = self.outp(f"dbg_hx{l}", [128, 16, 2304], BF16)
                self.dma("sp", dh, hxA[:], r=[("hx", a) for a in range(18)], w=[f"dbg_hx{l}"], sem="dbg")
                P.barrier()
                self.final_keys.add(f"dbg_hx{l}")
            if self.stop == "HX":
                return

            with self.scope() as pes:
              if not passB:
                cosk = self.sb(pes, [128, 2048], F32)
                sink = self.sb(pes, [128, 2048], F32)
                k_sb = [self.sb(pes, [128, 512], BF16) for _ in range(2)]
                t1 = [self.sb(pes, [128, 512], F32) for _ in range(2)]
                t2 = [self.sb(pes, [128, 512], F32) for _ in range(2)]
                kto = [self.sb(pes, [128, 512], BF16) for _ in range(2)]
                vst = [self.sb(pes, [128, 4, 132], BF16) for _ in range(2)]
                self.dma("sp", cosk[:], self.cosk_d, r=[], w=[L + "cosk"], sem="cosk")
                self.dma("sp", sink[:], self.sink_d, r=[], w=[L + "sink"], sem="sink")
                for j in range(2):
                    self.memset("dve", vst[j][:], 1.0, w=[(L + "vst", j)])
                cnt = 0
                for gk in (2, 3):
                    i = self.wload(win, gk * 512)
                    for hh in range(4):
                        h = (gk - 2) * 4 + hh
                        for (tok0, n, rope) in [(0, 256, False), (256, 512, True), (768, 512, True),
                                                (1280, 512, True), (1792, 512, True)]:
                            j = cnt % 2
                            cnt += 1
                            bi = self.bank()
                            self.mm16(bi, n, lambda kc: self.wb[i][:, kc, hh * 128:(hh + 1) * 128],
                                      lambda kc: hxA[:, kc, tok0:tok0 + n], r=[("wb", i)] + self.hxkeys(tok0, n))
                            if not rope or os.environ.get("KV_NOROPE"):
                                self.cp("act", kto[j][:, :n], B[bi][:, :n], r=[("ps", bi)], w=[(L + "kto", j)])
                            else:
                                RV = os.environ.get("ROPE_VAR", "")
                                self.cp("act", k_sb[j][:, :n], B[bi][:, :n], r=[("ps", bi)], w=[(L + "k_sb", j)])
                                br = self.bank()
                                if RV != "dve_only":
                                    if os.environ.get("ROPE_IDENT"):
                                        self.mm(B[br][:, :n], self.ident_b[:], k_sb[j][:, :n], True, True,
                                                r=["ident_b", (L + "k_sb", j)], w=[("ps", br)])
                                    else:
                                        self.mm(B[br][:, :n], self.rotm_b[:], k_sb[j][:, :n], True, True,
                                                r=["rotm_b", (L + "k_sb", j)], w=[("ps", br)])
                                else:
                                    br = bi
                                if RV == "mm_only":
                                    self.cp("act", kto[j][:, :n], B[br][:, :n], r=[("ps", br)], w=[(L + "kto", j)])
                                    continue
                                p0 = tok0 - 256
                                self.tt("dve", t1[j][:, :n], B[bi][:, :n], cosk[:, p0:p0 + n], ALU.mult,
                                        r=[("ps", bi), L + "cosk", (L + "k_sb", j)], w=[(L + "t1", j)])
                                self.tt("dve", t2[j][:, :n], B[br][:, :n], sink[:, p0:p0 + n], ALU.mult,
                                        r=[("ps", br), L + "sink"], w=[(L + "t2", j)])
                                self.tt("dve", kto[j][:, :n], t1[j][:, :n], t2[j][:, :n], ALU.add,
                                        r=[(L + "t1", j), (L + "t2", j)], w=[(L + "kto", j)])
                            if os.environ.get("KV_NOSTORE") and not (h == 7 and tok0 == 1792):
                                continue
                            self.dma("sp", self.kT_d[h, :, tok0:tok0 + n], kto[j][:, :n], r=[(L + "kto", j)],
                                     w=[("kT_d", h)], sem=f"kto{j}")
                if self.stop == "KVK":
                    self.final_keys |= {("kT_d", h) for h in range(8)}
                    P.barrier()
                    return
                cnt = 0
                for gv in (4, 5):
                    i = self.wload(win, gv * 512)
                    for a in range(18):
                        j = cnt % 2
                        cnt += 1
                        bi = self.bank()
                        self.mm16(bi, 512, lambda kc: hxA[:, kc, a * 128:(a + 1) * 128],
                                  lambda kc: self.wb[i][:, kc, :], r=[("wb", i), ("hx", a)])
                        self.cp("act", vst[j][:, :, 0:128], B[bi][:].rearrange("p (h e) -> p h e", h=4),
                                r=[("ps", bi)], w=[(L + "vst", j)])
                        h0 = (gv - 4) * 4
                        self.dma("sp", self.v_d[h0:h0 + 4, :, a, :].rearrange("h p e -> p h e"), vst[j][:],
                                 r=[(L + "vst", j)], w=[("v_d", h0 + q) for q in range(4)], sem=f"vst{j}")
                P.barrier()

            if self.debug:
                self.final_keys |= {("kT_d", h) for h in range(8)} | {("v_d", h) for h in range(8)}
            if self.stop == "KV":
                return
            nown = 1280 if has_ctx else 1024
            yTc = self.sb(les, [128, 16, 256], BF16) if has_ctx else None

            def ytv(kc, c0, n):
                if c0 < 1024:
                    return hxA[:, kc, 1280 + c0:1280 + c0 + n]
                return yTc[:, kc, c0 - 1024:c0 - 1024 + n]

            def ytk(kc, c0, n):
                return [("yT", kc, t) for t in range(c0 // 128, (c0 + n) // 128)]

            oblocks = [(256, 0, 512), (768, 512, 512)] + ([(0, 1024, 256)] if has_ctx else [])

            with self.scope() as cps:
                u_ext = self.sb(cps, [128, 4, 1056], BF16)
                uc_ext = self.sb(cps, [128, 4, 288], BF16) if has_ctx else None
                pps = self.scope()
                pps.__enter__()
                up_ext = self.sb(cps, [128, 4, 1056], F32)
                upc_ext = self.sb(cps, [128, 4, 288], F32) if has_ctx else None
                if has_ctx:
                    self.memset("dve", uc_ext[:], 0.0, w=[L + "uc_ext"])
                    self.memset("dve", upc_ext[:], 0.0, w=[L + "upc_ext"])
                with self.scope() as pes:
                    sig = [self.sb(pes, [128, 512], F32) for _ in range(2)]
                    ablocks = [(256, 512, False, 16, None), (768, 512, False, 528, None)]
                    if has_ctx:
                        ablocks.append((0, 256, True, 16, None))
                    ablocks += [(2288, 16, False, 0, mL), (1280, 16, False, 1040, mR)]
                    iA = self.wload(win, 5120)
                    iB = self.wload(win, 5632)
                    cnt = 0
                    for j in range(4):
                        for (tok0, n, isc, off, mc) in ablocks:
                            q = cnt % 2
                            cnt += 1
                            ba = self.bank()
                            self.mm16(ba, n, lambda kc: self.wb[iA][:, kc, j * 128:(j + 1) * 128],
                                      lambda kc: hxA[:, kc, tok0:tok0 + n], r=[("wb", iA)] + self.hxkeys(tok0, n))
                            bb = self.bank()
                            self.mm16(bb, n, lambda kc: self.wb[iB][:, kc, j * 128:(j + 1) * 128],
                                      lambda kc: hxA[:, kc, tok0:tok0 + n], r=[("wb", iB)] + self.hxkeys(tok0, n))
                            self.act(sig[q][:, :n], B[bb][:, :n], AF.Sigmoid, r=[("ps", bb)], w=[(L + "sig", q)])
                            dst = (uc_ext if isc else u_ext)[:, j, off:off + n]
                            dk = L + ("uc_ext" if isc else "u_ext")
                            if mc is None:
                                self.tt("dve", dst, B[ba][:, :n], sig[q][:, :n], ALU.mult,
                                        r=[("ps", ba), (L + "sig", q), dk], w=[dk])
                            else:
                                self.stt("dve", dst, B[ba][:, :n], self.msk[:, mc:mc + 1], sig[q][:, :n],
                                         ALU.mult, ALU.mult, r=[("ps", ba), (L + "sig", q), "msk", dk], w=[dk])
                    i8 = self.wload(win, 4096)
                    for j in range(4):
                        for (tok0, n, isc, off, mc) in ablocks:
                            ba = self.bank()
                            self.mm16(ba, n, lambda kc: self.wb[i8][:, kc, j * 128:(j + 1) * 128],
                                      lambda kc: hxA[:, kc, tok0:tok0 + n], r=[("wb", i8)] + self.hxkeys(tok0, n))
                            dst = (upc_ext if isc else up_ext)[:, j, off:off + n]
                            dk = L + ("upc_ext" if isc else "up_ext")
                            if mc is None:
                                self.cp("act", dst, B[ba][:, :n], r=[("ps", ba), dk], w=[dk])
                            else:
                                self.act(dst, B[ba][:, :n], AF.Identity, r=[("ps", ba), "msk", dk], w=[dk],
                                         scale=self.msk[:, mc:mc + 1])
                    P.barrier()

                with self.scope() as pes:
                    sa = [self.sb(pes, [128, 1056], F32) for _ in range(2)]
                    va = [self.sb(pes, [128, 1056], F32) for _ in range(3)]
                    dT = self.sb(pes, [128, 4, nown], BF16)
                    for q_ in range(2):
                        self.memset("dve", sa[q_][:], 0.0, w=[L + "sa" + str(q_)])
                        self.memset("dve", va[q_][:], 0.0, w=[L + "va" + str(q_)])
                    sgp = self.sb(pes, [128, 4, nown], BF16)
                    ig = self.wload(win, 4608)
                    for j in range(4):
                        for (tok0, c0, n) in oblocks:
                            ba = self.bank()
                            self.mm16(ba, n, lambda kc: self.wb[ig][:, kc, j * 128:(j + 1) * 128],
                                      lambda kc: hxA[:, kc, tok0:tok0 + n],
                                      r=[("wb", ig)] + self.hxkeys(tok0, n))
                            self.act(sgp[:, j, c0:c0 + n], B[ba][:, :n], AF.Silu, r=[("ps", ba), L + "sgp"],
                                     w=[L + "sgp"])
                    segs = [(up_ext, L + "up_ext", 1056, 1024, 0, True)]
                    if has_ctx:
                        segs.append((upc_ext, L + "upc_ext", 288, 256, 1024, False))
                    for (U, uk, E, N, c0, is_lat) in segs:
                        V = va[2]
                        self.memset("dve", V[:, :E], 1.0 if is_lat else 0.0, w=[L + "V"])
                        if is_lat:
                            self.tsm("dve", V[:, 0:16], V[:, 0:16], self.msk[:, mL:mL + 1], r=[L + "V", "msk"], w=[L + "V"])
                            self.tsm("dve", V[:, 1040:1056], V[:, 1040:1056], self.msk[:, mR:mR + 1], r=[L + "V", "msk"],
                                     w=[L + "V"])
                        else:
                            self.memset("dve", V[:, 16:16 + N], 1.0, w=[L + "V"])
                        for g in range(4):
                            def steps(src_ap, bufs, keyp, srck):
                                cur = src_ap
                                ck = srck
                                for i in range(g + 1):
                                    nb = bufs[i % 2]
                                    nk = keyp + str(i % 2)
                                    if i == 0:
                                        self.tt("dve", nb[:, 1:E], cur[:, 0:E - 1], cur[:, 1:E], ALU.add,
                                                r=[ck, nk], w=[nk])
                                    else:
                                        sh = 1 << (i - 1)
                                        self.tt("dve", nb[:, sh:E - sh], cur[:, 0:E - 2 * sh], cur[:, 2 * sh:E],
                                                ALU.add, r=[ck, nk], w=[nk])
                                    cur = nb
                                    ck = nk
                                return cur, ck
                            s_fin, sk_ = steps(U[:, g, :], sa, L + "sa", uk)
                            v_fin, vk_ = steps(V, va, L + "va", L + "V")
                            self.recip(v_fin[:, 16:16 + N], v_fin[:, 16:16 + N], r=[vk_], w=[vk_])
                            self.tt("dve", s_fin[:, 16:16 + N], s_fin[:, 16:16 + N], v_fin[:, 16:16 + N], ALU.mult,
                                    r=[sk_, vk_], w=[sk_])
                            self.tt("dve", dT[:, g, c0:c0 + N], s_fin[:, 16:16 + N], U[:, g, 16:16 + N],
                                    ALU.subtract, r=[sk_, uk, L + "dT"], w=[L + "dT"])
                    for g in range(4):
                        for (tok0, c0, n) in oblocks:
                            ba = self.bank()
                            self.mm(B[ba][:, :n], wpool_b[:, g, :], dT[:, g, c0:c0 + n], True, True,
                                    r=[L + "wpool", L + "dT"], w=[("ps", ba)])
                            self.stt("dve", ytv(8 + g, c0, n), B[ba][:, :n], pscale[:, g:g + 1], sgp[:, g, c0:c0 + n],
                                     ALU.mult, ALU.mult, r=[("ps", ba), L + "pscale", L + "sgp"],
                                     w=ytk(8 + g, c0, n))
                    P.barrier()
                pps.__exit__(None, None, None)

                with self.scope() as pes:
                    diag = self.sb(pes, [128, 4, 31, 128], BF16)
                    ybuf = self.sb(pes, [128, 4, 512], F32)
                    ysq = self.sb(pes, [128, 4, 512], F32)
                    mst = self.sb(pes, [128, 4, 512], F32)
                    sT = self.sb(pes, [128, 4, 512], BF16)
                    sgc = self.sb(pes, [128, 4, nown], BF16)
                    ig = self.wload(win, 6144)
                    for j in range(4):
                        for (tok0, c0, n) in oblocks:
                            ba = self.bank()
                            self.mm16(ba, n, lambda kc: self.wb[ig][:, kc, j * 128:(j + 1) * 128],
                                      lambda kc: hxA[:, kc, tok0:tok0 + n],
                                      r=[("wb", ig)] + self.hxkeys(tok0, n))
                            self.act(sgc[:, j, c0:c0 + n], B[ba][:, :n], AF.Silu, r=[("ps", ba), L + "sgc"],
                                     w=[L + "sgc"])
                    for c in range(4):
                        for k in range(31):
                            self.tsm("dve", diag[:, c, k, :], self.ident_b[:], wdw[:, c, k:k + 1],
                                     r=["ident_b", L + "wdw"], w=[(L + "diag", c)])
                    cb = [4, 5, 6, 7]
                    for (tok0, c0, n) in oblocks:
                        isc = c0 >= 1024
                        ue = uc_ext if isc else u_ext
                        uk = L + ("uc_ext" if isc else "u_ext")
                        e0 = (c0 - 1024) if isc else c0
                        for c in range(4):
                            bi = cb[c]
                            for k in range(31):
                                self.mm(B[bi][:, :n], diag[:, c, k, :], ue[:, c, e0 + k + 1:e0 + k + 1 + n],
                                        k == 0, k == 30, r=[(L + "diag", c), uk], w=[("ps", bi)])
                            self.act(ybuf[:, c, :n], B[bi][:, :n], AF.Identity, r=[("ps", bi), L + "cvec"],
                                     w=[(L + "ybuf", c)], bias=cvec[:, c:c + 1])
                            self.act(ysq[:, c, :n], B[bi][:, :n], AF.Square, r=[("ps", bi), L + "cvec"],
                                     w=[(L + "ysq", c)], bias=cvec[:, c:c + 1])
                        b1, b2 = 0, 1
                        for c in range(4):
                            self.mm(B[b1][:, :n], self.ones_f[:], ybuf[:, c, :n], c == 0, c == 3,
                                    r=["ones_f", (L + "ybuf", c)], w=[("ps", b1)])
                        for c in range(4):
                            self.mm(B[b2][:, :n], self.ones_f[:], ysq[:, c, :n], c == 0, c == 3,
                                    r=["ones_f", (L + "ysq", c)], w=[("ps", b2)])
                        self.ts("dve", mst[:, 0, :n], B[b1][:, :n], 1.0 / 512, 0.0, ALU.mult, ALU.add,
                                r=[("ps", b1)], w=[L + "m0"])
                        self.tt("dve", mst[:, 1, :n], mst[:, 0, :n], mst[:, 0, :n], ALU.mult, r=[L + "m0"], w=[L + "m1"])
                        self.stt("dve", mst[:, 2, :n], B[b2][:, :n], 1.0 / 512, mst[:, 1, :n], ALU.mult, ALU.subtract,
                                 r=[("ps", b2), L + "m1"], w=[L + "m2"])
                        self.ts("dve", mst[:, 2, :n], mst[:, 2, :n], EPS, 0.0, ALU.add, ALU.add,
                                r=[L + "m2"], w=[L + "m2"])
                        self.act(mst[:, 2, :n], mst[:, 2, :n], AF.Sqrt, r=[L + "m2"], w=[L + "m2"])
                        self.recip(mst[:, 3, :n], mst[:, 2, :n], r=[L + "m2"], w=[L + "m3"])
                        for c in range(4):
                            self.tt("dve", ybuf[:, c, :n], ybuf[:, c, :n], mst[:, 0, :n], ALU.subtract,
                                    r=[(L + "ybuf", c), L + "m0"], w=[(L + "ybuf", c)])
                            self.tt("dve", ybuf[:, c, :n], ybuf[:, c, :n], mst[:, 3, :n], ALU.mult,
                                    r=[(L + "ybuf", c), L + "m3"], w=[(L + "ybuf", c)])
                            self.act(sT[:, c, :n], ybuf[:, c, :n], AF.Silu, r=[(L + "ybuf", c), L + "cvec"],
                                     w=[(L + "sT", c)], scale=cvec[:, 4 + c:5 + c], bias=cvec[:, 8 + c:9 + c])
                        for j in range(4):
                            ba = 2 + (j % 2)
                            for c in range(4):
                                self.mm(B[ba][:, :n], wpw2_b[:, c, j * 128:(j + 1) * 128], sT[:, c, :n], c == 0, c == 3,
                                        r=[L + "wpw2", (L + "sT", c)], w=[("ps", ba)])
                            self.tt("dve", ytv(12 + j, c0, n), B[ba][:, :n], sgc[:, j, c0:c0 + n], ALU.mult,
                                    r=[("ps", ba), L + "sgc"], w=ytk(12 + j, c0, n))
                    P.barrier()

            with self.scope() as pes:
                cosq = self.sb(pes, [128, 1024], F32)
                sinq = self.sb(pes, [128, 1024], F32)
                qm = [[self.sb(pes, [128, nown], BF16) for _ in range(2)] for _ in range(2)]
                sgT = [self.sb(pes, [128, nown], BF16) for _ in range(2)]
                kTs = [self.sb(pes, [128, 2304], BF16) for _ in range(2)]
                Vhs = [self.sb(pes, [128, 18, 132], BF16) for _ in range(2)]
                k_sb = self.sb(pes, [128, 512], BF16)
                t1 = self.sb(pes, [128, 512], F32)
                t2 = self.sb(pes, [128, 512], F32)
                NPT = 6
                PT = [self.sb(pes, [128, 512], BF16) for _ in range(NPT)]
                Osb = self.sb(pes, [128, 2, 4, 132], F32)
                osm = self.sb(pes, [128, 8, 16], F32)
                o_sb = [self.sb(pes, [128, 128], F32) for _ in range(4)]
                on_b = [self.sb(pes, [128, 128], BF16) for _ in range(4)]
                junko4 = [self.sb(pes, [128, 128], F32) for _ in range(4)]
                self.dma("sp", cosq[:], self.cosk_d[:, qoff:qoff + 1024], r=[], w=[L + "cosq"], sem="cosq")
                self.dma("sp", sinq[:], self.sink_d[:, qoff:qoff + 1024], r=[], w=[L + "sinq"], sem="sinq")
                self.memset("dve", osm[:], 0.0, w=[L + "osm"])
                for hb_ in range(2):
                    for c_ in range(2):
                        self.memset("dve", qm[hb_][c_][:], 0.0, w=[(L + "qT", hb_, c0_) for c0_ in (0, 512, 1024)])
                gbs = [0, 1]
                gci = 0
                pendA = [None]
                pendB = [None]
                since = [0]
                ptc = 0
                sci = 0
                tcount = 0
                wl = {}

                def inproj(h):
                        nonlocal gci
                        hp, hh = divmod(h, 4)
                        hb = h % 2
                        if hh == 0:
                            wl[hp] = (self.wload(win, hp * 512), self.wload(win, 3072 + hp * 512))
                        iq, igt = wl[hp]
                        for (tok0, c0, n) in oblocks:
                            bi = gbs[gci % 2]; gci += 1
                            self.mm16(bi, n, lambda kc: self.wb[iq][:, kc, hh * 128:(hh + 1) * 128],
                                      lambda kc: hxA[:, kc, tok0:tok0 + n], r=[("wb", iq)] + self.hxkeys(tok0, n))
                            qk = (L + "qT", hb, c0)
                            if c0 >= 1024:
                                self.cp("act", qm[hb][0][0:64, c0:c0 + n], B[bi][0:64, :n], r=[("ps", bi)], w=[qk])
                                self.cp("act", qm[hb][1][64:128, c0:c0 + n], B[bi][64:128, :n], r=[("ps", bi)], w=[qk])
                            else:
                                self.cp("act", k_sb[:, :n], B[bi][:, :n], r=[("ps", bi)], w=[L + "qk_sb"])
                                br = gbs[gci % 2]; gci += 1
                                self.mm(B[br][:, :n], self.rotm_b[:], k_sb[:, :n], True, True,
                                        r=["rotm_b", L + "qk_sb"], w=[("ps", br)])
                                self.tt("dve", t1[:, :n], B[bi][:, :n], cosq[:, c0:c0 + n], ALU.mult,
                                        r=[("ps", bi), L + "cosq", L + "qk_sb"], w=[L + "qt1"])
                                self.tt("dve", t2[:, :n], B[br][:, :n], sinq[:, c0:c0 + n], ALU.mult,
                                        r=[("ps", br), L + "sinq"], w=[L + "qt2"])
                                self.tt("dve", qm[hb][0][0:64, c0:c0 + n], t1[0:64, :n], t2[0:64, :n], ALU.add,
                                        r=[L + "qt1", L + "qt2"], w=[qk])
                                self.tt("dve", qm[hb][1][64:128, c0:c0 + n], t1[64:128, :n], t2[64:128, :n], ALU.add,
                                        r=[L + "qt1", L + "qt2"], w=[qk])
                            bi = gbs[gci % 2]; gci += 1
                            self.mm16(bi, n, lambda kc: self.wb[igt][:, kc, hh * 128:(hh + 1) * 128],
                                      lambda kc: hxA[:, kc, tok0:tok0 + n], r=[("wb", igt)] + self.hxkeys(tok0, n))
                            self.act(sgT[hb][:, c0:c0 + n], B[bi][:, :n], AF.Silu, r=[("ps", bi)],
                                     w=[(L + "sgT", hb, c0)])
                        self.dma("sp", kTs[hb][:], self.kT_d[h], r=[("kT_d", h)], w=[(L + "kT", hb)], sem=f"kT{hb}")
                        self.dma("sp", Vhs[hb][:], self.v_d[h], r=[("v_d", h)], w=[(L + "Vh", hb)], sem=f"Vh{hb}")

                inproj(0)
                for h in range(8):
                        hb = h % 2
                        kT = kTs[hb]
                        Vh = Vhs[hb]
                        qblocks = [(0, 512, list(range(18))), (512, 512, list(range(18)))]
                        if has_ctx:
                            qblocks.append((1024, 256, [0, 1]))
                        for qbi, (c0, n, kts) in enumerate(qblocks):
                            if qbi == 1 and h + 1 < 8:
                                inproj(h + 1)
                            nqs = n // 128
                            items = [(c, ki, kt) for c in range(2) for ki, kt in enumerate(kts)]

                            def emit_pv(it, pi):
                                c, ki, kt = it
                                for qs in range(nqs):
                                    self.mm(B[4 + qs][:, 0:129], PT[pi][:, qs * 128:(qs + 1) * 128], Vh[:, kt, 0:129],
                                            ki == 0, ki == len(kts) - 1, r=[(L + "PT", pi), (L + "Vh", hb)],
                                            w=[("ps", 4 + qs)])
                                if ki == len(kts) - 1:
                                    if pendA[0] is not None:
                                        pendA[0](); pendA[0] = None
                                    for qs in range(nqs):
                                        self.cp("dve", Osb[:, c, qs, 0:129], B[4 + qs][:, 0:129], r=[("ps", 4 + qs)],
                                                w=[(L + "Osb", c, qs)])
                            pendq = []
                            for it in items:
                                c, ki, kt = it
                                sb_ = 1 + (sci % 3); sci += 1
                                self.mm(B[sb_][:, :n], kT[:, kt * 128:(kt + 1) * 128],
                                        qm[hb][c][:, c0:c0 + n], True, True,
                                        r=[(L + "kT", hb), (L + "qT", hb, c0)], w=[("ps", sb_)])
                                pi = ptc % NPT; ptc += 1
                                self.act(PT[pi][:, :n], B[sb_][:, :n], AF.Exp, r=[("ps", sb_)], w=[(L + "PT", pi)],
                                         scale=0.125)
                                if len(pendq) == 2:
                                    emit_pv(*pendq.pop(0))
                                pendq.append((it, pi))
                                since[0] += 1
                                if since[0] == 3 and pendA[0] is not None:
                                    pendA[0](); pendA[0] = None
                                if since[0] == 10 and pendB[0] is not None:
                                    if pendA[0] is not None:
                                        pendA[0](); pendA[0] = None
                                    pendB[0](); pendB[0] = None
                            while pendq:
                                emit_pv(*pendq.pop(0))
                            def make_epi(h=h, hb=hb, c0=c0, nqs=nqs):
                                def epiA():
                                    Q = range(nqs)
                                    okr = lambda qs: [(L + "Osb", 0, qs), (L + "Osb", 1, qs)]
                                    k = lambda nm, qs: (L + "osm_" + nm, qs)
                                    for qs in Q:
                                        self.recip(osm[:, qs, 0:1], Osb[:, 0, qs, 128:129], r=okr(qs) + [L + "osm"], w=[k("rz0", qs)])
                                    for qs in Q:
                                        self.recip(osm[:, qs, 1:2], Osb[:, 1, qs, 128:129], r=okr(qs) + [L + "osm"], w=[k("rz1", qs)])
                                    for qs in Q:
                                        self.tt("dve", osm[:, qs, 2:3], osm[:, qs, 1:2], neglam, ALU.mult,
                                                r=[k("rz1", qs), L + "neglam"], w=[k("nl", qs)])
                                    for qs in Q:
                                        self.tsm("dve", o_sb[qs][:], Osb[:, 0, qs, 0:128], osm[:, qs, 0:1], r=okr(qs) + [k("rz0", qs)],
                                                 w=[(L + "o_sb", qs)])
                                    for qs in Q:
                                        self.stt("dve", o_sb[qs][:], Osb[:, 1, qs, 0:128], osm[:, qs, 2:3], o_sb[qs][:],
                                                 ALU.mult, ALU.add, r=okr(qs) + [k("nl", qs), (L + "o_sb", qs)], w=[(L + "o_sb", qs)])
                                    for qs in Q:
                                        self.stt("dve", junko[:, qs * 32:qs * 32 + 32].bitcast(F32) if False else junko4[qs][:], o_sb[qs][:], 1.0, o_sb[qs][:], ALU.mult, ALU.mult,
                                                 r=[(L + "o_sb", qs), L + "osm"], w=[(L + "junko", qs), k("ss", qs)], accum_out=osm[:, qs, 3:4])
                                    for qs in Q:
                                        self.ts("dve", osm[:, qs, 4:5], osm[:, qs, 3:4], 1.0 / 128, EPS, ALU.mult, ALU.add,
                                                r=[k("ss", qs)], w=[k("ms", qs)])
                                    for qs in Q:
                                        self.act(osm[:, qs, 5:6], osm[:, qs, 4:5], AF.Ln, r=[k("ms", qs)], w=[k("ln", qs)])
                                    for qs in Q:
                                        self.act(osm[:, qs, 6:7], osm[:, qs, 5:6], AF.Exp, r=[k("ln", qs)], w=[k("rstd", qs)], scale=-0.5)
                                    for qs in Q:
                                        self.tsm("dve", on_b[qs][:], o_sb[qs][:], osm[:, qs, 6:7], r=[(L + "o_sb", qs), k("rstd", qs)],
                                                 w=[(L + "on_b", qs)])

                                def epiB():
                                    nonlocal gci
                                    for qs in range(nqs):
                                        tcol = c0 + qs * 128
                                        bt = 0
                                        pv = B[bt][:].bitcast(BF16)
                                        self.tr(pv[:, (qs % 4) * 128:(qs % 4 + 1) * 128], on_b[qs][:], self.ident_b[:], r=[(L + "on_b", qs), "ident_b"],
                                                w=[("ps", bt)])
                                        self.stt("dve", ytv(h, tcol, 128), pv[:, (qs % 4) * 128:(qs % 4 + 1) * 128], subg2, sgT[hb][:, tcol:tcol + 128],
                                                 ALU.mult, ALU.mult,
                                                 r=[("ps", bt), L + "subg2", (L + "sgT", hb, (tcol // 512) * 512 if tcol < 1024 else 1024)],
                                                 w=ytk(h, tcol, 128))
                                return epiA, epiB
                            if pendA[0] is not None:
                                pendA[0](); pendA[0] = None
                            if pendB[0] is not None:
                                pendB[0](); pendB[0] = None
                            eA, eB = make_epi()
                            pendA[0] = eA
                            pendB[0] = eB
                            since[0] = 0
                if pendA[0] is not None:
                    pendA[0](); pendA[0] = None
                if pendB[0] is not None:
                    pendB[0](); pendB[0] = None
                P.barrier()
            if self.debug and not passB:
                dy = self.outp(f"dbg_yT{l}", [128, 16, 1024], BF16)
                self.dma("sp", dy, hxA[:, :, 1280:2304], r=[("yT", kc, t) for kc in range(16) for t in range(8)],
                         w=[f"dbg_yT{l}"], sem="dbg2")
                P.barrier()
                self.final_keys.add(f"dbg_yT{l}")
            if self.stop == "ATT":
                return

            with self.scope() as pes:
                nr = 2 if has_ctx else 1
                gate = [self.sb(pes, [128, D], F32) for _ in range(nr)]
                bm = [self.sb(pes, [128, 512], F32) for _ in range(2)]
                s_f = self.sb(pes, [128, 32], F32)
                srep = self.sb(pes, [128, 32, 128], BF16)
                xs = [self.sb(pes, [128, 512], F32) for _ in range(3)]
                xo = [self.sb(pes, [128, 512], F32) for _ in range(3)]
                self.act(s_f[:], self.cT[:], AF.Silu, r=["cT"], w=[L + "s_f2"])
                for j in range(16 * nr):
                    self.tsm("dve", srep[:, j, :], self.ones_b[:], s_f[:, j:j + 1], r=["ones_b", L + "s_f2"],
                             w=[(L + "srep2", j)])
                for g in range(8, 12):
                    i = self.wload(wmod, g * 512)
                    self.dma("sp", bm[g % 2][:], bmod[0:1, g * 512:(g + 1) * 512].partition_broadcast(128),
                             r=[], w=[(L + "bm2", g % 2)], sem=f"bmo{g%2}")
                    for r_ in range(nr):
                        bi = self.bank()
                        self.mm16(bi, 512, lambda kc: srep[:, r_ * 16 + kc, :], lambda kc: self.wb[i][:, kc, :],
                                  r=[("wb", i)] + [(L + "srep2", r_ * 16 + kc) for kc in range(16)])
                        self.tt("dve", gate[r_][:, (g - 8) * 512:(g - 7) * 512], B[bi][:], bm[g % 2][:], ALU.add,
                                r=[("ps", bi), (L + "bm2", g % 2)], w=[(L + "gate", r_, g - 8)])
                tiles = [(t, False) for t in range(8)] + ([(8, True), (9, True)] if has_ctx else [])
                cnt = 0
                for gw in range(4):
                    i = self.wload(wout, gw * 512)
                    for (t, isc) in tiles:
                        q = cnt % 3
                        cnt += 1
                        if isc:
                            tt_ = t - 8
                            sap = src["ctx"][tt_ * 128:(tt_ + 1) * 128, gw * 512:(gw + 1) * 512]
                            dap = cdst[tt_ * 128:(tt_ + 1) * 128, gw * 512:(gw + 1) * 512]
                            dk = ("cdst", tt_)
                            r_ = 1
                        else:
                            sap = src["own"][t * 128:(t + 1) * 128, gw * 512:(gw + 1) * 512]
                            dap = xdst[t * 128:(t + 1) * 128, gw * 512:(gw + 1) * 512]
                            dk = (xkey, t)
                            r_ = 0
                        self.dma("sp", xs[q][:], sap, r=[], w=[(L + "xs", q)], sem=f"xs{q}")
                        bi = self.bank()
                        c0 = t * 128
                        self.mm16(bi, 512, lambda kc: ytv(kc, c0, 128), lambda kc: self.wb[i][:, kc, :],
                                  r=[("wb", i)] + [("yT", kc, t) for kc in range(16)])
                        self.tt("dve", xo[q][:], B[bi][:], gate[r_][:, gw * 512:(gw + 1) * 512], ALU.mult,
                                r=[("ps", bi), (L + "gate", r_, gw)], w=[(L + "xo", q)])
                        self.tt("dve", xo[q][:], xo[q][:], xs[q][:], ALU.add, r=[(L + "xo", q), (L + "xs", q)],
                                w=[(L + "xo", q)])
                        self.dma("sp", dap, xo[q][:], r=[(L + "xo", q)], w=[dk], sem=f"xo{q}")
                P.barrier()
            if not last:
                if not self.fused:
                    self.final_keys |= {("xdst", t) for t in range(8)} | {("cdst", t) for t in range(2)}
            else:
                with self.scope() as pes:
                    fg = self.sb(pes, [128, D], F32)
                    xt = [self.sb(pes, [128, D], F32) for _ in range(2)]
                    yo = [self.sb(pes, [128, D], F32) for _ in range(2)]
                    junkb = self.sb(pes, [128, D], BF16)
                    st = self.sb(pes, [128, 4, 8], F32)
                    self.memset("dve", st[:], 0.0, w=[L + "fst"])
                    self.dma("sp", fg[:], fg_d.partition_broadcast(128), r=[], w=[L + "fg"], sem="fg")
                    for t in range(8):
                        q = t % 2
                        self.dma("sp", xt[q][:], xdst[t * 128:(t + 1) * 128, :], r=[(xkey, t)], w=[(L + "fxt", q)],
                                 sem=f"fxt{q}")
                        self.act(junkb[:], xt[q][:], AF.Square, r=[(L + "fxt", q), L + "fst"],
                                 w=[L + "fjunk", (L + "fss", t)], accum_out=st[:, 0, t:t + 1])
                        self.ts("dve", st[:, 1, t:t + 1], st[:, 0, t:t + 1], 1.0 / D, EPS, ALU.mult, ALU.add,
                                r=[(L + "fss", t)], w=[(L + "fms", t)])
                        self.act(st[:, 2, t:t + 1], st[:, 1, t:t + 1], AF.Sqrt, r=[(L + "fms", t)], w=[(L + "fsd", t)])
                        self.recip(st[:, 3, t:t + 1], st[:, 2, t:t + 1], r=[(L + "fsd", t)], w=[(L + "frs", t)])
                        self.stt("dve", yo[q][:], xt[q][:], st[:, 3, t:t + 1], fg[:], ALU.mult, ALU.mult,
                                 r=[(L + "fxt", q), (L + "frs", t), L + "fg"], w=[(L + "yo", q)])
                        self.dma("sp", ydst[t * 128:(t + 1) * 128, :], yo[q][:], r=[(L + "yo", q)], w=[("ydst", t)],
                                 sem=f"yo{q}")
                    self.final_keys |= {("ydst", t) for t in range(8)}
                    P.barrier()


_CACHE = {}


def _get(layers, fused, debug=False, stop=None):
    key = (tuple(layers), fused, debug, stop)
    if key not in _CACHE:
        b = Builder(list(layers), fused, debug, stop)
        nc = b.build()
        _CACHE[key] = (b, nc)
    return _CACHE[key]


def _col(v, n):
    return np.ascontiguousarray(np.asarray(v, np.float32).reshape(n, 128).T)


def _rope_tables():
    n_freq = 16
    inv = (np.float32(10000.0) ** (-np.arange(n_freq, dtype=np.float32) / np.float32(n_freq))).astype(np.float32)
    t = np.arange(2048)
    row = (t // 64).astype(np.float32)
    col = (t % 64).astype(np.float32)
    cos = np.zeros((128, 2048), np.float32)
    sin = np.zeros((128, 2048), np.float32)
    for p in range(128):
        d = p % 64
        pos = row if d < 32 else col
        j = (d % 32) % 16
        ang = (pos * inv[j]).astype(np.float32)
        cos[p] = np.cos(ang)
        sin[p] = np.sin(ang)
    return cos, sin


def _rotm():
    m = np.zeros((128, 128), np.float32)
    for mm_ in range(128):
        if (mm_ % 32) < 16:
            m[mm_ + 16, mm_] = -1.0
        else:
            m[mm_ - 16, mm_] = 1.0
    return m


def _layer_inputs(l, inp):
    d = {}
    d[f"wmod{l}"] = np.ascontiguousarray(inp["w_mod"][l], np.float32)
    d[f"bmod{l}"] = np.ascontiguousarray(inp["b_mod"][l], np.float32).reshape(1, -1)
    d[f"ng{l}"] = _col(inp["norm_g"][l], 16)
    d[f"win{l}"] = np.ascontiguousarray(inp["w_in"][l], np.float32)
    d[f"lam{l}"] = np.concatenate([inp["lambda_q1"][l], inp["lambda_k1"][l], inp["lambda_q2"][l],
                                   inp["lambda_k2"][l]]).astype(np.float32).reshape(1, 256)
    d[f"subg{l}"] = np.ascontiguousarray(np.asarray(inp["subln_g"][l], np.float32).reshape(128, 1))
    d[f"wpool{l}"] = np.ascontiguousarray(inp["w_pool"][l], np.float32)
    d[f"pscale{l}"] = _col(inp["pool_scale"][l], 4)
    wdw = np.asarray(inp["w_dw"][l], np.float32)
    d[f"wdw{l}"] = np.ascontiguousarray(wdw.T.reshape(4, 128, 31).transpose(1, 0, 2))
    d[f"cvec{l}"] = np.ascontiguousarray(np.concatenate(
        [_col(inp["b_dw"][l], 4), _col(inp["conv_ln_g"][l], 4), _col(inp["conv_ln_b"][l], 4)], axis=1))
    d[f"wpw2{l}"] = np.ascontiguousarray(inp["w_pw2"][l], np.float32)
    d[f"wout{l}"] = np.ascontiguousarray(inp["w_out"][l], np.float32)
    if l == 1:
        d["fg"] = np.asarray(inp["final_g"], np.float32).reshape(1, -1)
    return d


def _core_consts(core, inp):
    b, h = core // 2, core % 2
    cos, sin = _rope_tables()
    order = np.concatenate([np.arange(h * 1024, (h + 1) * 1024), np.arange((1 - h) * 1024, (2 - h) * 1024)])
    d = {}
    d["ident"] = np.eye(128, dtype=np.float32)
    d["rotm"] = _rotm()
    d["cosk"] = np.ascontiguousarray(cos[:, order])
    d["sink"] = np.ascontiguousarray(sin[:, order])
    m = np.zeros((128, 4), np.float32)
    m[:, 0] = 1.0 if h == 1 else 0.0
    m[:, 1] = 1.0 if h == 0 else 0.0
    m[:, 2] = 1.0 if h == 1 else 0.0
    m[:, 3] = 1.0 if h == 0 else 0.0
    d["msk"] = m
    cT = np.concatenate([_col(inp["c"][b], 16), _col(inp["c_ctx"], 16)], axis=1)
    d["cT"] = np.ascontiguousarray(cT)
    return d


def _run(layers, fused, inp, x, ctx, debug=False, stop=None):
    b_, nc = _get(layers, fused, debug, stop)
    in_maps = []
    ncr = int(os.environ.get("DBG_NCORES", NCORES)) if debug else NCORES
    for core in range(ncr):
        b, h = core // 2, core % 2
        d = _core_consts(core, inp)
        for l in layers:
            d.update(_layer_inputs(l, inp))
        d["x_own"] = np.ascontiguousarray(x[b, h * 1024:(h + 1) * 1024])
        d["x_oth"] = np.ascontiguousarray(x[b, (1 - h) * 1024:(2 - h) * 1024])
        d["ctx_in"] = np.ascontiguousarray(ctx[b])
        in_maps.append(d)
    res = run_bass_kernel_spmd(nc, in_maps, core_ids=list(range(ncr)))
    return res.results


FUSED = True


def kernel(**inp):
    inp = {k: np.asarray(v) for k, v in inp.items()}
    x = np.asarray(inp["x"], np.float32)
    ctx = np.asarray(inp["ctx"], np.float32)
    if FUSED:
        res = _run([0, 1], True, inp, x, ctx)
    else:
        r0 = _run([0], False, inp, x, ctx)
        x1 = np.empty_like(x)
        ctx1 = np.empty_like(ctx)
        for core in range(NCORES):
            b, h = core // 2, core % 2
            x1[b, h * 1024:(h + 1) * 1024] = r0[core]["x1_own"]
            if h == 0:
                ctx1[b] = r0[core]["ctx1"]
        res = _run([1], False, inp, x1, ctx1)
    out = np.empty((4, 2048, 2048), np.float32)
    for core in range(NCORES):
        b, h = core // 2, core % 2
        out[b, h * 1024:(h + 1) * 1024] = res[core]["y"]
    return out
```

```python
import math
import os
import numpy as np
import ml_dtypes
from contextlib import ExitStack
import concourse.bass as bass
import concourse.mybir as mybir
from concourse.bass_utils import run_bass_kernel_spmd

F32 = mybir.dt.float32
BF16 = mybir.dt.bfloat16
AF = mybir.ActivationFunctionType
ALU = mybir.AluOpType
AX = mybir.AxisListType

ENGS = ("pe", "act", "dve", "pool", "sp")
D = 2048
NIN = 6656
EPS = 1e-6
NCORES = 8


class Op:
    __slots__ = ("eng", "fn", "dma", "deps_hard", "deps_war", "needs_inc", "inc_idx",
                 "dma_cnt", "waits", "idx")

    def __init__(self, eng, fn, dma):
        self.eng = eng
        self.fn = fn
        self.dma = dma
        self.deps_hard = set()
        self.deps_war = set()
        self.needs_inc = False
        self.inc_idx = 0
        self.dma_cnt = 0
        self.waits = []


class Prog:
    def __init__(self, nc, es):
        self.nc = nc
        self.es = es
        self.ops = []
        self.last_w = {}
        self.readers = {}
        self.dma_count = {}
        self.nt = 0

    def add(self, eng, fn, r=(), w=(), dma=None):
        op = Op(eng, fn, dma)
        op.idx = len(self.ops)
        for k in r:
            lw = self.last_w.get(k)
            if lw is not None:
                op.deps_hard.add(lw)
        for k in w:
            lw = self.last_w.get(k)
            if lw is not None:
                op.deps_hard.add(lw)
            rd = self.readers.get(k)
            if rd:
                for o in rd.values():
                    op.deps_war.add(o)
        ent = ("dma", dma) if dma else ("eng", eng)
        for k in r:
            if isinstance(k, tuple) and k and k[0] == "ps":
                rd = self.readers.get(k)
                if rd:
                    for ent2, o in rd.items():
                        if ent2 != ent:
                            op.deps_hard.add(o)
        if fn is not None:
            for k in r:
                self.readers.setdefault(k, {})[ent] = op.idx
            for k in w:
                self.last_w[k] = op.idx
                self.readers[k] = {}
        if dma:
            self.dma_count[dma] = self.dma_count.get(dma, 0) + 1
            op.dma_cnt = self.dma_count[dma]
        self.ops.append(op)
        return op

    def barrier(self):
        allk = list(dict.fromkeys(list(self.last_w.keys()) + list(self.readers.keys())))
        for e in ENGS:
            self.add(e, None, r=allk, w=allk)

    def resolve(self):
        ops = self.ops
        for op in ops:
            need = set()
            for d in op.deps_hard:
                po = ops[d]
                if po.dma:
                    need.add(d)
                elif po.eng == op.eng and not op.dma and op.eng == "pe":
                    continue
                else:
                    need.add(d)
            for d in op.deps_war:
                po = ops[d]
                if po.dma:
                    need.add(d)
                elif po.eng == op.eng and op.eng == "pe" and not op.dma:
                    continue
                else:
                    need.add(d)
            op.waits = need
            for d in need:
                if not ops[d].dma:
                    ops[d].needs_inc = True
        cnt = {e: 0 for e in ENGS}
        for op in ops:
            if op.needs_inc:
                assert op.fn is not None
                cnt[op.eng] += 1
                op.inc_idx = cnt[op.eng]
        self.eng_total = cnt
        waited = {e: {} for e in ENGS}
        for op in ops:
            wl = {}
            for d in op.waits:
                po = ops[d]
                if po.dma:
                    key = ("dma", po.dma)
                    val = 16 * po.dma_cnt
                else:
                    key = ("eng", po.eng)
                    val = po.inc_idx
                if val > wl.get(key, 0):
                    wl[key] = val
            out = []
            for key, val in wl.items():
                if waited[op.eng].get(key, 0) >= val:
                    continue
                waited[op.eng][key] = val
                out.append((key, val))
            op.waits = out

    def emit(self):
        nc = self.nc
        self.resolve()
        sems = {}
        for e in ENGS:
            sems[("eng", e)] = self.es.enter_context(nc.semaphore(f"s_{e}"))
        for name in self.dma_count:
            sems[("dma", name)] = self.es.enter_context(nc.semaphore(f"d_{name}"))
        per = {e: [op for op in self.ops if op.eng == e] for e in ENGS}

        def run(engname, eng):
            for op in per[engname]:
                for key, val in op.waits:
                    eng.wait_ge(sems[key], val)
                if op.fn is None:
                    continue
                ins = op.fn(eng)
                if op.dma:
                    ins.then_inc(sems[("dma", op.dma)], 16)
                elif op.needs_inc:
                    ins.then_inc(sems[("eng", engname)], 1)

        with nc.Block() as block:
            @block.tensor
            def _(e):
                run("pe", e)

            @block.scalar
            def _(e):
                run("act", e)

            @block.vector
            def _(e):
                run("dve", e)

            @block.gpsimd
            def _(e):
                run("pool", e)

            @block.sync
            def _(e):
                run("sp", e)


class Builder:
    def __init__(self, layers, fused, debug=False, stop=None):
        self.stop = stop
        self.layers = layers
        self.fused = fused
        self.debug = debug
        self.nc = bass.Bass("TRN2", target_bir_lowering=False)
        self.din = {}
        self.dout = {}

    def inp(self, name, shape, dt=F32):
        if name in self.din:
            return self.din[name]
        t = self.nc.dram_tensor(name, list(shape), dt, kind="ExternalInput").ap()
        self.din[name] = t
        return t

    def outp(self, name, shape, dt=F32):
        t = self.nc.dram_tensor(name, list(shape), dt, kind="ExternalOutput").ap()
        self.dout[name] = t
        return t

    def scratch(self, name, shape, dt):
        if self.debug:
            return self.outp(name, shape, dt)
        return self.nc.dram_tensor(name, list(shape), dt, kind="Internal").ap()

    def scope(self):
        b = self

        class _S:
            def __enter__(self_):
                self_.mark = b.off
                return self_

            def __exit__(self_, *a):
                b.off = self_.mark
                return False
        return _S()

    def sb(self, es, shape, dt, name=None):
        esz = 4 if dt == F32 else 2
        n = 1
        for d_ in shape[1:]:
            n *= d_
        nbytes = (n * esz + 255) // 256 * 256
        off = self.off
        self.off += nbytes
        self.peak = max(self.peak, self.off)
        assert self.off <= self.BIGBYTES, f"SBUF overflow {self.off}"
        v = self.big[:, off // 2:(off + n * esz) // 2]
        if dt == F32:
            v = v.bitcast(F32)
        if len(shape) == 3:
            v = v.rearrange("p (a b) -> p a b", a=shape[1])
        elif len(shape) == 4:
            v = v.rearrange("p (a b c) -> p a b c", a=shape[1], b=shape[2])
        return v

    def dma(self, q, out, in_, r, w, sem):
        self.P.add(q, lambda e, o=out, i=in_: e.dma_start(out=o, in_=i), r=r, w=w, dma=sem)

    def mm(self, ps, lhsT, rhs, start, stop, r, w):
        self.P.add("pe", lambda e, a=ps, b=lhsT, c=rhs, s0=start, s1=stop:
                   e.matmul(a, lhsT=b, rhs=c, start=s0, stop=s1), r=r, w=w)

    def tr(self, ps, in_, ident, r, w):
        self.P.add("pe", lambda e, a=ps, b=in_, c=ident: e.transpose(a, b, c), r=r, w=w)

    def act(self, out, in_, func, r, w, **kw):
        self.P.add("act", lambda e, o=out, i=in_, f=func, k=kw: e.activation(out=o, in_=i, func=f, **k), r=r, w=w)

    def tt(self, eng, out, in0, in1, op, r, w):
        self.P.add(eng, lambda e, o=out, a=in0, b=in1, p=op: e.tensor_tensor(out=o, in0=a, in1=b, op=p), r=r, w=w)

    def ts(self, eng, out, in0, s1, s2, op0, op1, r, w):
        self.P.add(eng, lambda e, o=out, a=in0, x=s1, y=s2, p=op0, q=op1:
                   e.tensor_scalar(out=o, in0=a, scalar1=x, scalar2=y, op0=p, op1=q), r=r, w=w)

    def tsm(self, eng, out, in0, s1, r, w):
        self.P.add(eng, lambda e, o=out, a=in0, x=s1: e.tensor_scalar_mul(out=o, in0=a, scalar1=x), r=r, w=w)

    def stt(self, eng, out, in0, scalar, in1, op0, op1, r, w, accum_out=None):
        if accum_out is None:
            self.P.add(eng, lambda e, o=out, a=in0, s=scalar, b=in1, p=op0, q=op1:
                       e.scalar_tensor_tensor(out=o, in0=a, scalar=s, in1=b, op0=p, op1=q), r=r, w=w)
        else:
            self.P.add(eng, lambda e, o=out, a=in0, s=scalar, b=in1, p=op0, q=op1, ac=accum_out:
                       e.scalar_tensor_tensor(out=o, in0=a, scalar=s, in1=b, op0=p, op1=q, accum_out=ac), r=r, w=w)

    def cp(self, eng, out, in_, r, w):
        if eng == "act":
            self.P.add("act", lambda e, o=out, i=in_: e.copy(out=o, in_=i), r=r, w=w)
        else:
            self.P.add(eng, lambda e, o=out, i=in_: e.tensor_copy(out=o, in_=i), r=r, w=w)

    def memset(self, eng, ap, val, w):
        self.P.add(eng, lambda e, a=ap, v=val: e.memset(a, v), w=w)

    def recip(self, out, in_, r, w):
        self.P.add("dve", lambda e, o=out, i=in_: e.reciprocal(out=o, in_=i), r=r, w=w)

    def wload(self, view, c0, ncols=512):
        i = self.wcur
        self.wcur ^= 1
        self.dma("pool", self.wb[i][:, :, 0:ncols], view[:, :, c0:c0 + ncols], r=[], w=[("wb", i)], sem=f"wb{i}")
        return i

    def bank(self):
        i = self.gb[self.gbi % len(self.gb)]
        self.gbi += 1
        return i

    def mm16(self, bi, n, lhs_fn, rhs_fn, r):
        for kc in range(16):
            self.mm(self.B[bi][:, 0:n], lhs_fn(kc), rhs_fn(kc), kc == 0, kc == 15, r=r, w=[("ps", bi)])

    def hxkeys(self, tok0, n):
        return [("hx", a) for a in range(tok0 // 128, (tok0 + n + 127) // 128)]

    def build(self):
        nc = self.nc
        with ExitStack() as es:
            self.P = Prog(nc, es)
            P = self.P
            ident_d = self.inp("ident", [128, 128])
            rotm_d = self.inp("rotm", [128, 128])
            cosk_d = self.inp("cosk", [128, 2048])
            sink_d = self.inp("sink", [128, 2048])
            msk_d = self.inp("msk", [128, 4])
            cT_d = self.inp("cT", [128, 32])
            self.cosk_d, self.sink_d = cosk_d, sink_d

            self.BIGBYTES = 206 * 1024
            self.off = 128
            self.peak = 0
            self.big = es.enter_context(nc.sbuf_tensor("big", [128, self.BIGBYTES // 2], BF16))
            self.ident_f = self.sb(es, [128, 128], F32)
            if True:
                self.rotm_b = self.sb(es, [128, 128], BF16)
                self.ident_b = self.sb(es, [128, 128], BF16)
            else:
                self.ident_b = self.sb(es, [128, 128], BF16)
                self.rotm_b = self.sb(es, [128, 128], BF16)
            self.ones_b = self.sb(es, [128, 128], BF16)
            self.ones_f = self.sb(es, [128, 128], F32)
            self.msk = self.sb(es, [128, 4], F32)
            self.cT = self.sb(es, [128, 32], F32)
            self.hxA = self.sb(es, [128, 16, 2304], BF16)
            self.wb = [self.sb(es, [128, 16, 512], BF16) for _ in range(2)]
            self.wcur = 0
            self.B = [es.enter_context(nc.psum_tensor(f"bank{i}", [128, 512], F32)) for i in range(8)]
            self.gb = [0, 1, 2, 3]
            self.gbi = 0

            self.dma("sp", self.ident_f[:], ident_d, r=[], w=["ident_f"], sem="c0")
            self.dma("pool", self.ident_b[:], ident_d, r=[], w=["ident_b"], sem="c1")
            self.dma("pool", self.rotm_b[:], rotm_d, r=[], w=["rotm_b"], sem="c2")
            self.dma("sp", self.msk[:], msk_d, r=[], w=["msk"], sem="c3")
            self.dma("sp", self.cT[:], cT_d, r=[], w=["cT"], sem="c4")
            self.memset("dve", self.ones_b[:], 1.0, w=["ones_b"])
            self.memset("dve", self.ones_f[:], 1.0, w=["ones_f"])

            self.modsave = {l_: ([self.sb(es, [128, 32], F32) for _ in range(2)],
                                 [self.sb(es, [128, 16], F32) for _ in range(2)]) for l_ in self.layers}
            self.kT_d = self.scratch("kT_d", [8, 128, 2304], BF16)
            self.v_d = self.scratch("v_d", [8, 128, 18, 132], BF16)

            if not self.fused:
                l = self.layers[0]
                last = (l == 1)
                src = {"own": self.inp("x_own", [1024, D]), "oth": self.inp("x_oth", [1024, D]),
                       "ctx": self.inp("ctx_in", [256, D]), "blend": None}
                if last:
                    xdst = self.scratch("x2_own", [1024, D], F32)
                    cdst = None
                    ydst = self.outp("y", [1024, D])
                else:
                    xdst = self.outp("x1_own", [1024, D])
                    cdst = self.outp("ctx1", [256, D])
                    ydst = None
                self.layer(l, last, src, xdst, cdst, ydst)
            else:
                x_own = self.inp("x_own", [1024, D])
                x_oth = self.inp("x_oth", [1024, D])
                ctx_in = self.inp("ctx_in", [256, D])
                x1_own = nc.dram_tensor("x1_own", [1024, D], F32, kind="Internal").ap()
                x1_oth = nc.dram_tensor("x1_oth", [1024, D], F32, kind="Internal").ap()
                ctx1 = nc.dram_tensor("ctx1", [256, D], F32, kind="Internal").ap()
                x2_own = nc.dram_tensor("x2_own", [1024, D], F32, kind="Internal").ap()
                ydst = self.outp("y", [1024, D])
                self.layer(0, False, {"own": x_own, "oth": x_oth, "ctx": ctx_in, "blend": None}, x1_own, ctx1, None,
                           mode="full", L="L0", xkey="x1own")
                self.layer(0, False, {"own": x_oth, "oth": x_own, "ctx": ctx_in, "blend": None}, x1_oth, None, None,
                           mode="B", L="L0B", xkey="x1oth")
                self.layer(1, True, {"own": x1_own, "oth": x1_oth, "ctx": ctx1, "blend": None, "own_key": "x1own",
                                     "oth_key": "x1oth"}, x2_own, None, ydst, mode="full", L="L1", xkey="x2own")
            P.add("sp", None, r=list(self.final_keys))
            P.emit()
        return nc

    def layer(self, l, last, src, xdst, cdst, ydst, mode="full", L=None, xkey="xdst"):
        nc, P, B = self.nc, self.P, self.B
        hxA = self.hxA
        lam_init = 0.8 - 0.6 * math.exp(-0.3 * l)
        has_ctx = (not last) and mode == "full"
        passB = mode == "B"
        mL, mR = (1, 0) if passB else (0, 1)
        qoff = 1024 if passB else 0
        with self.scope() as les:
            wmod = self.inp(f"wmod{l}", [D, 3 * D]).rearrange("(kc p) n -> p kc n", p=128)
            bmod = self.inp(f"bmod{l}", [1, 3 * D])
            ng_d = self.inp(f"ng{l}", [128, 16])
            win = self.inp(f"win{l}", [D, NIN]).rearrange("(kc p) n -> p kc n", p=128)
            lam_d = self.inp(f"lam{l}", [1, 256])
            subg_d = self.inp(f"subg{l}", [128, 1])
            wpool_d = self.inp(f"wpool{l}", [4, 128, 128])
            pscale_d = self.inp(f"pscale{l}", [128, 4])
            wdw_d = self.inp(f"wdw{l}", [128, 4, 31])
            cvec_d = self.inp(f"cvec{l}", [128, 12])
            wpw2_d = self.inp(f"wpw2{l}", [512, 512]).rearrange("(c p) n -> p c n", p=128)
            wout = self.inp(f"wout{l}", [D, D]).rearrange("(kc p) n -> p kc n", p=128)
            fg_d = self.inp("fg", [1, D]) if last else None

            sm = self.sb(les, [128, 64], F32)
            ng = self.sb(les, [128, 16], F32)
            lamb = self.sb(les, [128, 256], F32)
            subg = self.sb(les, [128, 1], F32)
            pscale = self.sb(les, [128, 4], F32)
            wdw = self.sb(les, [128, 4, 31], F32)
            cvec = self.sb(les, [128, 12], F32)
            wpool_b = self.sb(les, [128, 4, 128], BF16)
            wpw2_b = self.sb(les, [128, 4, 512], BF16)
            modcol, gs = self.modsave[l]
            ML = f"M{l}"
            junk128 = self.sb(les, [128, 128], F32)
            L = L or f"L{l}"
            self.dma("sp", ng[:], ng_d, r=[], w=[L + "ng"], sem="p0")
            self.dma("sp", lamb[:], lam_d.partition_broadcast(128), r=[], w=[L + "lamb"], sem="p1")
            self.dma("sp", subg[:], subg_d, r=[], w=[L + "subg"], sem="p2")
            self.dma("sp", pscale[:], pscale_d, r=[], w=[L + "pscale"], sem="p3")
            self.dma("sp", wdw[:], wdw_d, r=[], w=[L + "wdw"], sem="p4")
            self.dma("sp", cvec[:], cvec_d, r=[], w=[L + "cvec"], sem="p5")
            self.dma("pool", wpool_b[:], wpool_d.rearrange("g c d -> c g d"), r=[], w=[L + "wpool"], sem="p6")
            self.dma("pool", wpw2_b[:], wpw2_d, r=[], w=[L + "wpw2"], sem="p7")
            self.memset("dve", sm[:], 0.0, w=[L + "sm"])
            self.stt("dve", junk128[:, 0:64], lamb[:, 0:64], 1.0, lamb[:, 64:128], ALU.mult, ALU.mult,
                     r=[L + "lamb", L + "sm"], w=[L + "junk128", L + "sm0"], accum_out=sm[:, 0:1])
            self.stt("dve", junk128[:, 64:128], lamb[:, 128:192], 1.0, lamb[:, 192:256], ALU.mult, ALU.mult,
                     r=[L + "lamb", L + "sm"], w=[L + "junk128b", L + "sm1"], accum_out=sm[:, 1:2])
            self.act(sm[:, 2:4], sm[:, 0:2], AF.Exp, r=[L + "sm0", L + "sm1"], w=[L + "sm23"])
            self.tt("dve", sm[:, 4:5], sm[:, 2:3], sm[:, 3:4], ALU.subtract, r=[L + "sm23"], w=[L + "sm4"])
            self.ts("dve", sm[:, 5:6], sm[:, 4:5], lam_init, -1.0, ALU.add, ALU.mult, r=[L + "sm4"], w=[L + "neglam"])
            self.tsm("dve", sm[:, 6:7], subg[:], 1.0 - lam_init, r=[L + "subg", L + "sm"], w=[L + "subg2"])
            neglam = sm[:, 5:6]
            subg2 = sm[:, 6:7]

            pes_mod = self.scope()
            pes_mod.__enter__()
            pes = pes_mod
            if not passB:
                s_f = self.sb(pes, [128, 32], F32)
                srep = self.sb(pes, [128, 32, 128], BF16)
                bm = [self.sb(pes, [128, 512], F32) for _ in range(2)]
                rowb = [self.sb(pes, [128, 512], F32) for _ in range(2)]
                self.act(s_f[:], self.cT[:], AF.Silu, r=["cT"], w=[L + "s_f"])
                for j in range(32):
                    self.tsm("dve", srep[:, j, :], self.ones_b[:], s_f[:, j:j + 1], r=["ones_b", L + "s_f"],
                             w=[(L + "srep", j)])
                for r_ in range(2):
                    self.memset("dve", modcol[r_][:], 0.0, w=[(ML + "modcol", r_)])
                for g in range(8):
                    i = self.wload(wmod, g * 512)
                    self.dma("sp", bm[g % 2][:], bmod[0:1, g * 512:(g + 1) * 512].partition_broadcast(128),
                             r=[], w=[(L + "bm", g % 2)], sem=f"bm{g%2}")
                    for r_ in range(2):
                        bi = self.bank()
                        self.mm16(bi, 512, lambda kc: srep[:, r_ * 16 + kc, :], lambda kc: self.wb[i][:, kc, :],
                                  r=[("wb", i)] + [(L + "srep", r_ * 16 + kc) for kc in range(16)])
                        self.tt("dve", rowb[r_][:], B[bi][:], bm[g % 2][:], ALU.add,
                                r=[("ps", bi), (L + "bm", g % 2)], w=[(L + "rowb", r_)])
                        for j in range(4):
                            c = g * 4 + j
                            self.stt("dve", junk128[:], rowb[r_][:, j * 128:(j + 1) * 128], 1.0, self.ident_f[:],
                                     ALU.mult, ALU.mult, r=[(L + "rowb", r_), "ident_f", (ML + "modcol", r_)],
                                     w=[L + "junk128", (ML + "modcolc", r_, c)], accum_out=modcol[r_][:, c:c + 1])
                for r_ in range(2):
                    rk = [(ML + "modcolc", r_, c) for c in range(16, 32)]
                    self.ts("dve", gs[r_][:], modcol[r_][:, 16:32], 1.0, 1.0, ALU.add, ALU.mult,
                            r=rk, w=[(ML + "gs0", r_)])
                    self.tt("dve", gs[r_][:], gs[r_][:], ng[:], ALU.mult, r=[(ML + "gs0", r_), L + "ng"],
                            w=[(ML + "gs", r_)])
            shiftk = lambda r_: [(ML + "modcolc", r_, c) for c in range(16)]
            self.final_keys = set()
            if self.debug and not passB:
                dm = self.outp(f"dbg_mod{l}", [128, 96], F32)
                allmk = [(ML + "modcolc", r_, c) for r_ in range(2) for c in range(32)] + [(ML + "gs", 0), (ML + "gs", 1)]
                self.dma("sp", dm[:, 0:32], modcol[0][:], r=allmk, w=["dm0"], sem="dbgm0")
                self.dma("sp", dm[:, 32:64], modcol[1][:], r=allmk, w=["dm1"], sem="dbgm1")
                self.dma("sp", dm[:, 64:80], gs[0][:], r=allmk, w=["dm2"], sem="dbgm2")
                self.dma("sp", dm[:, 80:96], gs[1][:], r=allmk, w=["dm3"], sem="dbgm3")
                self.final_keys |= {"dm0", "dm1", "dm2", "dm3"}
                P.barrier()
            if self.stop == "MOD1":
                return

            with self.scope() as pes:
                xt = [self.sb(pes, [128, D], F32) for _ in range(2)]
                xt2 = [self.sb(pes, [128, D], F32) for _ in range(2)] if src["blend"] is not None else None
                xn = [self.sb(pes, [128, D], BF16) for _ in range(2)]
                junkb = self.sb(pes, [128, D], BF16)
                st = self.sb(pes, [128, 4, 18], F32)
                evt = [self.sb(pes, [128, 8, 128], F32) for _ in range(2)]
                self.memset("dve", st[:], 0.0, w=[L + "st"])
                hx_tiles = [2, 3, 4, 5, 6, 7, 8, 9, 10, 17] if passB else list(range(18))

                def hx_s1(a):
                        r_ = 1 if a < 2 else 0
                        x = xt[a % 2]
                        xk = (L + "xt", a % 2)
                        if a < 2:
                            sap = src["ctx"][a * 128:(a + 1) * 128, :]
                            self.dma("sp", x[:], sap, r=[("cdst", a)], w=[xk], sem=f"xt{a%2}")
                        elif a < 10:
                            t = a - 2
                            self.dma("sp", x[:], src["own"][t * 128:(t + 1) * 128, :], r=[(src.get("own_key", "none"), t)], w=[xk],
                                     sem=f"xt{a%2}")
                        else:
                            t = a - 10
                            if src["blend"] is None:
                                self.dma("sp", x[:], src["oth"][t * 128:(t + 1) * 128, :], r=[(src.get("oth_key", "none"), t)], w=[xk], sem=f"xt{a%2}")
                            else:
                                x2 = xt2[a % 2]
                                x2k = (L + "xt2", a % 2)
                                self.dma("sp", x[:], src["blend"][t * 128:(t + 1) * 128, :], r=["recv"], w=[xk],
                                         sem=f"xt{a%2}")
                                self.dma("sp", x2[:], src["blend"][1024 + t * 128:1024 + (t + 1) * 128, :], r=["recv"],
                                         w=[x2k], sem=f"xtb{a%2}")
                                self.tsm("dve", x[:], x[:], self.msk[:, 2:3], r=[xk, "msk"], w=[xk])
                                self.stt("dve", x[:], x2[:], self.msk[:, 3:4], x[:], ALU.mult, ALU.add,
                                         r=[xk, x2k, "msk"], w=[xk])
                        self.act(junkb[:], x[:], AF.Square, r=[xk, L + "st"], w=[L + "junkb", (L + "ss", a)],
                                 accum_out=st[:, 0, a:a + 1])
                        self.ts("dve", st[:, 1, a:a + 1], st[:, 0, a:a + 1], 1.0 / D, EPS, ALU.mult, ALU.add,
                                r=[(L + "ss", a)], w=[(L + "ms", a)])

                def hx_s2(a):
                        xk = (L + "xt", a % 2)
                        x = xt[a % 2]
                        self.act(st[:, 2, a:a + 1], st[:, 1, a:a + 1], AF.Sqrt, r=[(L + "ms", a)], w=[(L + "sd", a)])
                        self.recip(st[:, 3, a:a + 1], st[:, 2, a:a + 1], r=[(L + "sd", a)], w=[(L + "rstd", a)])
                        xnk = (L + "xn", a % 2)
                        self.act(xn[a % 2][:], x[:], AF.Identity, r=[xk, (L + "rstd", a)], w=[xnk],
                                 scale=st[:, 3, a:a + 1])

                def hx_s3(a):
                        r_ = 1 if a < 2 else 0
                        xnk = (L + "xn", a % 2)
                        b0 = (a % 2) * 2
                        for kc in range(16):
                            bi = b0 + kc // 8
                            pv = B[bi][:].bitcast(BF16)
                            self.tr(pv[:, (kc % 8) * 128:(kc % 8 + 1) * 128], xn[a % 2][:, kc * 128:(kc + 1) * 128],
                                    self.ident_b[:], r=[xnk, "ident_b"], w=[("ps", bi)])
                        for half in range(2):
                            bi = b0 + half
                            pv3 = B[bi][:].bitcast(BF16).rearrange("p (k t) -> p k t", k=8)
                            kc0 = half * 8
                            o3 = hxA[:, kc0:kc0 + 8, a * 128:(a + 1) * 128]
                            gs_b = gs[r_][:, kc0:kc0 + 8].unsqueeze(2).to_broadcast([128, 8, 128])
                            sh_b = modcol[r_][:, kc0:kc0 + 8].unsqueeze(2).to_broadcast([128, 8, 128])
                            tq = evt[(a + half) % 2]
                            tqk = (L + "evt", (a + half) % 2)
                            rr = [("ps", bi), (ML + "gs", r_)] + shiftk(r_)
                            self.tt("dve", tq[:], pv3, gs_b, ALU.mult, r=rr, w=[tqk])
                            self.tt("dve", o3, tq[:], sh_b, ALU.add, r=[tqk] + shiftk(r_), w=[("hx", a)])

                for i_t, a in enumerate(hx_tiles):
                    hx_s1(a)
                    if i_t >= 1:
                        hx_s2(hx_tiles[i_t - 1])
                        hx_s3(hx_tiles[i_t - 1])
                hx_s2(hx_tiles[-1])
                hx_s3(hx_tiles[-1])
                P.barrier()
            pes_mod.__exit__(None, None, None)
            if self.debug and not passB:
                dh = self.outp(f"dbg_hx{l}", [128, 16, 2304], BF16)
                self.dma("sp", dh, hxA[:], r=[("hx", a) for a in range(18)], w=[f"dbg_hx{l}"], sem="dbg")
                P.barrier()
                self.final_keys.add(f"dbg_hx{l}")
            if self.stop == "HX":
                return

            with self.scope() as pes:
              if not passB:
                cosk = self.sb(pes, [128, 2048], F32)
                sink = self.sb(pes, [128, 2048], F32)
                k_sb = [self.sb(pes, [128, 512], BF16) for _ in range(2)]
                t1 = [self.sb(pes, [128, 512], F32) for _ in range(2)]
                t2 = [self.sb(pes, [128, 512], F32) for _ in range(2)]
                kto = [self.sb(pes, [128, 512], BF16) for _ in range(2)]
                vst = [self.sb(pes, [128, 4, 132], BF16) for _ in range(2)]
                self.dma("sp", cosk[:], self.cosk_d, r=[], w=[L + "cosk"], sem="cosk")
                self.dma("sp", sink[:], self.sink_d, r=[], w=[L + "sink"], sem="sink")
                for j in range(2):
                    self.memset("dve", vst[j][:], 1.0, w=[(L + "vst", j)])
                cnt = 0
                for gk in (2, 3):
                    i = self.wload(win, gk * 512)
                    for hh in range(4):
                        h = (gk - 2) * 4 + hh
                        for (tok0, n, rope) in [(0, 256, False), (256, 512, True), (768, 512, True),
                                                (1280, 512, True), (1792, 512, True)]:
                            j = cnt % 2
                            cnt += 1
                            bi = self.bank()
                            self.mm16(bi, n, lambda kc: self.wb[i][:, kc, hh * 128:(hh + 1) * 128],
                                      lambda kc: hxA[:, kc, tok0:tok0 + n], r=[("wb", i)] + self.hxkeys(tok0, n))
                            if not rope or os.environ.get("KV_NOROPE"):
                                self.cp("act", kto[j][:, :n], B[bi][:, :n], r=[("ps", bi)], w=[(L + "kto", j)])
                            else:
                                RV = os.environ.get("ROPE_VAR", "")
                                self.cp("act", k_sb[j][:, :n], B[bi][:, :n], r=[("ps", bi)], w=[(L + "k_sb", j)])
                                br = self.bank()
                                if RV != "dve_only":
                                    if os.environ.get("ROPE_IDENT"):
                                        self.mm(B[br][:, :n], self.ident_b[:], k_sb[j][:, :n], True, True,
                                                r=["ident_b", (L + "k_sb", j)], w=[("ps", br)])
                                    else:
                                        self.mm(B[br][:, :n], self.rotm_b[:], k_sb[j][:, :n], True, True,
                                                r=["rotm_b", (L + "k_sb", j)], w=[("ps", br)])
                                else:
                                    br = bi
                                if RV == "mm_only":
                                    self.cp("act", kto[j][:, :n], B[br][:, :n], r=[("ps", br)], w=[(L + "kto", j)])
                                    continue
                                p0 = tok0 - 256
                                self.tt("dve", t1[j][:, :n], B[bi][:, :n], cosk[:, p0:p0 + n], ALU.mult,
                                        r=[("ps", bi), L + "cosk", (L + "k_sb", j)], w=[(L + "t1", j)])
                                self.tt("dve", t2[j][:, :n], B[br][:, :n], sink[:, p0:p0 + n], ALU.mult,
                                        r=[("ps", br), L + "sink"], w=[(L + "t2", j)])
                                self.tt("dve", kto[j][:, :n], t1[j][:, :n], t2[j][:, :n], ALU.add,
                                        r=[(L + "t1", j), (L + "t2", j)], w=[(L + "kto", j)])
                            if os.environ.get("KV_NOSTORE") and not (h == 7 and tok0 == 1792):
                                continue
                            self.dma("sp", self.kT_d[h, :, tok0:tok0 + n], kto[j][:, :n], r=[(L + "kto", j)],
                                     w=[("kT_d", h)], sem=f"kto{j}")
                if self.stop == "KVK":
                    self.final_keys |= {("kT_d", h) for h in range(8)}
                    P.barrier()
                    return
                cnt = 0
                for gv in (4, 5):
                    i = self.wload(win, gv * 512)
                    for a in range(18):
                        j = cnt % 2
                        cnt += 1
                        bi = self.bank()
                        self.mm16(bi, 512, lambda kc: hxA[:, kc, a * 128:(a + 1) * 128],
                                  lambda kc: self.wb[i][:, kc, :], r=[("wb", i), ("hx", a)])
                        self.cp("act", vst[j][:, :, 0:128], B[bi][:].rearrange("p (h e) -> p h e", h=4),
                                r=[("ps", bi)], w=[(L + "vst", j)])
                        h0 = (gv - 4) * 4
                        self.dma("sp", self.v_d[h0:h0 + 4, :, a, :].rearrange("h p e -> p h e"), vst[j][:],
                                 r=[(L + "vst", j)], w=[("v_d", h0 + q) for q in range(4)], sem=f"vst{j}")
                P.barrier()

            if self.debug:
                self.final_keys |= {("kT_d", h) for h in range(8)} | {("v_d", h) for h in range(8)}
            if self.stop == "KV":
                return
            nown = 1280 if has_ctx else 1024
            yTc = self.sb(les, [128, 16, 256], BF16) if has_ctx else None

            def ytv(kc, c0, n):
                if c0 < 1024:
                    return hxA[:, kc, 1280 + c0:1280 + c0 + n]
                return yTc[:, kc, c0 - 1024:c0 - 1024 + n]

            def ytk(kc, c0, n):
                return [("yT", kc, t) for t in range(c0 // 128, (c0 + n) // 128)]

            oblocks = [(256, 0, 512), (768, 512, 512)] + ([(0, 1024, 256)] if has_ctx else [])

            with self.scope() as cps:
                u_ext = self.sb(cps, [128, 4, 1056], BF16)
                uc_ext = self.sb(cps, [128, 4, 288], BF16) if has_ctx else None
                pps = self.scope()
                pps.__enter__()
                up_ext = self.sb(cps, [128, 4, 1056], F32)
                upc_ext = self.sb(cps, [128, 4, 288], F32) if has_ctx else None
                if has_ctx:
                    self.memset("dve", uc_ext[:], 0.0, w=[L + "uc_ext"])
                    self.memset("dve", upc_ext[:], 0.0, w=[L + "upc_ext"])
                with self.scope() as pes:
                    sig = [self.sb(pes, [128, 512], F32) for _ in range(2)]
                    ablocks = [(256, 512, False, 16, None), (768, 512, False, 528, None)]
                    if has_ctx:
                        ablocks.append((0, 256, True, 16, None))
                    ablocks += [(2288, 16, False, 0, mL), (1280, 16, False, 1040, mR)]
                    iA = self.wload(win, 5120)
                    iB = self.wload(win, 5632)
                    cnt = 0
                    for j in range(4):
                        for (tok0, n, isc, off, mc) in ablocks:
                            q = cnt % 2
                            cnt += 1
                            ba = self.bank()
                            self.mm16(ba, n, lambda kc: self.wb[iA][:, kc, j * 128:(j + 1) * 128],
                                      lambda kc: hxA[:, kc, tok0:tok0 + n], r=[("wb", iA)] + self.hxkeys(tok0, n))
                            bb = self.bank()
                            self.mm16(bb, n, lambda kc: self.wb[iB][:, kc, j * 128:(j + 1) * 128],
                                      lambda kc: hxA[:, kc, tok0:tok0 + n], r=[("wb", iB)] + self.hxkeys(tok0, n))
                            self.act(sig[q][:, :n], B[bb][:, :n], AF.Sigmoid, r=[("ps", bb)], w=[(L + "sig", q)])
                            dst = (uc_ext if isc else u_ext)[:, j, off:off + n]
                            dk = L + ("uc_ext" if isc else "u_ext")
                            if mc is None:
                                self.tt("dve", dst, B[ba][:, :n], sig[q][:, :n], ALU.mult,
                                        r=[("ps", ba), (L + "sig", q), dk], w=[dk])
                            else:
                                self.stt("dve", dst, B[ba][:, :n], self.msk[:, mc:mc + 1], sig[q][:, :n],
                                         ALU.mult, ALU.mult, r=[("ps", ba), (L + "sig", q), "msk", dk], w=[dk])
                    i8 = self.wload(win, 4096)
                    for j in range(4):
                        for (tok0, n, isc, off, mc) in ablocks:
                            ba = self.bank()
                            self.mm16(ba, n, lambda kc: self.wb[i8][:, kc, j * 128:(j + 1) * 128],
                                      lambda kc: hxA[:, kc, tok0:tok0 + n], r=[("wb", i8)] + self.hxkeys(tok0, n))
                            dst = (upc_ext if isc else up_ext)[:, j, off:off + n]
                            dk = L + ("upc_ext" if isc else "up_ext")
                            if mc is None:
                                self.cp("act", dst, B[ba][:, :n], r=[("ps", ba), dk], w=[dk])
                            else:
                                self.act(dst, B[ba][:, :n], AF.Identity, r=[("ps", ba), "msk", dk], w=[dk],
                                         scale=self.msk[:, mc:mc + 1])
                    P.barrier()

                with self.scope() as pes:
                    sa = [self.sb(pes, [128, 1056], F32) for _ in range(2)]
                    va = [self.sb(pes, [128, 1056], F32) for _ in range(3)]
                    dT = self.sb(pes, [128, 4, nown], BF16)
                    for q_ in range(2):
                        self.memset("dve", sa[q_][:], 0.0, w=[L + "sa" + str(q_)])
                        self.memset("dve", va[q_][:], 0.0, w=[L + "va" + str(q_)])
                    sgp = self.sb(pes, [128, 4, nown], BF16)
                    ig = self.wload(win, 4608)
                    for j in range(4):
                        for (tok0, c0, n) in oblocks:
                            ba = self.bank()
                            self.mm16(ba, n, lambda kc: self.wb[ig][:, kc, j * 128:(j + 1) * 128],
                                      lambda kc: hxA[:, kc, tok0:tok0 + n],
                                      r=[("wb", ig)] + self.hxkeys(tok0, n))
                            self.act(sgp[:, j, c0:c0 + n], B[ba][:, :n], AF.Silu, r=[("ps", ba), L + "sgp"],
                                     w=[L + "sgp"])
                    segs = [(up_ext, L + "up_ext", 1056, 1024, 0, True)]
                    if has_ctx:
                        segs.append((upc_ext, L + "upc_ext", 288, 256, 1024, False))
                    for (U, uk, E, N, c0, is_lat) in segs:
                        V = va[2]
                        self.memset("dve", V[:, :E], 1.0 if is_lat else 0.0, w=[L + "V"])
                        if is_lat:
                            self.tsm("dve", V[:, 0:16], V[:, 0:16], self.msk[:, mL:mL + 1], r=[L + "V", "msk"], w=[L + "V"])
                            self.tsm("dve", V[:, 1040:1056], V[:, 1040:1056], self.msk[:, mR:mR + 1], r=[L + "V", "msk"],
                                     w=[L + "V"])
                        else:
                            self.memset("dve", V[:, 16:16 + N], 1.0, w=[L + "V"])
                        for g in range(4):
                            def steps(src_ap, bufs, keyp, srck):
                                cur = src_ap
                                ck = srck
                                for i in range(g + 1):
                                    nb = bufs[i % 2]
                                    nk = keyp + str(i % 2)
                                    if i == 0:
                                        self.tt("dve", nb[:, 1:E], cur[:, 0:E - 1], cur[:, 1:E], ALU.add,
                                                r=[ck, nk], w=[nk])
                                    else:
                                        sh = 1 << (i - 1)
                                        self.tt("dve", nb[:, sh:E - sh], cur[:, 0:E - 2 * sh], cur[:, 2 * sh:E],
                                                ALU.add, r=[ck, nk], w=[nk])
                                    cur = nb
                                    ck = nk
                                return cur, ck
                            s_fin, sk_ = steps(U[:, g, :], sa, L + "sa", uk)
                            v_fin, vk_ = steps(V, va, L + "va", L + "V")
                            self.recip(v_fin[:, 16:16 + N], v_fin[:, 16:16 + N], r=[vk_], w=[vk_])
                            self.tt("dve", s_fin[:, 16:16 + N], s_fin[:, 16:16 + N], v_fin[:, 16:16 + N], ALU.mult,
                                    r=[sk_, vk_], w=[sk_])
                            self.tt("dve", dT[:, g, c0:c0 + N], s_fin[:, 16:16 + N], U[:, g, 16:16 + N],
                                    ALU.subtract, r=[sk_, uk, L + "dT"], w=[L + "dT"])
                    for g in range(4):
                        for (tok0, c0, n) in oblocks:
                            ba = self.bank()
                            self.mm(B[ba][:, :n], wpool_b[:, g, :], dT[:, g, c0:c0 + n], True, True,
                                    r=[L + "wpool", L + "dT"], w=[("ps", ba)])
                            self.stt("dve", ytv(8 + g, c0, n), B[ba][:, :n], pscale[:, g:g + 1], sgp[:, g, c0:c0 + n],
                                     ALU.mult, ALU.mult, r=[("ps", ba), L + "pscale", L + "sgp"],
                                     w=ytk(8 + g, c0, n))
                    P.barrier()
                pps.__exit__(None, None, None)

                with self.scope() as pes:
                    diag = self.sb(pes, [128, 4, 31, 128], BF16)
                    ybuf = self.sb(pes, [128, 4, 512], F32)
                    ysq = self.sb(pes, [128, 4, 512], F32)
                    mst = self.sb(pes, [128, 4, 512], F32)
                    sT = self.sb(pes, [128, 4, 512], BF16)
                    sgc = self.sb(pes, [128, 4, nown], BF16)
                    ig = self.wload(win, 6144)
                    for j in range(4):
                        for (tok0, c0, n) in oblocks:
                            ba = self.bank()
                            self.mm16(ba, n, lambda kc: self.wb[ig][:, kc, j * 128:(j + 1) * 128],
                                      lambda kc: hxA[:, kc, tok0:tok0 + n],
                                      r=[("wb", ig)] + self.hxkeys(tok0, n))
                            self.act(sgc[:, j, c0:c0 + n], B[ba][:, :n], AF.Silu, r=[("ps", ba), L + "sgc"],
                                     w=[L + "sgc"])
                    for c in range(4):
                        for k in range(31):
                            self.tsm("dve", diag[:, c, k, :], self.ident_b[:], wdw[:, c, k:k + 1],
                                     r=["ident_b", L + "wdw"], w=[(L + "diag", c)])
                    cb = [4, 5, 6, 7]
                    for (tok0, c0, n) in oblocks:
                        isc = c0 >= 1024
                        ue = uc_ext if isc else u_ext
                        uk = L + ("uc_ext" if isc else "u_ext")
                        e0 = (c0 - 1024) if isc else c0
                        for c in range(4):
                            bi = cb[c]
                            for k in range(31):
                                self.mm(B[bi][:, :n], diag[:, c, k, :], ue[:, c, e0 + k + 1:e0 + k + 1 + n],
                                        k == 0, k == 30, r=[(L + "diag", c), uk], w=[("ps", bi)])
                            self.act(ybuf[:, c, :n], B[bi][:, :n], AF.Identity, r=[("ps", bi), L + "cvec"],
                                     w=[(L + "ybuf", c)], bias=cvec[:, c:c + 1])
                            self.act(ysq[:, c, :n], B[bi][:, :n], AF.Square, r=[("ps", bi), L + "cvec"],
                                     w=[(L + "ysq", c)], bias=cvec[:, c:c + 1])
                        b1, b2 = 0, 1
                        for c in range(4):
                            self.mm(B[b1][:, :n], self.ones_f[:], ybuf[:, c, :n], c == 0, c == 3,
                                    r=["ones_f", (L + "ybuf", c)], w=[("ps", b1)])
                        for c in range(4):
                            self.mm(B[b2][:, :n], self.ones_f[:], ysq[:, c, :n], c == 0, c == 3,
                                    r=["ones_f", (L + "ysq", c)], w=[("ps", b2)])
                        self.ts("dve", mst[:, 0, :n], B[b1][:, :n], 1.0 / 512, 0.0, ALU.mult, ALU.add,
                                r=[("ps", b1)], w=[L + "m0"])
                        self.tt("dve", mst[:, 1, :n], mst[:, 0, :n], mst[:, 0, :n], ALU.mult, r=[L + "m0"], w=[L + "m1"])
                        self.stt("dve", mst[:, 2, :n], B[b2][:, :n], 1.0 / 512, mst[:, 1, :n], ALU.mult, ALU.subtract,
                                 r=[("ps", b2), L + "m1"], w=[L + "m2"])
                        self.ts("dve", mst[:, 2, :n], mst[:, 2, :n], EPS, 0.0, ALU.add, ALU.add,
                                r=[L + "m2"], w=[L + "m2"])
                        self.act(mst[:, 2, :n], mst[:, 2, :n], AF.Sqrt, r=[L + "m2"], w=[L + "m2"])
                        self.recip(mst[:, 3, :n], mst[:, 2, :n], r=[L + "m2"], w=[L + "m3"])
                        for c in range(4):
                            self.tt("dve", ybuf[:, c, :n], ybuf[:, c, :n], mst[:, 0, :n], ALU.subtract,
                                    r=[(L + "ybuf", c), L + "m0"], w=[(L + "ybuf", c)])
                            self.tt("dve", ybuf[:, c, :n], ybuf[:, c, :n], mst[:, 3, :n], ALU.mult,
                                    r=[(L + "ybuf", c), L + "m3"], w=[(L + "ybuf", c)])
                            self.act(sT[:, c, :n], ybuf[:, c, :n], AF.Silu, r=[(L + "ybuf", c), L + "cvec"],
                                     w=[(L + "sT", c)], scale=cvec[:, 4 + c:5 + c], bias=cvec[:, 8 + c:9 + c])
                        for j in range(4):
                            ba = 2 + (j % 2)
                            for c in range(4):
                                self.mm(B[ba][:, :n], wpw2_b[:, c, j * 128:(j + 1) * 128], sT[:, c, :n], c == 0, c == 3,
                                        r=[L + "wpw2", (L + "sT", c)], w=[("ps", ba)])
                            self.tt("dve", ytv(12 + j, c0, n), B[ba][:, :n], sgc[:, j, c0:c0 + n], ALU.mult,
                                    r=[("ps", ba), L + "sgc"], w=ytk(12 + j, c0, n))
                    P.barrier()

            with self.scope() as pes:
                cosq = self.sb(pes, [128, 1024], F32)
                sinq = self.sb(pes, [128, 1024], F32)
                qm = [[self.sb(pes, [128, nown], BF16) for _ in range(2)] for _ in range(2)]
                sgT = [self.sb(pes, [128, nown], BF16) for _ in range(2)]
                kTs = [self.sb(pes, [128, 2304], BF16) for _ in range(2)]
                Vhs = [self.sb(pes, [128, 18, 132], BF16) for _ in range(2)]
                k_sb = self.sb(pes, [128, 512], BF16)
                t1 = self.sb(pes, [128, 512], F32)
                t2 = self.sb(pes, [128, 512], F32)
                NPT = 6
                PT = [self.sb(pes, [128, 512], BF16) for _ in range(NPT)]
                Osb = self.sb(pes, [128, 2, 4, 132], F32)
                osm = self.sb(pes, [128, 8, 16], F32)
                o_sb = [self.sb(pes, [128, 128], F32) for _ in range(4)]
                on_b = [self.sb(pes, [128, 128], BF16) for _ in range(4)]
                junko4 = [self.sb(pes, [128, 128], F32) for _ in range(4)]
                self.dma("sp", cosq[:], self.cosk_d[:, qoff:qoff + 1024], r=[], w=[L + "cosq"], sem="cosq")
                self.dma("sp", sinq[:], self.sink_d[:, qoff:qoff + 1024], r=[], w=[L + "sinq"], sem="sinq")
                self.memset("dve", osm[:], 0.0, w=[L + "osm"])
                for hb_ in range(2):
                    for c_ in range(2):
                        self.memset("dve", qm[hb_][c_][:], 0.0, w=[(L + "qT", hb_, c0_) for c0_ in (0, 512, 1024)])
                gbs = [0, 1]
                gci = 0
                pendA = [None]
                pendB = [None]
                since = [0]
                ptc = 0
                sci = 0
                tcount = 0
                wl = {}

                def inproj(h):
                        nonlocal gci
                        hp, hh = divmod(h, 4)
                        hb = h % 2
                        if hh == 0:
                            wl[hp] = (self.wload(win, hp * 512), self.wload(win, 3072 + hp * 512))
                        iq, igt = wl[hp]
                        for (tok0, c0, n) in oblocks:
                            bi = gbs[gci % 2]; gci += 1
                            self.mm16(bi, n, lambda kc: self.wb[iq][:, kc, hh * 128:(hh + 1) * 128],
                                      lambda kc: hxA[:, kc, tok0:tok0 + n], r=[("wb", iq)] + self.hxkeys(tok0, n))
                            qk = (L + "qT", hb, c0)
                            if c0 >= 1024:
                                self.cp("act", qm[hb][0][0:64, c0:c0 + n], B[bi][0:64, :n], r=[("ps", bi)], w=[qk])
                                self.cp("act", qm[hb][1][64:128, c0:c0 + n], B[bi][64:128, :n], r=[("ps", bi)], w=[qk])
                            else:
                                self.cp("act", k_sb[:, :n], B[bi][:, :n], r=[("ps", bi)], w=[L + "qk_sb"])
                                br = gbs[gci % 2]; gci += 1
                                self.mm(B[br][:, :n], self.rotm_b[:], k_sb[:, :n], True, True,
                                        r=["rotm_b", L + "qk_sb"], w=[("ps", br)])
                                self.tt("dve", t1[:, :n], B[bi][:, :n], cosq[:, c0:c0 + n], ALU.mult,
                                        r=[("ps", bi), L + "cosq", L + "qk_sb"], w=[L + "qt1"])
                                self.tt("dve", t2[:, :n], B[br][:, :n], sinq[:, c0:c0 + n], ALU.mult,
                                        r=[("ps", br), L + "sinq"], w=[L + "qt2"])
                                self.tt("dve", qm[hb][0][0:64, c0:c0 + n], t1[0:64, :n], t2[0:64, :n], ALU.add,
                                        r=[L + "qt1", L + "qt2"], w=[qk])
                                self.tt("dve", qm[hb][1][64:128, c0:c0 + n], t1[64:128, :n], t2[64:128, :n], ALU.add,
                                        r=[L + "qt1", L + "qt2"], w=[qk])
                            bi = gbs[gci % 2]; gci += 1
                            self.mm16(bi, n, lambda kc: self.wb[igt][:, kc, hh * 128:(hh + 1) * 128],
                                      lambda kc: hxA[:, kc, tok0:tok0 + n], r=[("wb", igt)] + self.hxkeys(tok0, n))
                            self.act(sgT[hb][:, c0:c0 + n], B[bi][:, :n], AF.Silu, r=[("ps", bi)],
                                     w=[(L + "sgT", hb, c0)])
                        self.dma("sp", kTs[hb][:], self.kT_d[h], r=[("kT_d", h)], w=[(L + "kT", hb)], sem=f"kT{hb}")
                        self.dma("sp", Vhs[hb][:], self.v_d[h], r=[("v_d", h)], w=[(L + "Vh", hb)], sem=f"Vh{hb}")

                inproj(0)
                for h in range(8):
                        hb = h % 2
                        kT = kTs[hb]
                        Vh = Vhs[hb]
                        qblocks = [(0, 512, list(range(18))), (512, 512, list(range(18)))]
                        if has_ctx:
                            qblocks.append((1024, 256, [0, 1]))
                        for qbi, (c0, n, kts) in enumerate(qblocks):
                            if qbi == 1 and h + 1 < 8:
                                inproj(h + 1)
                            nqs = n // 128
                            items = [(c, ki, kt) for c in range(2) for ki, kt in enumerate(kts)]

                            def emit_pv(it, pi):
                                c, ki, kt = it
                                for qs in range(nqs):
                                    self.mm(B[4 + qs][:, 0:129], PT[pi][:, qs * 128:(qs + 1) * 128], Vh[:, kt, 0:129],
                                            ki == 0, ki == len(kts) - 1, r=[(L + "PT", pi), (L + "Vh", hb)],
                                            w=[("ps", 4 + qs)])
                                if ki == len(kts) - 1:
                                    if pendA[0] is not None:
                                        pendA[0](); pendA[0] = None
                                    for qs in range(nqs):
                                        self.cp("dve", Osb[:, c, qs, 0:129], B[4 + qs][:, 0:129], r=[("ps", 4 + qs)],
                                                w=[(L + "Osb", c, qs)])
                            pendq = []
                            for it in items:
                                c, ki, kt = it
                                sb_ = 1 + (sci % 3); sci += 1
                                self.mm(B[sb_][:, :n], kT[:, kt * 128:(kt + 1) * 128],
                                        qm[hb][c][:, c0:c0 + n], True, True,
                                        r=[(L + "kT", hb), (L + "qT", hb, c0)], w=[("ps", sb_)])
                                pi = ptc % NPT; ptc += 1
                                self.act(PT[pi][:, :n], B[sb_][:, :n], AF.Exp, r=[("ps", sb_)], w=[(L + "PT", pi)],
                                         scale=0.125)
                                if len(pendq) == 2:
                                    emit_pv(*pendq.pop(0))
                                pendq.append((it, pi))
                                since[0] += 1
                                if since[0] == 3 and pendA[0] is not None:
                                    pendA[0](); pendA[0] = None
                                if since[0] == 10 and pendB[0] is not None:
                                    if pendA[0] is not None:
                                        pendA[0](); pendA[0] = None
                                    pendB[0](); pendB[0] = None
                            while pendq:
                                emit_pv(*pendq.pop(0))
                            def make_epi(h=h, hb=hb, c0=c0, nqs=nqs):
                                def epiA():
                                    Q = range(nqs)
                                    okr = lambda qs: [(L + "Osb", 0, qs), (L + "Osb", 1, qs)]
                                    k = lambda nm, qs: (L + "osm_" + nm, qs)
                                    for qs in Q:
                                        self.recip(osm[:, qs, 0:1], Osb[:, 0, qs, 128:129], r=okr(qs) + [L + "osm"], w=[k("rz0", qs)])
                                    for qs in Q:
                                        self.recip(osm[:, qs, 1:2], Osb[:, 1, qs, 128:129], r=okr(qs) + [L + "osm"], w=[k("rz1", qs)])
                                    for qs in Q:
                                        self.tt("dve", osm[:, qs, 2:3], osm[:, qs, 1:2], neglam, ALU.mult,
                                                r=[k("rz1", qs), L + "neglam"], w=[k("nl", qs)])
                                    for qs in Q:
                                        self.tsm("dve", o_sb[qs][:], Osb[:, 0, qs, 0:128], osm[:, qs, 0:1], r=okr(qs) + [k("rz0", qs)],
                                                 w=[(L + "o_sb", qs)])
                                    for qs in Q:
                                        self.stt("dve", o_sb[qs][:], Osb[:, 1, qs, 0:128], osm[:, qs, 2:3], o_sb[qs][:],
                                                 ALU.mult, ALU.add, r=okr(qs) + [k("nl", qs), (L + "o_sb", qs)], w=[(L + "o_sb", qs)])
                                    for qs in Q:
                                        self.stt("dve", junko[:, qs * 32:qs * 32 + 32].bitcast(F32) if False else junko4[qs][:], o_sb[qs][:], 1.0, o_sb[qs][:], ALU.mult, ALU.mult,
                                                 r=[(L + "o_sb", qs), L + "osm"], w=[(L + "junko", qs), k("ss", qs)], accum_out=osm[:, qs, 3:4])
                                    for qs in Q:
                                        self.ts("dve", osm[:, qs, 4:5], osm[:, qs, 3:4], 1.0 / 128, EPS, ALU.mult, ALU.add,
                                                r=[k("ss", qs)], w=[k("ms", qs)])
                                    for qs in Q:
                                        self.act(osm[:, qs, 5:6], osm[:, qs, 4:5], AF.Ln, r=[k("ms", qs)], w=[k("ln", qs)])
                                    for qs in Q:
                                        self.act(osm[:, qs, 6:7], osm[:, qs, 5:6], AF.Exp, r=[k("ln", qs)], w=[k("rstd", qs)], scale=-0.5)
                                    for qs in Q:
                                        self.tsm("dve", on_b[qs][:], o_sb[qs][:], osm[:, qs, 6:7], r=[(L + "o_sb", qs), k("rstd", qs)],
                                                 w=[(L + "on_b", qs)])

                                def epiB():
                                    nonlocal gci
                                    for qs in range(nqs):
                                        tcol = c0 + qs * 128
                                        bt = 0
                                        pv = B[bt][:].bitcast(BF16)
                                        self.tr(pv[:, (qs % 4) * 128:(qs % 4 + 1) * 128], on_b[qs][:], self.ident_b[:], r=[(L + "on_b", qs), "ident_b"],
                                                w=[("ps", bt)])
                                        self.stt("dve", ytv(h, tcol, 128), pv[:, (qs % 4) * 128:(qs % 4 + 1) * 128], subg2, sgT[hb][:, tcol:tcol + 128],
                                                 ALU.mult, ALU.mult,
                                                 r=[("ps", bt), L + "subg2", (L + "sgT", hb, (tcol // 512) * 512 if tcol < 1024 else 1024)],
                                                 w=ytk(h, tcol, 128))
                                return epiA, epiB
                            if pendA[0] is not None:
                                pendA[0](); pendA[0] = None
                            if pendB[0] is not None:
                                pendB[0](); pendB[0] = None
                            eA, eB = make_epi()
                            pendA[0] = eA
                            pendB[0] = eB
                            since[0] = 0
                if pendA[0] is not None:
                    pendA[0](); pendA[0] = None
                if pendB[0] is not None:
                    pendB[0](); pendB[0] = None
                P.barrier()
            if self.debug and not passB:
                dy = self.outp(f"dbg_yT{l}", [128, 16, 1024], BF16)
                self.dma("sp", dy, hxA[:, :, 1280:2304], r=[("yT", kc, t) for kc in range(16) for t in range(8)],
                         w=[f"dbg_yT{l}"], sem="dbg2")
                P.barrier()
                self.final_keys.add(f"dbg_yT{l}")
            if self.stop == "ATT":
                return

            with self.scope() as pes:
                nr = 2 if has_ctx else 1
                gate = [self.sb(pes, [128, D], F32) for _ in range(nr)]
                bm = [self.sb(pes, [128, 512], F32) for _ in range(2)]
                s_f = self.sb(pes, [128, 32], F32)
                srep = self.sb(pes, [128, 32, 128], BF16)
                xs = [self.sb(pes, [128, 512], F32) for _ in range(3)]
                xo = [self.sb(pes, [128, 512], F32) for _ in range(3)]
                self.act(s_f[:], self.cT[:], AF.Silu, r=["cT"], w=[L + "s_f2"])
                for j in range(16 * nr):
                    self.tsm("dve", srep[:, j, :], self.ones_b[:], s_f[:, j:j + 1], r=["ones_b", L + "s_f2"],
                             w=[(L + "srep2", j)])
                for g in range(8, 12):
                    i = self.wload(wmod, g * 512)
                    self.dma("sp", bm[g % 2][:], bmod[0:1, g * 512:(g + 1) * 512].partition_broadcast(128),
                             r=[], w=[(L + "bm2", g % 2)], sem=f"bmo{g%2}")
                    for r_ in range(nr):
                        bi = self.bank()
                        self.mm16(bi, 512, lambda kc: srep[:, r_ * 16 + kc, :], lambda kc: self.wb[i][:, kc, :],
                                  r=[("wb", i)] + [(L + "srep2", r_ * 16 + kc) for kc in range(16)])
                        self.tt("dve", gate[r_][:, (g - 8) * 512:(g - 7) * 512], B[bi][:], bm[g % 2][:], ALU.add,
                                r=[("ps", bi), (L + "bm2", g % 2)], w=[(L + "gate", r_, g - 8)])
                tiles = [(t, False) for t in range(8)] + ([(8, True), (9, True)] if has_ctx else [])
                cnt = 0
                for gw in range(4):
                    i = self.wload(wout, gw * 512)
                    for (t, isc) in tiles:
                        q = cnt % 3
                        cnt += 1
                        if isc:
                            tt_ = t - 8
                            sap = src["ctx"][tt_ * 128:(tt_ + 1) * 128, gw * 512:(gw + 1) * 512]
                            dap = cdst[tt_ * 128:(tt_ + 1) * 128, gw * 512:(gw + 1) * 512]
                            dk = ("cdst", tt_)
                            r_ = 1
                        else:
                            sap = src["own"][t * 128:(t + 1) * 128, gw * 512:(gw + 1) * 512]
                            dap = xdst[t * 128:(t + 1) * 128, gw * 512:(gw + 1) * 512]
                            dk = (xkey, t)
                            r_ = 0
                        self.dma("sp", xs[q][:], sap, r=[], w=[(L + "xs", q)], sem=f"xs{q}")
                        bi = self.bank()
                        c0 = t * 128
                        self.mm16(bi, 512, lambda kc: ytv(kc, c0, 128), lambda kc: self.wb[i][:, kc, :],
                                  r=[("wb", i)] + [("yT", kc, t) for kc in range(16)])
                        self.tt("dve", xo[q][:], B[bi][:], gate[r_][:, gw * 512:(gw + 1) * 512], ALU.mult,
                                r=[("ps", bi), (L + "gate", r_, gw)], w=[(L + "xo", q)])
                        self.tt("dve", xo[q][:], xo[q][:], xs[q][:], ALU.add, r=[(L + "xo", q), (L + "xs", q)],
                                w=[(L + "xo", q)])
                        self.dma("sp", dap, xo[q][:], r=[(L + "xo", q)], w=[dk], sem=f"xo{q}")
                P.barrier()
            if not last:
                if not self.fused:
                    self.final_keys |= {("xdst", t) for t in range(8)} | {("cdst", t) for t in range(2)}
            else:
                with self.scope() as pes:
                    fg = self.sb(pes, [128, D], F32)
                    xt = [self.sb(pes, [128, D], F32) for _ in range(2)]
                    yo = [self.sb(pes, [128, D], F32) for _ in range(2)]
                    junkb = self.sb(pes, [128, D], BF16)
                    st = self.sb(pes, [128, 4, 8], F32)
                    self.memset("dve", st[:], 0.0, w=[L + "fst"])
                    self.dma("sp", fg[:], fg_d.partition_broadcast(128), r=[], w=[L + "fg"], sem="fg")
                    for t in range(8):
                        q = t % 2
                        self.dma("sp", xt[q][:], xdst[t * 128:(t + 1) * 128, :], r=[(xkey, t)], w=[(L + "fxt", q)],
                                 sem=f"fxt{q}")
                        self.act(junkb[:], xt[q][:], AF.Square, r=[(L + "fxt", q), L + "fst"],
                                 w=[L + "fjunk", (L + "fss", t)], accum_out=st[:, 0, t:t + 1])
                        self.ts("dve", st[:, 1, t:t + 1], st[:, 0, t:t + 1], 1.0 / D, EPS, ALU.mult, ALU.add,
                                r=[(L + "fss", t)], w=[(L + "fms", t)])
                        self.act(st[:, 2, t:t + 1], st[:, 1, t:t + 1], AF.Sqrt, r=[(L + "fms", t)], w=[(L + "fsd", t)])
                        self.recip(st[:, 3, t:t + 1], st[:, 2, t:t + 1], r=[(L + "fsd", t)], w=[(L + "frs", t)])
                        self.stt("dve", yo[q][:], xt[q][:], st[:, 3, t:t + 1], fg[:], ALU.mult, ALU.mult,
                                 r=[(L + "fxt", q), (L + "frs", t), L + "fg"], w=[(L + "yo", q)])
                        self.dma("sp", ydst[t * 128:(t + 1) * 128, :], yo[q][:], r=[(L + "yo", q)], w=[("ydst", t)],
                                 sem=f"yo{q}")
                    self.final_keys |= {("ydst", t) for t in range(8)}
                    P.barrier()


_CACHE = {}


def _get(layers, fused, debug=False, stop=None):
    key = (tuple(layers), fused, debug, stop)
    if key not in _CACHE:
        b = Builder(list(layers), fused, debug, stop)
        nc = b.build()
        _CACHE[key] = (b, nc)
    return _CACHE[key]


def _col(v, n):
    return np.ascontiguousarray(np.asarray(v, np.float32).reshape(n, 128).T)


def _rope_tables():
    n_freq = 16
    inv = (np.float32(10000.0) ** (-np.arange(n_freq, dtype=np.float32) / np.float32(n_freq))).astype(np.float32)
    t = np.arange(2048)
    row = (t // 64).astype(np.float32)
    col = (t % 64).astype(np.float32)
    cos = np.zeros((128, 2048), np.float32)
    sin = np.zeros((128, 2048), np.float32)
    for p in range(128):
        d = p % 64
        pos = row if d < 32 else col
        j = (d % 32) % 16
        ang = (pos * inv[j]).astype(np.float32)
        cos[p] = np.cos(ang)
        sin[p] = np.sin(ang)
    return cos, sin


def _rotm():
    m = np.zeros((128, 128), np.float32)
    for mm_ in range(128):
        if (mm_ % 32) < 16:
            m[mm_ + 16, mm_] = -1.0
        else:
            m[mm_ - 16, mm_] = 1.0
    return m


def _layer_inputs(l, inp):
    d = {}
    d[f"wmod{l}"] = np.ascontiguousarray(inp["w_mod"][l], np.float32)
    d[f"bmod{l}"] = np.ascontiguousarray(inp["b_mod"][l], np.float32).reshape(1, -1)
    d[f"ng{l}"] = _col(inp["norm_g"][l], 16)
    d[f"win{l}"] = np.ascontiguousarray(inp["w_in"][l], np.float32)
    d[f"lam{l}"] = np.concatenate([inp["lambda_q1"][l], inp["lambda_k1"][l], inp["lambda_q2"][l],
                                   inp["lambda_k2"][l]]).astype(np.float32).reshape(1, 256)
    d[f"subg{l}"] = np.ascontiguousarray(np.asarray(inp["subln_g"][l], np.float32).reshape(128, 1))
    d[f"wpool{l}"] = np.ascontiguousarray(inp["w_pool"][l], np.float32)
    d[f"pscale{l}"] = _col(inp["pool_scale"][l], 4)
    wdw = np.asarray(inp["w_dw"][l], np.float32)
    d[f"wdw{l}"] = np.ascontiguousarray(wdw.T.reshape(4, 128, 31).transpose(1, 0, 2))
    d[f"cvec{l}"] = np.ascontiguousarray(np.concatenate(
        [_col(inp["b_dw"][l], 4), _col(inp["conv_ln_g"][l], 4), _col(inp["conv_ln_b"][l], 4)], axis=1))
    d[f"wpw2{l}"] = np.ascontiguousarray(inp["w_pw2"][l], np.float32)
    d[f"wout{l}"] = np.ascontiguousarray(inp["w_out"][l], np.float32)
    if l == 1:
        d["fg"] = np.asarray(inp["final_g"], np.float32).reshape(1, -1)
    return d


def _core_consts(core, inp):
    b, h = core // 2, core % 2
    cos, sin = _rope_tables()
    order = np.concatenate([np.arange(h * 1024, (h + 1) * 1024), np.arange((1 - h) * 1024, (2 - h) * 1024)])
    d = {}
    d["ident"] = np.eye(128, dtype=np.float32)
    d["rotm"] = _rotm()
    d["cosk"] = np.ascontiguousarray(cos[:, order])
    d["sink"] = np.ascontiguousarray(sin[:, order])
    m = np.zeros((128, 4), np.float32)
    m[:, 0] = 1.0 if h == 1 else 0.0
    m[:, 1] = 1.0 if h == 0 else 0.0
    m[:, 2] = 1.0 if h == 1 else 0.0
    m[:, 3] = 1.0 if h == 0 else 0.0
    d["msk"] = m
    cT = np.concatenate([_col(inp["c"][b], 16), _col(inp["c_ctx"], 16)], axis=1)
    d["cT"] = np.ascontiguousarray(cT)
    return d


def _run(layers, fused, inp, x, ctx, debug=False, stop=None):
    b_, nc = _get(layers, fused, debug, stop)
    in_maps = []
    ncr = int(os.environ.get("DBG_NCORES", NCORES)) if debug else NCORES
    for core in range(ncr):
        b, h = core // 2, core % 2
        d = _core_consts(core, inp)
        for l in layers:
            d.update(_layer_inputs(l, inp))
        d["x_own"] = np.ascontiguousarray(x[b, h * 1024:(h + 1) * 1024])
        d["x_oth"] = np.ascontiguousarray(x[b, (1 - h) * 1024:(2 - h) * 1024])
        d["ctx_in"] = np.ascontiguousarray(ctx[b])
        in_maps.append(d)
    res = run_bass_kernel_spmd(nc, in_maps, core_ids=list(range(ncr)))
    return res.results


FUSED = True


def kernel(**inp):
    inp = {k: np.asarray(v) for k, v in inp.items()}
    x = np.asarray(inp["x"], np.float32)
    ctx = np.asarray(inp["ctx"], np.float32)
    if FUSED:
        res = _run([0, 1], True, inp, x, ctx)
    else:
        r0 = _run([0], False, inp, x, ctx)
        x1 = np.empty_like(x)
        ctx1 = np.empty_like(ctx)
        for core in range(NCORES):
            b, h = core // 2, core % 2
            x1[b, h * 1024:(h + 1) * 1024] = r0[core]["x1_own"]
            if h == 0:
                ctx1[b] = r0[core]["ctx1"]
        res = _run([1], False, inp, x1, ctx1)
    out = np.empty((4, 2048, 2048), np.float32)
    for core in range(NCORES):
        b, h = core // 2, core % 2
        out[b, h * 1024:(h + 1) * 1024] = res[core]["y"]
    return out
```

```python
import math
import os
import numpy as np
import ml_dtypes
from contextlib import ExitStack
import concourse.bass as bass
import concourse.mybir as mybir
from concourse.bass_utils import run_bass_kernel_spmd

F32 = mybir.dt.float32
BF16 = mybir.dt.bfloat16
AF = mybir.ActivationFunctionType
ALU = mybir.AluOpType
AX = mybir.AxisListType

ENGS = ("pe", "act", "dve", "pool", "sp")
D = 2048
NIN = 6656
EPS = 1e-6
NCORES = 8


class Op:
    __slots__ = ("eng", "fn", "dma", "deps_hard", "deps_war", "needs_inc", "inc_idx",
                 "dma_cnt", "waits", "idx")

    def __init__(self, eng, fn, dma):
        self.eng = eng
        self.fn = fn
        self.dma = dma
        self.deps_hard = set()
        self.deps_war = set()
        self.needs_inc = False
        self.inc_idx = 0
        self.dma_cnt = 0
        self.waits = []


class Prog:
    def __init__(self, nc, es):
        self.nc = nc
        self.es = es
        self.ops = []
        self.last_w = {}
        self.readers = {}
        self.dma_count = {}
        self.nt = 0

    def add(self, eng, fn, r=(), w=(), dma=None):
        op = Op(eng, fn, dma)
        op.idx = len(self.ops)
        for k in r:
            lw = self.last_w.get(k)
            if lw is not None:
                op.deps_hard.add(lw)
        for k in w:
            lw = self.last_w.get(k)
            if lw is not None:
                op.deps_hard.add(lw)
            rd = self.readers.get(k)
            if rd:
                for o in rd.values():
                    op.deps_war.add(o)
        ent = ("dma", dma) if dma else ("eng", eng)
        for k in r:
            if isinstance(k, tuple) and k and k[0] == "ps":
                rd = self.readers.get(k)
                if rd:
                    for ent2, o in rd.items():
                        if ent2 != ent:
                            op.deps_hard.add(o)
        if fn is not None:
            for k in r:
                self.readers.setdefault(k, {})[ent] = op.idx
            for k in w:
                self.last_w[k] = op.idx
                self.readers[k] = {}
        if dma:
            self.dma_count[dma] = self.dma_count.get(dma, 0) + 1
            op.dma_cnt = self.dma_count[dma]
        self.ops.append(op)
        return op

    def barrier(self):
        allk = list(dict.fromkeys(list(self.last_w.keys()) + list(self.readers.keys())))
        for e in ENGS:
            self.add(e, None, r=allk, w=allk)

    def resolve(self):
        ops = self.ops
        for op in ops:
            need = set()
            for d in op.deps_hard:
                po = ops[d]
                if po.dma:
                    need.add(d)
                elif po.eng == op.eng and not op.dma and op.eng == "pe":
                    continue
                else:
                    need.add(d)
            for d in op.deps_war:
                po = ops[d]
                if po.dma:
                    need.add(d)
                elif po.eng == op.eng and op.eng == "pe" and not op.dma:
                    continue
                else:
                    need.add(d)
            op.waits = need
            for d in need:
                if not ops[d].dma:
                    ops[d].needs_inc = True
        cnt = {e: 0 for e in ENGS}
        for op in ops:
            if op.needs_inc:
                assert op.fn is not None
                cnt[op.eng] += 1
                op.inc_idx = cnt[op.eng]
        self.eng_total = cnt
        waited = {e: {} for e in ENGS}
        for op in ops:
            wl = {}
            for d in op.waits:
                po = ops[d]
                if po.dma:
                    key = ("dma", po.dma)
                    val = 16 * po.dma_cnt
                else:
                    key = ("eng", po.eng)
                    val = po.inc_idx
                if val > wl.get(key, 0):
                    wl[key] = val
            out = []
            for key, val in wl.items():
                if waited[op.eng].get(key, 0) >= val:
                    continue
                waited[op.eng][key] = val
                out.append((key, val))
            op.waits = out

    def emit(self):
        nc = self.nc
        self.resolve()
        sems = {}
        for e in ENGS:
            sems[("eng", e)] = self.es.enter_context(nc.semaphore(f"s_{e}"))
        for name in self.dma_count:
            sems[("dma", name)] = self.es.enter_context(nc.semaphore(f"d_{name}"))
        per = {e: [op for op in self.ops if op.eng == e] for e in ENGS}

        def run(engname, eng):
            for op in per[engname]:
                for key, val in op.waits:
                    eng.wait_ge(sems[key], val)
                if op.fn is None:
                    continue
                ins = op.fn(eng)
                if op.dma:
                    ins.then_inc(sems[("dma", op.dma)], 16)
                elif op.needs_inc:
                    ins.then_inc(sems[("eng", engname)], 1)

        with nc.Block() as block:
            @block.tensor
            def _(e):
                run("pe", e)

            @block.scalar
            def _(e):
                run("act", e)

            @block.vector
            def _(e):
                run("dve", e)

            @block.gpsimd
            def _(e):
                run("pool", e)

            @block.sync
            def _(e):
                run("sp", e)


class Builder:
    def __init__(self, layers, fused, debug=False, stop=None):
        self.stop = stop
        self.layers = layers
        self.fused = fused
        self.debug = debug
        self.nc = bass.Bass("TRN2", target_bir_lowering=False)
        self.din = {}
        self.dout = {}

    def inp(self, name, shape, dt=F32):
        if name in self.din:
            return self.din[name]
        t = self.nc.dram_tensor(name, list(shape), dt, kind="ExternalInput").ap()
        self.din[name] = t
        return t

    def outp(self, name, shape, dt=F32):
        t = self.nc.dram_tensor(name, list(shape), dt, kind="ExternalOutput").ap()
        self.dout[name] = t
        return t

    def scratch(self, name, shape, dt):
        if self.debug:
            return self.outp(name, shape, dt)
        return self.nc.dram_tensor(name, list(shape), dt, kind="Internal").ap()

    def scope(self):
        b = self

        class _S:
            def __enter__(self_):
                self_.mark = b.off
                return self_

            def __exit__(self_, *a):
                b.off = self_.mark
                return False
        return _S()

    def sb(self, es, shape, dt, name=None):
        esz = 4 if dt == F32 else 2
        n = 1
        for d_ in shape[1:]:
            n *= d_
        nbytes = (n * esz + 255) // 256 * 256
        off = self.off
        self.off += nbytes
        self.peak = max(self.peak, self.off)
        assert self.off <= self.BIGBYTES, f"SBUF overflow {self.off}"
        v = self.big[:, off // 2:(off + n * esz) // 2]
        if dt == F32:
            v = v.bitcast(F32)
        if len(shape) == 3:
            v = v.rearrange("p (a b) -> p a b", a=shape[1])
        elif len(shape) == 4:
            v = v.rearrange("p (a b c) -> p a b c", a=shape[1], b=shape[2])
        return v

    def dma(self, q, out, in_, r, w, sem):
        self.P.add(q, lambda e, o=out, i=in_: e.dma_start(out=o, in_=i), r=r, w=w, dma=sem)

    def mm(self, ps, lhsT, rhs, start, stop, r, w):
        self.P.add("pe", lambda e, a=ps, b=lhsT, c=rhs, s0=start, s1=stop:
                   e.matmul(a, lhsT=b, rhs=c, start=s0, stop=s1), r=r, w=w)

    def tr(self, ps, in_, ident, r, w):
        self.P.add("pe", lambda e, a=ps, b=in_, c=ident: e.transpose(a, b, c), r=r, w=w)

    def act(self, out, in_, func, r, w, **kw):
        self.P.add("act", lambda e, o=out, i=in_, f=func, k=kw: e.activation(out=o, in_=i, func=f, **k), r=r, w=w)

    def tt(self, eng, out, in0, in1, op, r, w):
        self.P.add(eng, lambda e, o=out, a=in0, b=in1, p=op: e.tensor_tensor(out=o, in0=a, in1=b, op=p), r=r, w=w)

    def ts(self, eng, out, in0, s1, s2, op0, op1, r, w):
        self.P.add(eng, lambda e, o=out, a=in0, x=s1, y=s2, p=op0, q=op1:
                   e.tensor_scalar(out=o, in0=a, scalar1=x, scalar2=y, op0=p, op1=q), r=r, w=w)

    def tsm(self, eng, out, in0, s1, r, w):
        self.P.add(eng, lambda e, o=out, a=in0, x=s1: e.tensor_scalar_mul(out=o, in0=a, scalar1=x), r=r, w=w)

    def stt(self, eng, out, in0, scalar, in1, op0, op1, r, w, accum_out=None):
        if accum_out is None:
            self.P.add(eng, lambda e, o=out, a=in0, s=scalar, b=in1, p=op0, q=op1:
                       e.scalar_tensor_tensor(out=o, in0=a, scalar=s, in1=b, op0=p, op1=q), r=r, w=w)
        else:
            self.P.add(eng, lambda e, o=out, a=in0, s=scalar, b=in1, p=op0, q=op1, ac=accum_out:
                       e.scalar_tensor_tensor(out=o, in0=a, scalar=s, in1=b, op0=p, op1=q, accum_out=ac), r=r, w=w)

    def cp(self, eng, out, in_, r, w):
        if eng == "act":
            self.P.add("act", lambda e, o=out, i=in_: e.copy(out=o, in_=i), r=r, w=w)
        else:
            self.P.add(eng, lambda e, o=out, i=in_: e.tensor_copy(out=o, in_=i), r=r, w=w)

    def memset(self, eng, ap, val, w):
        self.P.add(eng, lambda e, a=ap, v=val: e.memset(a, v), w=w)

    def recip(self, out, in_, r, w):
        self.P.add("dve", lambda e, o=out, i=in_: e.reciprocal(out=o, in_=i), r=r, w=w)

    def wload(self, view, c0, ncols=512):
        i = self.wcur
        self.wcur ^= 1
        self.dma("pool", self.wb[i][:, :, 0:ncols], view[:, :, c0:c0 + ncols], r=[], w=[("wb", i)], sem=f"wb{i}")
        return i

    def bank(self):
        i = self.gb[self.gbi % len(self.gb)]
        self.gbi += 1
        return i

    def mm16(self, bi, n, lhs_fn, rhs_fn, r):
        for kc in range(16):
            self.mm(self.B[bi][:, 0:n], lhs_fn(kc), rhs_fn(kc), kc == 0, kc == 15, r=r, w=[("ps", bi)])

    def hxkeys(self, tok0, n):
        return [("hx", a) for a in range(tok0 // 128, (tok0 + n + 127) // 128)]

    def build(self):
        nc = self.nc
        with ExitStack() as es:
            self.P = Prog(nc, es)
            P = self.P
            ident_d = self.inp("ident", [128, 128])
            rotm_d = self.inp("rotm", [128, 128])
            cosk_d = self.inp("cosk", [128, 2048])
            sink_d = self.inp("sink", [128, 2048])
            msk_d = self.inp("msk", [128, 4])
            cT_d = self.inp("cT", [128, 32])
            self.cosk_d, self.sink_d = cosk_d, sink_d

            self.BIGBYTES = 206 * 1024
            self.off = 128
            self.peak = 0
            self.big = es.enter_context(nc.sbuf_tensor("big", [128, self.BIGBYTES // 2], BF16))
            self.ident_f = self.sb(es, [128, 128], F32)
            if True:
                self.rotm_b = self.sb(es, [128, 128], BF16)
                self.ident_b = self.sb(es, [128, 128], BF16)
            else:
                self.ident_b = self.sb(es, [128, 128], BF16)
                self.rotm_b = self.sb(es, [128, 128], BF16)
            self.ones_b = self.sb(es, [128, 128], BF16)
            self.ones_f = self.sb(es, [128, 128], F32)
            self.msk = self.sb(es, [128, 4], F32)
            self.cT = self.sb(es, [128, 32], F32)
            self.hxA = self.sb(es, [128, 16, 2304], BF16)
            self.wb = [self.sb(es, [128, 16, 512], BF16) for _ in range(2)]
            self.wcur = 0
            self.B = [es.enter_context(nc.psum_tensor(f"bank{i}", [128, 512], F32)) for i in range(8)]
            self.gb = [0, 1, 2, 3]
            self.gbi = 0

            self.dma("sp", self.ident_f[:], ident_d, r=[], w=["ident_f"], sem="c0")
            self.dma("pool", self.ident_b[:], ident_d, r=[], w=["ident_b"], sem="c1")
            self.dma("pool", self.rotm_b[:], rotm_d, r=[], w=["rotm_b"], sem="c2")
            self.dma("sp", self.msk[:], msk_d, r=[], w=["msk"], sem="c3")
            self.dma("sp", self.cT[:], cT_d, r=[], w=["cT"], sem="c4")
            self.memset("dve", self.ones_b[:], 1.0, w=["ones_b"])
            self.memset("dve", self.ones_f[:], 1.0, w=["ones_f"])

            self.modsave = {l_: ([self.sb(es, [128, 32], F32) for _ in range(2)],
                                 [self.sb(es, [128, 16], F32) for _ in range(2)]) for l_ in self.layers}
            self.kT_d = self.scratch("kT_d", [8, 128, 2304], BF16)
            self.v_d = self.scratch("v_d", [8, 128, 18, 132], BF16)

            if not self.fused:
                l = self.layers[0]
                last = (l == 1)
                src = {"own": self.inp("x_own", [1024, D]), "oth": self.inp("x_oth", [1024, D]),
                       "ctx": self.inp("ctx_in", [256, D]), "blend": None}
                if last:
                    xdst = self.scratch("x2_own", [1024, D], F32)
                    cdst = None
                    ydst = self.outp("y", [1024, D])
                else:
                    xdst = self.outp("x1_own", [1024, D])
                    cdst = self.outp("ctx1", [256, D])
                    ydst = None
                self.layer(l, last, src, xdst, cdst, ydst)
            else:
                x_own = self.inp("x_own", [1024, D])
                x_oth = self.inp("x_oth", [1024, D])
                ctx_in = self.inp("ctx_in", [256, D])
                x1_own = nc.dram_tensor("x1_own", [1024, D], F32, kind="Internal").ap()
                x1_oth = nc.dram_tensor("x1_oth", [1024, D], F32, kind="Internal").ap()
                ctx1 = nc.dram_tensor("ctx1", [256, D], F32, kind="Internal").ap()
                x2_own = nc.dram_tensor("x2_own", [1024, D], F32, kind="Internal").ap()
                ydst = self.outp("y", [1024, D])
                self.layer(0, False, {"own": x_own, "oth": x_oth, "ctx": ctx_in, "blend": None}, x1_own, ctx1, None,
                           mode="full", L="L0", xkey="x1own")
                self.layer(0, False, {"own": x_oth, "oth": x_own, "ctx": ctx_in, "blend": None}, x1_oth, None, None,
                           mode="B", L="L0B", xkey="x1oth")
                self.layer(1, True, {"own": x1_own, "oth": x1_oth, "ctx": ctx1, "blend": None, "own_key": "x1own",
                                     "oth_key": "x1oth"}, x2_own, None, ydst, mode="full", L="L1", xkey="x2own")
            P.add("sp", None, r=list(self.final_keys))
            P.emit()
        return nc

    def layer(self, l, last, src, xdst, cdst, ydst, mode="full", L=None, xkey="xdst"):
        nc, P, B = self.nc, self.P, self.B
        hxA = self.hxA
        lam_init = 0.8 - 0.6 * math.exp(-0.3 * l)
        has_ctx = (not last) and mode == "full"
        passB = mode == "B"
        mL, mR = (1, 0) if passB else (0, 1)
        qoff = 1024 if passB else 0
        with self.scope() as les:
            wmod = self.inp(f"wmod{l}", [D, 3 * D]).rearrange("(kc p) n -> p kc n", p=128)
            bmod = self.inp(f"bmod{l}", [1, 3 * D])
            ng_d = self.inp(f"ng{l}", [128, 16])
            win = self.inp(f"win{l}", [D, NIN]).rearrange("(kc p) n -> p kc n", p=128)
            lam_d = self.inp(f"lam{l}", [1, 256])
            subg_d = self.inp(f"subg{l}", [128, 1])
            wpool_d = self.inp(f"wpool{l}", [4, 128, 128])
            pscale_d = self.inp(f"pscale{l}", [128, 4])
            wdw_d = self.inp(f"wdw{l}", [128, 4, 31])
            cvec_d = self.inp(f"cvec{l}", [128, 12])
            wpw2_d = self.inp(f"wpw2{l}", [512, 512]).rearrange("(c p) n -> p c n", p=128)
            wout = self.inp(f"wout{l}", [D, D]).rearrange("(kc p) n -> p kc n", p=128)
            fg_d = self.inp("fg", [1, D]) if last else None

            sm = self.sb(les, [128, 64], F32)
            ng = self.sb(les, [128, 16], F32)
            lamb = self.sb(les, [128, 256], F32)
            subg = self.sb(les, [128, 1], F32)
            pscale = self.sb(les, [128, 4], F32)
            wdw = self.sb(les, [128, 4, 31], F32)
            cvec = self.sb(les, [128, 12], F32)
            wpool_b = self.sb(les, [128, 4, 128], BF16)
            wpw2_b = self.sb(les, [128, 4, 512], BF16)
            modcol, gs = self.modsave[l]
            ML = f"M{l}"
            junk128 = self.sb(les, [128, 128], F32)
            L = L or f"L{l}"
            self.dma("sp", ng[:], ng_d, r=[], w=[L + "ng"], sem="p0")
            self.dma("sp", lamb[:], lam_d.partition_broadcast(128), r=[], w=[L + "lamb"], sem="p1")
            self.dma("sp", subg[:], subg_d, r=[], w=[L + "subg"], sem="p2")
            self.dma("sp", pscale[:], pscale_d, r=[], w=[L + "pscale"], sem="p3")
            self.dma("sp", wdw[:], wdw_d, r=[], w=[L + "wdw"], sem="p4")
            self.dma("sp", cvec[:], cvec_d, r=[], w=[L + "cvec"], sem="p5")
            self.dma("pool", wpool_b[:], wpool_d.rearrange("g c d -> c g d"), r=[], w=[L + "wpool"], sem="p6")
            self.dma("pool", wpw2_b[:], wpw2_d, r=[], w=[L + "wpw2"], sem="p7")
            self.memset("dve", sm[:], 0.0, w=[L + "sm"])
            self.stt("dve", junk128[:, 0:64], lamb[:, 0:64], 1.0, lamb[:, 64:128], ALU.mult, ALU.mult,
                     r=[L + "lamb", L + "sm"], w=[L + "junk128", L + "sm0"], accum_out=sm[:, 0:1])
            self.stt("dve", junk128[:, 64:128], lamb[:, 128:192], 1.0, lamb[:, 192:256], ALU.mult, ALU.mult,
                     r=[L + "lamb", L + "sm"], w=[L + "junk128b", L + "sm1"], accum_out=sm[:, 1:2])
            self.act(sm[:, 2:4], sm[:, 0:2], AF.Exp, r=[L + "sm0", L + "sm1"], w=[L + "sm23"])
            self.tt("dve", sm[:, 4:5], sm[:, 2:3], sm[:, 3:4], ALU.subtract, r=[L + "sm23"], w=[L + "sm4"])
            self.ts("dve", sm[:, 5:6], sm[:, 4:5], lam_init, -1.0, ALU.add, ALU.mult, r=[L + "sm4"], w=[L + "neglam"])
            self.tsm("dve", sm[:, 6:7], subg[:], 1.0 - lam_init, r=[L + "subg", L + "sm"], w=[L + "subg2"])
            neglam = sm[:, 5:6]
            subg2 = sm[:, 6:7]

            pes_mod = self.scope()
            pes_mod.__enter__()
            pes = pes_mod
            if not passB:
                s_f = self.sb(pes, [128, 32], F32)
                srep = self.sb(pes, [128, 32, 128], BF16)
                bm = [self.sb(pes, [128, 512], F32) for _ in range(2)]
                rowb = [self.sb(pes, [128, 512], F32) for _ in range(2)]
                self.act(s_f[:], self.cT[:], AF.Silu, r=["cT"], w=[L + "s_f"])
                for j in range(32):
                    self.tsm("dve", srep[:, j, :], self.ones_b[:], s_f[:, j:j + 1], r=["ones_b", L + "s_f"],
                             w=[(L + "srep", j)])
                for r_ in range(2):
                    self.memset("dve", modcol[r_][:], 0.0, w=[(ML + "modcol", r_)])
                for g in range(8):
                    i = self.wload(wmod, g * 512)
                    self.dma("sp", bm[g % 2][:], bmod[0:1, g * 512:(g + 1) * 512].partition_broadcast(128),
                             r=[], w=[(L + "bm", g % 2)], sem=f"bm{g%2}")
                    for r_ in range(2):
                        bi = self.bank()
                        self.mm16(bi, 512, lambda kc: srep[:, r_ * 16 + kc, :], lambda kc: self.wb[i][:, kc, :],
                                  r=[("wb", i)] + [(L + "srep", r_ * 16 + kc) for kc in range(16)])
                        self.tt("dve", rowb[r_][:], B[bi][:], bm[g % 2][:], ALU.add,
                                r=[("ps", bi), (L + "bm", g % 2)], w=[(L + "rowb", r_)])
                        for j in range(4):
                            c = g * 4 + j
                            self.stt("dve", junk128[:], rowb[r_][:, j * 128:(j + 1) * 128], 1.0, self.ident_f[:],
                                     ALU.mult, ALU.mult, r=[(L + "rowb", r_), "ident_f", (ML + "modcol", r_)],
                                     w=[L + "junk128", (ML + "modcolc", r_, c)], accum_out=modcol[r_][:, c:c + 1])
                for r_ in range(2):
                    rk = [(ML + "modcolc", r_, c) for c in range(16, 32)]
                    self.ts("dve", gs[r_][:], modcol[r_][:, 16:32], 1.0, 1.0, ALU.add, ALU.mult,
                            r=rk, w=[(ML + "gs0", r_)])
                    self.tt("dve", gs[r_][:], gs[r_][:], ng[:], ALU.mult, r=[(ML + "gs0", r_), L + "ng"],
                            w=[(ML + "gs", r_)])
            shiftk = lambda r_: [(ML + "modcolc", r_, c) for c in range(16)]
            self.final_keys = set()
            if self.debug and not passB:
                dm = self.outp(f"dbg_mod{l}", [128, 96], F32)
                allmk = [(ML + "modcolc", r_, c) for r_ in range(2) for c in range(32)] + [(ML + "gs", 0), (ML + "gs", 1)]
                self.dma("sp", dm[:, 0:32], modcol[0][:], r=allmk, w=["dm0"], sem="dbgm0")
                self.dma("sp", dm[:, 32:64], modcol[1][:], r=allmk, w=["dm1"], sem="dbgm1")
                self.dma("sp", dm[:, 64:80], gs[0][:], r=allmk, w=["dm2"], sem="dbgm2")
                self.dma("sp", dm[:, 80:96], gs[1][:], r=allmk, w=["dm3"], sem="dbgm3")
                self.final_keys |= {"dm0", "dm1", "dm2", "dm3"}
                P.barrier()
            if self.stop == "MOD1":
                return

            with self.scope() as pes:
                xt = [self.sb(pes, [128, D], F32) for _ in range(2)]
                xt2 = [self.sb(pes, [128, D], F32) for _ in range(2)] if src["blend"] is not None else None
                xn = [self.sb(pes, [128, D], BF16) for _ in range(2)]
                junkb = self.sb(pes, [128, D], BF16)
                st = self.sb(pes, [128, 4, 18], F32)
                self.memset("dve", st[:], 0.0, w=[L + "st"])
                hx_tiles = [2, 3, 4, 5, 6, 7, 8, 9, 10, 17] if passB else list(range(18))

                def hx_s1(a):
                        r_ = 1 if a < 2 else 0
                        x = xt[a % 2]
                        xk = (L + "xt", a % 2)
                        if a < 2:
                            sap = src["ctx"][a * 128:(a + 1) * 128, :]
                            self.dma("sp", x[:], sap, r=[("cdst", a)], w=[xk], sem=f"xt{a%2}")
                        elif a < 10:
                            t = a - 2
                            self.dma("sp", x[:], src["own"][t * 128:(t + 1) * 128, :], r=[(src.get("own_key", "none"), t)], w=[xk],
                                     sem=f"xt{a%2}")
                        else:
                            t = a - 10
                            if src["blend"] is None:
                                self.dma("sp", x[:], src["oth"][t * 128:(t + 1) * 128, :], r=[(src.get("oth_key", "none"), t)], w=[xk], sem=f"xt{a%2}")
                            else:
                                x2 = xt2[a % 2]
                                x2k = (L + "xt2", a % 2)
                                self.dma("sp", x[:], src["blend"][t * 128:(t + 1) * 128, :], r=["recv"], w=[xk],
                                         sem=f"xt{a%2}")
                                self.dma("sp", x2[:], src["blend"][1024 + t * 128:1024 + (t + 1) * 128, :], r=["recv"],
                                         w=[x2k], sem=f"xtb{a%2}")
                                self.tsm("dve", x[:], x[:], self.msk[:, 2:3], r=[xk, "msk"], w=[xk])
                                self.stt("dve", x[:], x2[:], self.msk[:, 3:4], x[:], ALU.mult, ALU.add,
                                         r=[xk, x2k, "msk"], w=[xk])
                        self.act(junkb[:], x[:], AF.Square, r=[xk, L + "st"], w=[L + "junkb", (L + "ss", a)],
                                 accum_out=st[:, 0, a:a + 1])
                        self.ts("dve", st[:, 1, a:a + 1], st[:, 0, a:a + 1], 1.0 / D, EPS, ALU.mult, ALU.add,
                                r=[(L + "ss", a)], w=[(L + "ms", a)])

                def hx_s2(a):
                        xk = (L + "xt", a % 2)
                        x = xt[a % 2]
                        self.act(st[:, 2, a:a + 1], st[:, 1, a:a + 1], AF.Ln, r=[(L + "ms", a)], w=[(L + "sd", a)])
                        self.act(st[:, 3, a:a + 1], st[:, 2, a:a + 1], AF.Exp, r=[(L + "sd", a)], w=[(L + "rstd", a)],
                                 scale=-0.5)
                        xnk = (L + "xn", a % 2)
                        self.act(xn[a % 2][:], x[:], AF.Identity, r=[xk, (L + "rstd", a)], w=[xnk],
                                 scale=st[:, 3, a:a + 1])

                def hx_s3(a):
                        r_ = 1 if a < 2 else 0
                        xnk = (L + "xn", a % 2)
                        b0 = (a % 2) * 2
                        for kc in range(16):
                            bi = b0 + kc // 8
                            pv = B[bi][:].bitcast(BF16)
                            self.tr(pv[:, (kc % 8) * 128:(kc % 8 + 1) * 128], xn[a % 2][:, kc * 128:(kc + 1) * 128],
                                    self.ident_b[:], r=[xnk, "ident_b"], w=[("ps", bi)])
                        for kc in range(16):
                            bi = b0 + kc // 8
                            pv = B[bi][:].bitcast(BF16)
                            o = hxA[:, kc, a * 128:(a + 1) * 128]
                            i_ = pv[:, (kc % 8) * 128:(kc % 8 + 1) * 128]
                            rr = [("ps", bi), (ML + "gs", r_)] + shiftk(r_)
                            if True:
                                self.ts("dve", o, i_, gs[r_][:, kc:kc + 1], modcol[r_][:, kc:kc + 1], ALU.mult, ALU.add,
                                        r=rr, w=[("hx", a)])
                            else:
                                self.act(o, i_, AF.Identity, r=rr, w=[("hx", a)], scale=gs[r_][:, kc:kc + 1],
                                         bias=modcol[r_][:, kc:kc + 1])

                for i_t, a in enumerate(hx_tiles):
                    hx_s1(a)
                    if i_t >= 1:
                        hx_s2(hx_tiles[i_t - 1])
                        hx_s3(hx_tiles[i_t - 1])
                hx_s2(hx_tiles[-1])
                hx_s3(hx_tiles[-1])
                P.barrier()
            pes_mod.__exit__(None, None, None)
            if self.debug and not passB:
                dh = self.outp(f"dbg_hx{l}", [128, 16, 2304], BF16)
                self.dma("sp", dh, hxA[:], r=[("hx", a) for a in range(18)], w=[f"dbg_hx{l}"], sem="dbg")
                P.barrier()
                self.final_keys.add(f"dbg_hx{l}")
            if self.stop == "HX":
                return

            with self.scope() as pes:
              if not passB:
                cosk = self.sb(pes, [128, 2048], F32)
                sink = self.sb(pes, [128, 2048], F32)
                k_sb = [self.sb(pes, [128, 512], BF16) for _ in range(2)]
                t1 = [self.sb(pes, [128, 512], F32) for _ in range(2)]
                t2 = [self.sb(pes, [128, 512], F32) for _ in range(2)]
                kto = [self.sb(pes, [128, 512], BF16) for _ in range(2)]
                vst = [self.sb(pes, [128, 4, 132], BF16) for _ in range(2)]
                self.dma("sp", cosk[:], self.cosk_d, r=[], w=[L + "cosk"], sem="cosk")
                self.dma("sp", sink[:], self.sink_d, r=[], w=[L + "sink"], sem="sink")
                for j in range(2):
                    self.memset("dve", vst[j][:], 1.0, w=[(L + "vst", j)])
                cnt = 0
                for gk in (2, 3):
                    i = self.wload(win, gk * 512)
                    for hh in range(4):
                        h = (gk - 2) * 4 + hh
                        for (tok0, n, rope) in [(0, 256, False), (256, 512, True), (768, 512, True),
                                                (1280, 512, True), (1792, 512, True)]:
                            j = cnt % 2
                            cnt += 1
                            bi = self.bank()
                            self.mm16(bi, n, lambda kc: self.wb[i][:, kc, hh * 128:(hh + 1) * 128],
                                      lambda kc: hxA[:, kc, tok0:tok0 + n], r=[("wb", i)] + self.hxkeys(tok0, n))
                            if not rope or os.environ.get("KV_NOROPE"):
                                self.cp("act", kto[j][:, :n], B[bi][:, :n], r=[("ps", bi)], w=[(L + "kto", j)])
                            else:
                                RV = os.environ.get("ROPE_VAR", "")
                                self.cp("act", k_sb[j][:, :n], B[bi][:, :n], r=[("ps", bi)], w=[(L + "k_sb", j)])
                                br = self.bank()
                                if RV != "dve_only":
                                    if os.environ.get("ROPE_IDENT"):
                                        self.mm(B[br][:, :n], self.ident_b[:], k_sb[j][:, :n], True, True,
                                                r=["ident_b", (L + "k_sb", j)], w=[("ps", br)])
                                    else:
                                        self.mm(B[br][:, :n], self.rotm_b[:], k_sb[j][:, :n], True, True,
                                                r=["rotm_b", (L + "k_sb", j)], w=[("ps", br)])
                                else:
                                    br = bi
                                if RV == "mm_only":
                                    self.cp("act", kto[j][:, :n], B[br][:, :n], r=[("ps", br)], w=[(L + "kto", j)])
                                    continue
                                p0 = tok0 - 256
                                self.tt("dve", t1[j][:, :n], B[bi][:, :n], cosk[:, p0:p0 + n], ALU.mult,
                                        r=[("ps", bi), L + "cosk", (L + "k_sb", j)], w=[(L + "t1", j)])
                                self.tt("dve", t2[j][:, :n], B[br][:, :n], sink[:, p0:p0 + n], ALU.mult,
                                        r=[("ps", br), L + "sink"], w=[(L + "t2", j)])
                                self.tt("dve", kto[j][:, :n], t1[j][:, :n], t2[j][:, :n], ALU.add,
                                        r=[(L + "t1", j), (L + "t2", j)], w=[(L + "kto", j)])
                            if os.environ.get("KV_NOSTORE") and not (h == 7 and tok0 == 1792):
                                continue
                            self.dma("sp", self.kT_d[h, :, tok0:tok0 + n], kto[j][:, :n], r=[(L + "kto", j)],
                                     w=[("kT_d", h)], sem=f"kto{j}")
                if self.stop == "KVK":
                    self.final_keys |= {("kT_d", h) for h in range(8)}
                    P.barrier()
                    return
                cnt = 0
                for gv in (4, 5):
                    i = self.wload(win, gv * 512)
                    for a in range(18):
                        j = cnt % 2
                        cnt += 1
                        bi = self.bank()
                        self.mm16(bi, 512, lambda kc: hxA[:, kc, a * 128:(a + 1) * 128],
                                  lambda kc: self.wb[i][:, kc, :], r=[("wb", i), ("hx", a)])
                        self.cp("act", vst[j][:, :, 0:128], B[bi][:].rearrange("p (h e) -> p h e", h=4),
                                r=[("ps", bi)], w=[(L + "vst", j)])
                        h0 = (gv - 4) * 4
                        self.dma("sp", self.v_d[h0:h0 + 4, :, a, :].rearrange("h p e -> p h e"), vst[j][:],
                                 r=[(L + "vst", j)], w=[("v_d", h0 + q) for q in range(4)], sem=f"vst{j}")
                P.barrier()

            if self.debug:
                self.final_keys |= {("kT_d", h) for h in range(8)} | {("v_d", h) for h in range(8)}
            if self.stop == "KV":
                return
            nown = 1280 if has_ctx else 1024
            yTc = self.sb(les, [128, 16, 256], BF16) if has_ctx else None

            def ytv(kc, c0, n):
                if c0 < 1024:
                    return hxA[:, kc, 1280 + c0:1280 + c0 + n]
                return yTc[:, kc, c0 - 1024:c0 - 1024 + n]

            def ytk(kc, c0, n):
                return [("yT", kc, t) for t in range(c0 // 128, (c0 + n) // 128)]

            oblocks = [(256, 0, 512), (768, 512, 512)] + ([(0, 1024, 256)] if has_ctx else [])

            with self.scope() as cps:
                u_ext = self.sb(cps, [128, 4, 1056], BF16)
                uc_ext = self.sb(cps, [128, 4, 288], BF16) if has_ctx else None
                pps = self.scope()
                pps.__enter__()
                up_ext = self.sb(cps, [128, 4, 1056], F32)
                upc_ext = self.sb(cps, [128, 4, 288], F32) if has_ctx else None
                if has_ctx:
                    self.memset("dve", uc_ext[:], 0.0, w=[L + "uc_ext"])
                    self.memset("dve", upc_ext[:], 0.0, w=[L + "upc_ext"])
                with self.scope() as pes:
                    sig = [self.sb(pes, [128, 512], F32) for _ in range(2)]
                    ablocks = [(256, 512, False, 16, None), (768, 512, False, 528, None)]
                    if has_ctx:
                        ablocks.append((0, 256, True, 16, None))
                    ablocks += [(2288, 16, False, 0, mL), (1280, 16, False, 1040, mR)]
                    iA = self.wload(win, 5120)
                    iB = self.wload(win, 5632)
                    cnt = 0
                    for j in range(4):
                        for (tok0, n, isc, off, mc) in ablocks:
                            q = cnt % 2
                            cnt += 1
                            ba = self.bank()
                            self.mm16(ba, n, lambda kc: self.wb[iA][:, kc, j * 128:(j + 1) * 128],
                                      lambda kc: hxA[:, kc, tok0:tok0 + n], r=[("wb", iA)] + self.hxkeys(tok0, n))
                            bb = self.bank()
                            self.mm16(bb, n, lambda kc: self.wb[iB][:, kc, j * 128:(j + 1) * 128],
                                      lambda kc: hxA[:, kc, tok0:tok0 + n], r=[("wb", iB)] + self.hxkeys(tok0, n))
                            self.act(sig[q][:, :n], B[bb][:, :n], AF.Sigmoid, r=[("ps", bb)], w=[(L + "sig", q)])
                            dst = (uc_ext if isc else u_ext)[:, j, off:off + n]
                            dk = L + ("uc_ext" if isc else "u_ext")
                            if mc is None:
                                self.tt("dve", dst, B[ba][:, :n], sig[q][:, :n], ALU.mult,
                                        r=[("ps", ba), (L + "sig", q), dk], w=[dk])
                            else:
                                self.stt("dve", dst, B[ba][:, :n], self.msk[:, mc:mc + 1], sig[q][:, :n],
                                         ALU.mult, ALU.mult, r=[("ps", ba), (L + "sig", q), "msk", dk], w=[dk])
                    i8 = self.wload(win, 4096)
                    for j in range(4):
                        for (tok0, n, isc, off, mc) in ablocks:
                            ba = self.bank()
                            self.mm16(ba, n, lambda kc: self.wb[i8][:, kc, j * 128:(j + 1) * 128],
                                      lambda kc: hxA[:, kc, tok0:tok0 + n], r=[("wb", i8)] + self.hxkeys(tok0, n))
                            dst = (upc_ext if isc else up_ext)[:, j, off:off + n]
                            dk = L + ("upc_ext" if isc else "up_ext")
                            if mc is None:
                                self.cp("act", dst, B[ba][:, :n], r=[("ps", ba), dk], w=[dk])
                            else:
                                self.act(dst, B[ba][:, :n], AF.Identity, r=[("ps", ba), "msk", dk], w=[dk],
                                         scale=self.msk[:, mc:mc + 1])
                    P.barrier()

                with self.scope() as pes:
                    sa = [self.sb(pes, [128, 1056], F32) for _ in range(2)]
                    va = [self.sb(pes, [128, 1056], F32) for _ in range(3)]
                    dT = self.sb(pes, [128, 4, nown], BF16)
                    for q_ in range(2):
                        self.memset("dve", sa[q_][:], 0.0, w=[L + "sa" + str(q_)])
                        self.memset("dve", va[q_][:], 0.0, w=[L + "va" + str(q_)])
                    sgp = self.sb(pes, [128, 4, nown], BF16)
                    ig = self.wload(win, 4608)
                    for j in range(4):
                        for (tok0, c0, n) in oblocks:
                            ba = self.bank()
                            self.mm16(ba, n, lambda kc: self.wb[ig][:, kc, j * 128:(j + 1) * 128],
                                      lambda kc: hxA[:, kc, tok0:tok0 + n],
                                      r=[("wb", ig)] + self.hxkeys(tok0, n))
                            self.act(sgp[:, j, c0:c0 + n], B[ba][:, :n], AF.Silu, r=[("ps", ba), L + "sgp"],
                                     w=[L + "sgp"])
                    segs = [(up_ext, L + "up_ext", 1056, 1024, 0, True)]
                    if has_ctx:
                        segs.append((upc_ext, L + "upc_ext", 288, 256, 1024, False))
                    for (U, uk, E, N, c0, is_lat) in segs:
                        V = va[2]
                        self.memset("dve", V[:, :E], 1.0 if is_lat else 0.0, w=[L + "V"])
                        if is_lat:
                            self.tsm("dve", V[:, 0:16], V[:, 0:16], self.msk[:, mL:mL + 1], r=[L + "V", "msk"], w=[L + "V"])
                            self.tsm("dve", V[:, 1040:1056], V[:, 1040:1056], self.msk[:, mR:mR + 1], r=[L + "V", "msk"],
                                     w=[L + "V"])
                        else:
                            self.memset("dve", V[:, 16:16 + N], 1.0, w=[L + "V"])
                        for g in range(4):
                            def steps(src_ap, bufs, keyp, srck):
                                cur = src_ap
                                ck = srck
                                for i in range(g + 1):
                                    nb = bufs[i % 2]
                                    nk = keyp + str(i % 2)
                                    if i == 0:
                                        self.tt("dve", nb[:, 1:E], cur[:, 0:E - 1], cur[:, 1:E], ALU.add,
                                                r=[ck, nk], w=[nk])
                                    else:
                                        sh = 1 << (i - 1)
                                        self.tt("dve", nb[:, sh:E - sh], cur[:, 0:E - 2 * sh], cur[:, 2 * sh:E],
                                                ALU.add, r=[ck, nk], w=[nk])
                                    cur = nb
                                    ck = nk
                                return cur, ck
                            s_fin, sk_ = steps(U[:, g, :], sa, L + "sa", uk)
                            v_fin, vk_ = steps(V, va, L + "va", L + "V")
                            self.recip(v_fin[:, 16:16 + N], v_fin[:, 16:16 + N], r=[vk_], w=[vk_])
                            self.tt("dve", s_fin[:, 16:16 + N], s_fin[:, 16:16 + N], v_fin[:, 16:16 + N], ALU.mult,
                                    r=[sk_, vk_], w=[sk_])
                            self.tt("dve", dT[:, g, c0:c0 + N], s_fin[:, 16:16 + N], U[:, g, 16:16 + N],
                                    ALU.subtract, r=[sk_, uk, L + "dT"], w=[L + "dT"])
                    for g in range(4):
                        for (tok0, c0, n) in oblocks:
                            ba = self.bank()
                            self.mm(B[ba][:, :n], wpool_b[:, g, :], dT[:, g, c0:c0 + n], True, True,
                                    r=[L + "wpool", L + "dT"], w=[("ps", ba)])
                            self.stt("dve", ytv(8 + g, c0, n), B[ba][:, :n], pscale[:, g:g + 1], sgp[:, g, c0:c0 + n],
                                     ALU.mult, ALU.mult, r=[("ps", ba), L + "pscale", L + "sgp"],
                                     w=ytk(8 + g, c0, n))
                    P.barrier()
                pps.__exit__(None, None, None)

                with self.scope() as pes:
                    diag = self.sb(pes, [128, 4, 31, 128], BF16)
                    ybuf = self.sb(pes, [128, 4, 512], F32)
                    ysq = self.sb(pes, [128, 4, 512], F32)
                    mst = self.sb(pes, [128, 4, 512], F32)
                    sT = self.sb(pes, [128, 4, 512], BF16)
                    sgc = self.sb(pes, [128, 4, nown], BF16)
                    ig = self.wload(win, 6144)
                    for j in range(4):
                        for (tok0, c0, n) in oblocks:
                            ba = self.bank()
                            self.mm16(ba, n, lambda kc: self.wb[ig][:, kc, j * 128:(j + 1) * 128],
                                      lambda kc: hxA[:, kc, tok0:tok0 + n],
                                      r=[("wb", ig)] + self.hxkeys(tok0, n))
                            self.act(sgc[:, j, c0:c0 + n], B[ba][:, :n], AF.Silu, r=[("ps", ba), L + "sgc"],
                                     w=[L + "sgc"])
                    for c in range(4):
                        for k in range(31):
                            self.tsm("dve", diag[:, c, k, :], self.ident_b[:], wdw[:, c, k:k + 1],
                                     r=["ident_b", L + "wdw"], w=[(L + "diag", c)])
                    cb = [4, 5, 6, 7]
                    for (tok0, c0, n) in oblocks:
                        isc = c0 >= 1024
                        ue = uc_ext if isc else u_ext
                        uk = L + ("uc_ext" if isc else "u_ext")
                        e0 = (c0 - 1024) if isc else c0
                        for c in range(4):
                            bi = cb[c]
                            for k in range(31):
                                self.mm(B[bi][:, :n], diag[:, c, k, :], ue[:, c, e0 + k + 1:e0 + k + 1 + n],
                                        k == 0, k == 30, r=[(L + "diag", c), uk], w=[("ps", bi)])
                            self.act(ybuf[:, c, :n], B[bi][:, :n], AF.Identity, r=[("ps", bi), L + "cvec"],
                                     w=[(L + "ybuf", c)], bias=cvec[:, c:c + 1])
                            self.act(ysq[:, c, :n], B[bi][:, :n], AF.Square, r=[("ps", bi), L + "cvec"],
                                     w=[(L + "ysq", c)], bias=cvec[:, c:c + 1])
                        b1, b2 = 0, 1
                        for c in range(4):
                            self.mm(B[b1][:, :n], self.ones_f[:], ybuf[:, c, :n], c == 0, c == 3,
                                    r=["ones_f", (L + "ybuf", c)], w=[("ps", b1)])
                        for c in range(4):
                            self.mm(B[b2][:, :n], self.ones_f[:], ysq[:, c, :n], c == 0, c == 3,
                                    r=["ones_f", (L + "ysq", c)], w=[("ps", b2)])
                        self.ts("dve", mst[:, 0, :n], B[b1][:, :n], 1.0 / 512, 0.0, ALU.mult, ALU.add,
                                r=[("ps", b1)], w=[L + "m0"])
                        self.tt("dve", mst[:, 1, :n], mst[:, 0, :n], mst[:, 0, :n], ALU.mult, r=[L + "m0"], w=[L + "m1"])
                        self.stt("dve", mst[:, 2, :n], B[b2][:, :n], 1.0 / 512, mst[:, 1, :n], ALU.mult, ALU.subtract,
                                 r=[("ps", b2), L + "m1"], w=[L + "m2"])
                        self.ts("dve", mst[:, 2, :n], mst[:, 2, :n], EPS, 0.0, ALU.add, ALU.add,
                                r=[L + "m2"], w=[L + "m2"])
                        self.act(mst[:, 2, :n], mst[:, 2, :n], AF.Sqrt, r=[L + "m2"], w=[L + "m2"])
                        self.recip(mst[:, 3, :n], mst[:, 2, :n], r=[L + "m2"], w=[L + "m3"])
                        for c in range(4):
                            self.tt("dve", ybuf[:, c, :n], ybuf[:, c, :n], mst[:, 0, :n], ALU.subtract,
                                    r=[(L + "ybuf", c), L + "m0"], w=[(L + "ybuf", c)])
                            self.tt("dve", ybuf[:, c, :n], ybuf[:, c, :n], mst[:, 3, :n], ALU.mult,
                                    r=[(L + "ybuf", c), L + "m3"], w=[(L + "ybuf", c)])
                            self.act(sT[:, c, :n], ybuf[:, c, :n], AF.Silu, r=[(L + "ybuf", c), L + "cvec"],
                                     w=[(L + "sT", c)], scale=cvec[:, 4 + c:5 + c], bias=cvec[:, 8 + c:9 + c])
                        for j in range(4):
                            ba = 2 + (j % 2)
                            for c in range(4):
                                self.mm(B[ba][:, :n], wpw2_b[:, c, j * 128:(j + 1) * 128], sT[:, c, :n], c == 0, c == 3,
                                        r=[L + "wpw2", (L + "sT", c)], w=[("ps", ba)])
                            self.tt("dve", ytv(12 + j, c0, n), B[ba][:, :n], sgc[:, j, c0:c0 + n], ALU.mult,
                                    r=[("ps", ba), L + "sgc"], w=ytk(12 + j, c0, n))
                    P.barrier()

            with self.scope() as pes:
                cosq = self.sb(pes, [128, 1024], F32)
                sinq = self.sb(pes, [128, 1024], F32)
                qm = [[self.sb(pes, [128, nown], BF16) for _ in range(2)] for _ in range(2)]
                sgT = [self.sb(pes, [128, nown], BF16) for _ in range(2)]
                kTs = [self.sb(pes, [128, 2304], BF16) for _ in range(2)]
                Vhs = [self.sb(pes, [128, 18, 132], BF16) for _ in range(2)]
                k_sb = self.sb(pes, [128, 512], BF16)
                t1 = self.sb(pes, [128, 512], F32)
                t2 = self.sb(pes, [128, 512], F32)
                NPT = 6
                PT = [self.sb(pes, [128, 512], BF16) for _ in range(NPT)]
                Osb = self.sb(pes, [128, 2, 4, 132], F32)
                osm = self.sb(pes, [128, 8, 16], F32)
                o_sb = [self.sb(pes, [128, 128], F32) for _ in range(4)]
                on_b = [self.sb(pes, [128, 128], BF16) for _ in range(4)]
                junko4 = [self.sb(pes, [128, 128], F32) for _ in range(4)]
                self.dma("sp", cosq[:], self.cosk_d[:, qoff:qoff + 1024], r=[], w=[L + "cosq"], sem="cosq")
                self.dma("sp", sinq[:], self.sink_d[:, qoff:qoff + 1024], r=[], w=[L + "sinq"], sem="sinq")
                self.memset("dve", osm[:], 0.0, w=[L + "osm"])
                for hb_ in range(2):
                    for c_ in range(2):
                        self.memset("dve", qm[hb_][c_][:], 0.0, w=[(L + "qT", hb_, c0_) for c0_ in (0, 512, 1024)])
                gbs = [0, 1]
                gci = 0
                pendA = [None]
                pendB = [None]
                since = [0]
                ptc = 0
                sci = 0
                tcount = 0
                wl = {}

                def inproj(h):
                        nonlocal gci
                        hp, hh = divmod(h, 4)
                        hb = h % 2
                        if hh == 0:
                            wl[hp] = (self.wload(win, hp * 512), self.wload(win, 3072 + hp * 512))
                        iq, igt = wl[hp]
                        for (tok0, c0, n) in oblocks:
                            bi = gbs[gci % 2]; gci += 1
                            self.mm16(bi, n, lambda kc: self.wb[iq][:, kc, hh * 128:(hh + 1) * 128],
                                      lambda kc: hxA[:, kc, tok0:tok0 + n], r=[("wb", iq)] + self.hxkeys(tok0, n))
                            qk = (L + "qT", hb, c0)
                            if c0 >= 1024:
                                self.cp("act", qm[hb][0][0:64, c0:c0 + n], B[bi][0:64, :n], r=[("ps", bi)], w=[qk])
                                self.cp("act", qm[hb][1][64:128, c0:c0 + n], B[bi][64:128, :n], r=[("ps", bi)], w=[qk])
                            else:
                                self.cp("act", k_sb[:, :n], B[bi][:, :n], r=[("ps", bi)], w=[L + "qk_sb"])
                                br = gbs[gci % 2]; gci += 1
                                self.mm(B[br][:, :n], self.rotm_b[:], k_sb[:, :n], True, True,
                                        r=["rotm_b", L + "qk_sb"], w=[("ps", br)])
                                self.tt("dve", t1[:, :n], B[bi][:, :n], cosq[:, c0:c0 + n], ALU.mult,
                                        r=[("ps", bi), L + "cosq", L + "qk_sb"], w=[L + "qt1"])
                                self.tt("dve", t2[:, :n], B[br][:, :n], sinq[:, c0:c0 + n], ALU.mult,
                                        r=[("ps", br), L + "sinq"], w=[L + "qt2"])
                                self.tt("dve", qm[hb][0][0:64, c0:c0 + n], t1[0:64, :n], t2[0:64, :n], ALU.add,
                                        r=[L + "qt1", L + "qt2"], w=[qk])
                                self.tt("dve", qm[hb][1][64:128, c0:c0 + n], t1[64:128, :n], t2[64:128, :n], ALU.add,
                                        r=[L + "qt1", L + "qt2"], w=[qk])
                            bi = gbs[gci % 2]; gci += 1
                            self.mm16(bi, n, lambda kc: self.wb[igt][:, kc, hh * 128:(hh + 1) * 128],
                                      lambda kc: hxA[:, kc, tok0:tok0 + n], r=[("wb", igt)] + self.hxkeys(tok0, n))
                            self.act(sgT[hb][:, c0:c0 + n], B[bi][:, :n], AF.Silu, r=[("ps", bi)],
                                     w=[(L + "sgT", hb, c0)])
                        self.dma("sp", kTs[hb][:], self.kT_d[h], r=[("kT_d", h)], w=[(L + "kT", hb)], sem=f"kT{hb}")
                        self.dma("sp", Vhs[hb][:], self.v_d[h], r=[("v_d", h)], w=[(L + "Vh", hb)], sem=f"Vh{hb}")

                inproj(0)
                for h in range(8):
                        hb = h % 2
                        kT = kTs[hb]
                        Vh = Vhs[hb]
                        qblocks = [(0, 512, list(range(18))), (512, 512, list(range(18)))]
                        if has_ctx:
                            qblocks.append((1024, 256, [0, 1]))
                        for qbi, (c0, n, kts) in enumerate(qblocks):
                            if qbi == 1 and h + 1 < 8:
                                inproj(h + 1)
                            nqs = n // 128
                            items = [(c, ki, kt) for c in range(2) for ki, kt in enumerate(kts)]

                            def emit_pv(it, pi):
                                c, ki, kt = it
                                for qs in range(nqs):
                                    self.mm(B[4 + qs][:, 0:129], PT[pi][:, qs * 128:(qs + 1) * 128], Vh[:, kt, 0:129],
                                            ki == 0, ki == len(kts) - 1, r=[(L + "PT", pi), (L + "Vh", hb)],
                                            w=[("ps", 4 + qs)])
                                if ki == len(kts) - 1:
                                    if pendA[0] is not None:
                                        pendA[0](); pendA[0] = None
                                    for qs in range(nqs):
                                        self.cp("dve", Osb[:, c, qs, 0:129], B[4 + qs][:, 0:129], r=[("ps", 4 + qs)],
                                                w=[(L + "Osb", c, qs)])
                            pendq = []
                            for it in items:
                                c, ki, kt = it
                                sb_ = 1 + (sci % 3); sci += 1
                                self.mm(B[sb_][:, :n], kT[:, kt * 128:(kt + 1) * 128],
                                        qm[hb][c][:, c0:c0 + n], True, True,
                                        r=[(L + "kT", hb), (L + "qT", hb, c0)], w=[("ps", sb_)])
                                pi = ptc % NPT; ptc += 1
                                self.act(PT[pi][:, :n], B[sb_][:, :n], AF.Exp, r=[("ps", sb_)], w=[(L + "PT", pi)],
                                         scale=0.125)
                                if len(pendq) == 2:
                                    emit_pv(*pendq.pop(0))
                                pendq.append((it, pi))
                                since[0] += 1
                                if since[0] == 3 and pendA[0] is not None:
                                    pendA[0](); pendA[0] = None
                                if since[0] == 10 and pendB[0] is not None:
                                    if pendA[0] is not None:
                                        pendA[0](); pendA[0] = None
                                    pendB[0](); pendB[0] = None
                            while pendq:
                                emit_pv(*pendq.pop(0))
                            def make_epi(h=h, hb=hb, c0=c0, nqs=nqs):
                                def epiA():
                                    Q = range(nqs)
                                    okr = lambda qs: [(L + "Osb", 0, qs), (L + "Osb", 1, qs)]
                                    k = lambda nm, qs: (L + "osm_" + nm, qs)
                                    for qs in Q:
                                        self.recip(osm[:, qs, 0:1], Osb[:, 0, qs, 128:129], r=okr(qs) + [L + "osm"], w=[k("rz0", qs)])
                                    for qs in Q:
                                        self.recip(osm[:, qs, 1:2], Osb[:, 1, qs, 128:129], r=okr(qs) + [L + "osm"], w=[k("rz1", qs)])
                                    for qs in Q:
                                        self.tt("dve", osm[:, qs, 2:3], osm[:, qs, 1:2], neglam, ALU.mult,
                                                r=[k("rz1", qs), L + "neglam"], w=[k("nl", qs)])
                                    for qs in Q:
                                        self.tsm("dve", o_sb[qs][:], Osb[:, 0, qs, 0:128], osm[:, qs, 0:1], r=okr(qs) + [k("rz0", qs)],
                                                 w=[(L + "o_sb", qs)])
                                    for qs in Q:
                                        self.stt("dve", o_sb[qs][:], Osb[:, 1, qs, 0:128], osm[:, qs, 2:3], o_sb[qs][:],
                                                 ALU.mult, ALU.add, r=okr(qs) + [k("nl", qs), (L + "o_sb", qs)], w=[(L + "o_sb", qs)])
                                    for qs in Q:
                                        self.stt("dve", junko[:, qs * 32:qs * 32 + 32].bitcast(F32) if False else junko4[qs][:], o_sb[qs][:], 1.0, o_sb[qs][:], ALU.mult, ALU.mult,
                                                 r=[(L + "o_sb", qs), L + "osm"], w=[(L + "junko", qs), k("ss", qs)], accum_out=osm[:, qs, 3:4])
                                    for qs in Q:
                                        self.ts("dve", osm[:, qs, 4:5], osm[:, qs, 3:4], 1.0 / 128, EPS, ALU.mult, ALU.add,
                                                r=[k("ss", qs)], w=[k("ms", qs)])
                                    for qs in Q:
                                        self.act(osm[:, qs, 5:6], osm[:, qs, 4:5], AF.Ln, r=[k("ms", qs)], w=[k("ln", qs)])
                                    for qs in Q:
                                        self.act(osm[:, qs, 6:7], osm[:, qs, 5:6], AF.Exp, r=[k("ln", qs)], w=[k("rstd", qs)], scale=-0.5)
                                    for qs in Q:
                                        self.tsm("dve", on_b[qs][:], o_sb[qs][:], osm[:, qs, 6:7], r=[(L + "o_sb", qs), k("rstd", qs)],
                                                 w=[(L + "on_b", qs)])

                                def epiB():
                                    nonlocal gci
                                    for qs in range(nqs):
                                        tcol = c0 + qs * 128
                                        bt = 0
                                        pv = B[bt][:].bitcast(BF16)
                                        self.tr(pv[:, (qs % 4) * 128:(qs % 4 + 1) * 128], on_b[qs][:], self.ident_b[:], r=[(L + "on_b", qs), "ident_b"],
                                                w=[("ps", bt)])
                                        self.stt("dve", ytv(h, tcol, 128), pv[:, (qs % 4) * 128:(qs % 4 + 1) * 128], subg2, sgT[hb][:, tcol:tcol + 128],
                                                 ALU.mult, ALU.mult,
                                                 r=[("ps", bt), L + "subg2", (L + "sgT", hb, (tcol // 512) * 512 if tcol < 1024 else 1024)],
                                                 w=ytk(h, tcol, 128))
                                return epiA, epiB
                            if pendA[0] is not None:
                                pendA[0](); pendA[0] = None
                            if pendB[0] is not None:
                                pendB[0](); pendB[0] = None
                            eA, eB = make_epi()
                            pendA[0] = eA
                            pendB[0] = eB
                            since[0] = 0
                if pendA[0] is not None:
                    pendA[0](); pendA[0] = None
                if pendB[0] is not None:
                    pendB[0](); pendB[0] = None
                P.barrier()
            if self.debug and not passB:
                dy = self.outp(f"dbg_yT{l}", [128, 16, 1024], BF16)
                self.dma("sp", dy, hxA[:, :, 1280:2304], r=[("yT", kc, t) for kc in range(16) for t in range(8)],
                         w=[f"dbg_yT{l}"], sem="dbg2")
                P.barrier()
                self.final_keys.add(f"dbg_yT{l}")
            if self.stop == "ATT":
                return

            with self.scope() as pes:
                nr = 2 if has_ctx else 1
                gate = [self.sb(pes, [128, D], F32) for _ in range(nr)]
                bm = [self.sb(pes, [128, 512], F32) for _ in range(2)]
                s_f = self.sb(pes, [128, 32], F32)
                srep = self.sb(pes, [128, 32, 128], BF16)
                xs = [self.sb(pes, [128, 512], F32) for _ in range(3)]
                xo = [self.sb(pes, [128, 512], F32) for _ in range(3)]
                self.act(s_f[:], self.cT[:], AF.Silu, r=["cT"], w=[L + "s_f2"])
                for j in range(16 * nr):
                    self.tsm("dve", srep[:, j, :], self.ones_b[:], s_f[:, j:j + 1], r=["ones_b", L + "s_f2"],
                             w=[(L + "srep2", j)])
                for g in range(8, 12):
                    i = self.wload(wmod, g * 512)
                    self.dma("sp", bm[g % 2][:], bmod[0:1, g * 512:(g + 1) * 512].partition_broadcast(128),
                             r=[], w=[(L + "bm2", g % 2)], sem=f"bmo{g%2}")
                    for r_ in range(nr):
                        bi = self.bank()
                        self.mm16(bi, 512, lambda kc: srep[:, r_ * 16 + kc, :], lambda kc: self.wb[i][:, kc, :],
                                  r=[("wb", i)] + [(L + "srep2", r_ * 16 + kc) for kc in range(16)])
                        self.tt("dve", gate[r_][:, (g - 8) * 512:(g - 7) * 512], B[bi][:], bm[g % 2][:], ALU.add,
                                r=[("ps", bi), (L + "bm2", g % 2)], w=[(L + "gate", r_, g - 8)])
                tiles = [(t, False) for t in range(8)] + ([(8, True), (9, True)] if has_ctx else [])
                cnt = 0
                for gw in range(4):
                    i = self.wload(wout, gw * 512)
                    for (t, isc) in tiles:
                        q = cnt % 3
                        cnt += 1
                        if isc:
                            tt_ = t - 8
                            sap = src["ctx"][tt_ * 128:(tt_ + 1) * 128, gw * 512:(gw + 1) * 512]
                            dap = cdst[tt_ * 128:(tt_ + 1) * 128, gw * 512:(gw + 1) * 512]
                            dk = ("cdst", tt_)
                            r_ = 1
                        else:
                            sap = src["own"][t * 128:(t + 1) * 128, gw * 512:(gw + 1) * 512]
                            dap = xdst[t * 128:(t + 1) * 128, gw * 512:(gw + 1) * 512]
                            dk = (xkey, t)
                            r_ = 0
                        self.dma("sp", xs[q][:], sap, r=[], w=[(L + "xs", q)], sem=f"xs{q}")
                        bi = self.bank()
                        c0 = t * 128
                        self.mm16(bi, 512, lambda kc: ytv(kc, c0, 128), lambda kc: self.wb[i][:, kc, :],
                                  r=[("wb", i)] + [("yT", kc, t) for kc in range(16)])
                        self.tt("dve", xo[q][:], B[bi][:], gate[r_][:, gw * 512:(gw + 1) * 512], ALU.mult,
                                r=[("ps", bi), (L + "gate", r_, gw)], w=[(L + "xo", q)])
                        self.tt("dve", xo[q][:], xo[q][:], xs[q][:], ALU.add, r=[(L + "xo", q), (L + "xs", q)],
                                w=[(L + "xo", q)])
                        self.dma("sp", dap, xo[q][:], r=[(L + "xo", q)], w=[dk], sem=f"xo{q}")
                P.barrier()
            if not last:
                if not self.fused:
                    self.final_keys |= {("xdst", t) for t in range(8)} | {("cdst", t) for t in range(2)}
            else:
                with self.scope() as pes:
                    fg = self.sb(pes, [128, D], F32)
                    xt = [self.sb(pes, [128, D], F32) for _ in range(2)]
                    yo = [self.sb(pes, [128, D], F32) for _ in range(2)]
                    junkb = self.sb(pes, [128, D], BF16)
                    st = self.sb(pes, [128, 4, 8], F32)
                    self.memset("dve", st[:], 0.0, w=[L + "fst"])
                    self.dma("sp", fg[:], fg_d.partition_broadcast(128), r=[], w=[L + "fg"], sem="fg")
                    for t in range(8):
                        q = t % 2
                        self.dma("sp", xt[q][:], xdst[t * 128:(t + 1) * 128, :], r=[(xkey, t)], w=[(L + "fxt", q)],
                                 sem=f"fxt{q}")
                        self.act(junkb[:], xt[q][:], AF.Square, r=[(L + "fxt", q), L + "fst"],
                                 w=[L + "fjunk", (L + "fss", t)], accum_out=st[:, 0, t:t + 1])
                        self.ts("dve", st[:, 1, t:t + 1], st[:, 0, t:t + 1], 1.0 / D, EPS, ALU.mult, ALU.add,
                                r=[(L + "fss", t)], w=[(L + "fms", t)])
                        self.act(st[:, 2, t:t + 1], st[:, 1, t:t + 1], AF.Ln, r=[(L + "fms", t)], w=[(L + "fsd", t)])
                        self.act(st[:, 3, t:t + 1], st[:, 2, t:t + 1], AF.Exp, r=[(L + "fsd", t)], w=[(L + "frs", t)],
                                 scale=-0.5)
                        self.stt("dve", yo[q][:], xt[q][:], st[:, 3, t:t + 1], fg[:], ALU.mult, ALU.mult,
                                 r=[(L + "fxt", q), (L + "frs", t), L + "fg"], w=[(L + "yo", q)])
                        self.dma("sp", ydst[t * 128:(t + 1) * 128, :], yo[q][:], r=[(L + "yo", q)], w=[("ydst", t)],
                                 sem=f"yo{q}")
                    self.final_keys |= {("ydst", t) for t in range(8)}
                    P.barrier()


_CACHE = {}


def _get(layers, fused, debug=False, stop=None):
    key = (tuple(layers), fused, debug, stop)
    if key not in _CACHE:
        b = Builder(list(layers), fused, debug, stop)
        nc = b.build()
        _CACHE[key] = (b, nc)
    return _CACHE[key]


def _col(v, n):
    return np.ascontiguousarray(np.asarray(v, np.float32).reshape(n, 128).T)


def _rope_tables():
    n_freq = 16
    inv = (np.float32(10000.0) ** (-np.arange(n_freq, dtype=np.float32) / np.float32(n_freq))).astype(np.float32)
    t = np.arange(2048)
    row = (t // 64).astype(np.float32)
    col = (t % 64).astype(np.float32)
    cos = np.zeros((128, 2048), np.float32)
    sin = np.zeros((128, 2048), np.float32)
    for p in range(128):
        d = p % 64
        pos = row if d < 32 else col
        j = (d % 32) % 16
        ang = (pos * inv[j]).astype(np.float32)
        cos[p] = np.cos(ang)
        sin[p] = np.sin(ang)
    return cos, sin


def _rotm():
    m = np.zeros((128, 128), np.float32)
    for mm_ in range(128):
        if (mm_ % 32) < 16:
            m[mm_ + 16, mm_] = -1.0
        else:
            m[mm_ - 16, mm_] = 1.0
    return m


def _layer_inputs(l, inp):
    d = {}
    d[f"wmod{l}"] = np.ascontiguousarray(inp["w_mod"][l], np.float32)
    d[f"bmod{l}"] = np.ascontiguousarray(inp["b_mod"][l], np.float32).reshape(1, -1)
    d[f"ng{l}"] = _col(inp["norm_g"][l], 16)
    d[f"win{l}"] = np.ascontiguousarray(inp["w_in"][l], np.float32)
    d[f"lam{l}"] = np.concatenate([inp["lambda_q1"][l], inp["lambda_k1"][l], inp["lambda_q2"][l],
                                   inp["lambda_k2"][l]]).astype(np.float32).reshape(1, 256)
    d[f"subg{l}"] = np.ascontiguousarray(np.asarray(inp["subln_g"][l], np.float32).reshape(128, 1))
    d[f"wpool{l}"] = np.ascontiguousarray(inp["w_pool"][l], np.float32)
    d[f"pscale{l}"] = _col(inp["pool_scale"][l], 4)
    wdw = np.asarray(inp["w_dw"][l], np.float32)
    d[f"wdw{l}"] = np.ascontiguousarray(wdw.T.reshape(4, 128, 31).transpose(1, 0, 2))
    d[f"cvec{l}"] = np.ascontiguousarray(np.concatenate(
        [_col(inp["b_dw"][l], 4), _col(inp["conv_ln_g"][l], 4), _col(inp["conv_ln_b"][l], 4)], axis=1))
    d[f"wpw2{l}"] = np.ascontiguousarray(inp["w_pw2"][l], np.float32)
    d[f"wout{l}"] = np.ascontiguousarray(inp["w_out"][l], np.float32)
    if l == 1:
        d["fg"] = np.asarray(inp["final_g"], np.float32).reshape(1, -1)
    return d


def _core_consts(core, inp):
    b, h = core // 2, core % 2
    cos, sin = _rope_tables()
    order = np.concatenate([np.arange(h * 1024, (h + 1) * 1024), np.arange((1 - h) * 1024, (2 - h) * 1024)])
    d = {}
    d["ident"] = np.eye(128, dtype=np.float32)
    d["rotm"] = _rotm()
    d["cosk"] = np.ascontiguousarray(cos[:, order])
    d["sink"] = np.ascontiguousarray(sin[:, order])
    m = np.zeros((128, 4), np.float32)
    m[:, 0] = 1.0 if h == 1 else 0.0
    m[:, 1] = 1.0 if h == 0 else 0.0
    m[:, 2] = 1.0 if h == 1 else 0.0
    m[:, 3] = 1.0 if h == 0 else 0.0
    d["msk"] = m
    cT = np.concatenate([_col(inp["c"][b], 16), _col(inp["c_ctx"], 16)], axis=1)
    d["cT"] = np.ascontiguousarray(cT)
    return d


def _run(layers, fused, inp, x, ctx, debug=False, stop=None):
    b_, nc = _get(layers, fused, debug, stop)
    in_maps = []
    ncr = int(os.environ.get("DBG_NCORES", NCORES)) if debug else NCORES
    for core in range(ncr):
        b, h = core // 2, core % 2
        d = _core_consts(core, inp)
        for l in layers:
            d.update(_layer_inputs(l, inp))
        d["x_own"] = np.ascontiguousarray(x[b, h * 1024:(h + 1) * 1024])
        d["x_oth"] = np.ascontiguousarray(x[b, (1 - h) * 1024:(2 - h) * 1024])
        d["ctx_in"] = np.ascontiguousarray(ctx[b])
        in_maps.append(d)
    res = run_bass_kernel_spmd(nc, in_maps, core_ids=list(range(ncr)))
    return res.results


FUSED = True


def kernel(**inp):
    inp = {k: np.asarray(v) for k, v in inp.items()}
    x = np.asarray(inp["x"], np.float32)
    ctx = np.asarray(inp["ctx"], np.float32)
    if FUSED:
        res = _run([0, 1], True, inp, x, ctx)
    else:
        r0 = _run([0], False, inp, x, ctx)
        x1 = np.empty_like(x)
        ctx1 = np.empty_like(ctx)
        for core in range(NCORES):
            b, h = core // 2, core % 2
            x1[b, h * 1024:(h + 1) * 1024] = r0[core]["x1_own"]
            if h == 0:
                ctx1[b] = r0[core]["ctx1"]
        res = _run([1], False, inp, x1, ctx1)
    out = np.empty((4, 2048, 2048), np.float32)
    for core in range(NCORES):
        b, h = core // 2, core % 2
        out[b, h * 1024:(h + 1) * 1024] = res[core]["y"]
    return out
```

```python
import math
import os
import numpy as np
import ml_dtypes
from contextlib import ExitStack
import concourse.bass as bass
import concourse.mybir as mybir
from concourse.bass_utils import run_bass_kernel_spmd

F32 = mybir.dt.float32
BF16 = mybir.dt.bfloat16
AF = mybir.ActivationFunctionType
ALU = mybir.AluOpType
AX = mybir.AxisListType

ENGS = ("pe", "act", "dve", "pool", "sp")
D = 2048
NIN = 6656
EPS = 1e-6
NCORES = 8


class Op:
    __slots__ = ("eng", "fn", "dma", "deps_hard", "deps_war", "needs_inc", "inc_idx",
                 "dma_cnt", "waits", "idx")

    def __init__(self, eng, fn, dma):
        self.eng = eng
        self.fn = fn
        self.dma = dma
        self.deps_hard = set()
        self.deps_war = set()
        self.needs_inc = False
        self.inc_idx = 0
        self.dma_cnt = 0
        self.waits = []


class Prog:
    def __init__(self, nc, es):
        self.nc = nc
        self.es = es
        self.ops = []
        self.last_w = {}
        self.readers = {}
        self.dma_count = {}
        self.nt = 0

    def add(self, eng, fn, r=(), w=(), dma=None):
        op = Op(eng, fn, dma)
        op.idx = len(self.ops)
        for k in r:
            lw = self.last_w.get(k)
            if lw is not None:
                op.deps_hard.add(lw)
        for k in w:
            lw = self.last_w.get(k)
            if lw is not None:
                op.deps_hard.add(lw)
            rd = self.readers.get(k)
            if rd:
                for o in rd.values():
                    op.deps_war.add(o)
        ent = ("dma", dma) if dma else ("eng", eng)
        for k in r:
            if isinstance(k, tuple) and k and k[0] == "ps":
                rd = self.readers.get(k)
                if rd:
                    for ent2, o in rd.items():
                        if ent2 != ent:
                            op.deps_hard.add(o)
        if fn is not None:
            for k in r:
                self.readers.setdefault(k, {})[ent] = op.idx
            for k in w:
                self.last_w[k] = op.idx
                self.readers[k] = {}
        if dma:
            self.dma_count[dma] = self.dma_count.get(dma, 0) + 1
            op.dma_cnt = self.dma_count[dma]
        self.ops.append(op)
        return op

    def barrier(self):
        allk = list(dict.fromkeys(list(self.last_w.keys()) + list(self.readers.keys())))
        for e in ENGS:
            self.add(e, None, r=allk, w=allk)

    def resolve(self):
        ops = self.ops
        for op in ops:
            need = set()
            for d in op.deps_hard:
                po = ops[d]
                if po.dma:
                    need.add(d)
                elif po.eng == op.eng and not op.dma and op.eng == "pe":
                    continue
                else:
                    need.add(d)
            for d in op.deps_war:
                po = ops[d]
                if po.dma:
                    need.add(d)
                elif po.eng == op.eng and op.eng == "pe" and not op.dma:
                    continue
                else:
                    need.add(d)
            op.waits = need
            for d in need:
                if not ops[d].dma:
                    ops[d].needs_inc = True
        cnt = {e: 0 for e in ENGS}
        for op in ops:
            if op.needs_inc:
                assert op.fn is not None
                cnt[op.eng] += 1
                op.inc_idx = cnt[op.eng]
        self.eng_total = cnt
        waited = {e: {} for e in ENGS}
        for op in ops:
            wl = {}
            for d in op.waits:
                po = ops[d]
                if po.dma:
                    key = ("dma", po.dma)
                    val = 16 * po.dma_cnt
                else:
                    key = ("eng", po.eng)
                    val = po.inc_idx
                if val > wl.get(key, 0):
                    wl[key] = val
            out = []
            for key, val in wl.items():
                if waited[op.eng].get(key, 0) >= val:
                    continue
                waited[op.eng][key] = val
                out.append((key, val))
            op.waits = out

    def emit(self):
        nc = self.nc
        self.resolve()
        sems = {}
        for e in ENGS:
            sems[("eng", e)] = self.es.enter_context(nc.semaphore(f"s_{e}"))
        for name in self.dma_count:
            sems[("dma", name)] = self.es.enter_context(nc.semaphore(f"d_{name}"))
        per = {e: [op for op in self.ops if op.eng == e] for e in ENGS}

        def run(engname, eng):
            for op in per[engname]:
                for key, val in op.waits:
                    eng.wait_ge(sems[key], val)
                if op.fn is None:
                    continue
                ins = op.fn(eng)
                if op.dma:
                    ins.then_inc(sems[("dma", op.dma)], 16)
                elif op.needs_inc:
                    ins.then_inc(sems[("eng", engname)], 1)

        with nc.Block() as block:
            @block.tensor
            def _(e):
                run("pe", e)

            @block.scalar
            def _(e):
                run("act", e)

            @block.vector
            def _(e):
                run("dve", e)

            @block.gpsimd
            def _(e):
                run("pool", e)

            @block.sync
            def _(e):
                run("sp", e)


class Builder:
    def __init__(self, layers, fused, debug=False, stop=None):
        self.stop = stop
        self.layers = layers
        self.fused = fused
        self.debug = debug
        self.nc = bass.Bass("TRN2", target_bir_lowering=False)
        self.din = {}
        self.dout = {}

    def inp(self, name, shape, dt=F32):
        if name in self.din:
            return self.din[name]
        t = self.nc.dram_tensor(name, list(shape), dt, kind="ExternalInput").ap()
        self.din[name] = t
        return t

    def outp(self, name, shape, dt=F32):
        t = self.nc.dram_tensor(name, list(shape), dt, kind="ExternalOutput").ap()
        self.dout[name] = t
        return t

    def scratch(self, name, shape, dt):
        if self.debug:
            return self.outp(name, shape, dt)
        return self.nc.dram_tensor(name, list(shape), dt, kind="Internal").ap()

    def scope(self):
        b = self

        class _S:
            def __enter__(self_):
                self_.mark = b.off
                return self_

            def __exit__(self_, *a):
                b.off = self_.mark
                return False
        return _S()

    def sb(self, es, shape, dt, name=None):
        esz = 4 if dt == F32 else 2
        n = 1
        for d_ in shape[1:]:
            n *= d_
        nbytes = (n * esz + 255) // 256 * 256
        off = self.off
        self.off += nbytes
        self.peak = max(self.peak, self.off)
        assert self.off <= self.BIGBYTES, f"SBUF overflow {self.off}"
        v = self.big[:, off // 2:(off + n * esz) // 2]
        if dt == F32:
            v = v.bitcast(F32)
        if len(shape) == 3:
            v = v.rearrange("p (a b) -> p a b", a=shape[1])
        elif len(shape) == 4:
            v = v.rearrange("p (a b c) -> p a b c", a=shape[1], b=shape[2])
        return v

    def dma(self, q, out, in_, r, w, sem):
        self.P.add(q, lambda e, o=out, i=in_: e.dma_start(out=o, in_=i), r=r, w=w, dma=sem)

    def mm(self, ps, lhsT, rhs, start, stop, r, w):
        self.P.add("pe", lambda e, a=ps, b=lhsT, c=rhs, s0=start, s1=stop:
                   e.matmul(a, lhsT=b, rhs=c, start=s0, stop=s1), r=r, w=w)

    def tr(self, ps, in_, ident, r, w):
        self.P.add("pe", lambda e, a=ps, b=in_, c=ident: e.transpose(a, b, c), r=r, w=w)

    def act(self, out, in_, func, r, w, **kw):
        self.P.add("act", lambda e, o=out, i=in_, f=func, k=kw: e.activation(out=o, in_=i, func=f, **k), r=r, w=w)

    def tt(self, eng, out, in0, in1, op, r, w):
        self.P.add(eng, lambda e, o=out, a=in0, b=in1, p=op: e.tensor_tensor(out=o, in0=a, in1=b, op=p), r=r, w=w)

    def ts(self, eng, out, in0, s1, s2, op0, op1, r, w):
        self.P.add(eng, lambda e, o=out, a=in0, x=s1, y=s2, p=op0, q=op1:
                   e.tensor_scalar(out=o, in0=a, scalar1=x, scalar2=y, op0=p, op1=q), r=r, w=w)

    def tsm(self, eng, out, in0, s1, r, w):
        self.P.add(eng, lambda e, o=out, a=in0, x=s1: e.tensor_scalar_mul(out=o, in0=a, scalar1=x), r=r, w=w)

    def stt(self, eng, out, in0, scalar, in1, op0, op1, r, w, accum_out=None):
        if accum_out is None:
            self.P.add(eng, lambda e, o=out, a=in0, s=scalar, b=in1, p=op0, q=op1:
                       e.scalar_tensor_tensor(out=o, in0=a, scalar=s, in1=b, op0=p, op1=q), r=r, w=w)
        else:
            self.P.add(eng, lambda e, o=out, a=in0, s=scalar, b=in1, p=op0, q=op1, ac=accum_out:
                       e.scalar_tensor_tensor(out=o, in0=a, scalar=s, in1=b, op0=p, op1=q, accum_out=ac), r=r, w=w)

    def cp(self, eng, out, in_, r, w):
        if eng == "act":
            self.P.add("act", lambda e, o=out, i=in_: e.copy(out=o, in_=i), r=r, w=w)
        else:
            self.P.add(eng, lambda e, o=out, i=in_: e.tensor_copy(out=o, in_=i), r=r, w=w)

    def memset(self, eng, ap, val, w):
        self.P.add(eng, lambda e, a=ap, v=val: e.memset(a, v), w=w)

    def recip(self, out, in_, r, w):
        self.P.add("dve", lambda e, o=out, i=in_: e.reciprocal(out=o, in_=i), r=r, w=w)

    def wload(self, view, c0, ncols=512):
        i = self.wcur
        self.wcur ^= 1
        self.dma("pool", self.wb[i][:, :, 0:ncols], view[:, :, c0:c0 + ncols], r=[], w=[("wb", i)], sem=f"wb{i}")
        return i

    def bank(self):
        i = self.gb[self.gbi % len(self.gb)]
        self.gbi += 1
        return i

    def mm16(self, bi, n, lhs_fn, rhs_fn, r):
        for kc in range(16):
            self.mm(self.B[bi][:, 0:n], lhs_fn(kc), rhs_fn(kc), kc == 0, kc == 15, r=r, w=[("ps", bi)])

    def hxkeys(self, tok0, n):
        return [("hx", a) for a in range(tok0 // 128, (tok0 + n + 127) // 128)]

    def build(self):
        nc = self.nc
        with ExitStack() as es:
            self.P = Prog(nc, es)
            P = self.P
            ident_d = self.inp("ident", [128, 128])
            rotm_d = self.inp("rotm", [128, 128])
            cosk_d = self.inp("cosk", [128, 2048])
            sink_d = self.inp("sink", [128, 2048])
            msk_d = self.inp("msk", [128, 4])
            cT_d = self.inp("cT", [128, 32])
            self.cosk_d, self.sink_d = cosk_d, sink_d

            self.BIGBYTES = 206 * 1024
            self.off = 128
            self.peak = 0
            self.big = es.enter_context(nc.sbuf_tensor("big", [128, self.BIGBYTES // 2], BF16))
            self.ident_f = self.sb(es, [128, 128], F32)
            if True:
                self.rotm_b = self.sb(es, [128, 128], BF16)
                self.ident_b = self.sb(es, [128, 128], BF16)
            else:
                self.ident_b = self.sb(es, [128, 128], BF16)
                self.rotm_b = self.sb(es, [128, 128], BF16)
            self.ones_b = self.sb(es, [128, 128], BF16)
            self.ones_f = self.sb(es, [128, 128], F32)
            self.msk = self.sb(es, [128, 4], F32)
            self.cT = self.sb(es, [128, 32], F32)
            self.hxA = self.sb(es, [128, 16, 2304], BF16)
            self.wb = [self.sb(es, [128, 16, 512], BF16) for _ in range(2)]
            self.wcur = 0
            self.B = [es.enter_context(nc.psum_tensor(f"bank{i}", [128, 512], F32)) for i in range(8)]
            self.gb = [0, 1, 2, 3]
            self.gbi = 0

            self.dma("sp", self.ident_f[:], ident_d, r=[], w=["ident_f"], sem="c0")
            self.dma("pool", self.ident_b[:], ident_d, r=[], w=["ident_b"], sem="c1")
            self.dma("pool", self.rotm_b[:], rotm_d, r=[], w=["rotm_b"], sem="c2")
            self.dma("sp", self.msk[:], msk_d, r=[], w=["msk"], sem="c3")
            self.dma("sp", self.cT[:], cT_d, r=[], w=["cT"], sem="c4")
            self.memset("dve", self.ones_b[:], 1.0, w=["ones_b"])
            self.memset("dve", self.ones_f[:], 1.0, w=["ones_f"])

            self.modsave = {l_: ([self.sb(es, [128, 32], F32) for _ in range(2)],
                                 [self.sb(es, [128, 16], F32) for _ in range(2)]) for l_ in self.layers}
            self.kT_d = self.scratch("kT_d", [8, 128, 2304], BF16)
            self.v_d = self.scratch("v_d", [8, 128, 18, 132], BF16)

            if not self.fused:
                l = self.layers[0]
                last = (l == 1)
                src = {"own": self.inp("x_own", [1024, D]), "oth": self.inp("x_oth", [1024, D]),
                       "ctx": self.inp("ctx_in", [256, D]), "blend": None}
                if last:
                    xdst = self.scratch("x2_own", [1024, D], F32)
                    cdst = None
                    ydst = self.outp("y", [1024, D])
                else:
                    xdst = self.outp("x1_own", [1024, D])
                    cdst = self.outp("ctx1", [256, D])
                    ydst = None
                self.layer(l, last, src, xdst, cdst, ydst)
            else:
                x_own = self.inp("x_own", [1024, D])
                x_oth = self.inp("x_oth", [1024, D])
                ctx_in = self.inp("ctx_in", [256, D])
                x1_own = nc.dram_tensor("x1_own", [1024, D], F32, kind="Internal").ap()
                x1_oth = nc.dram_tensor("x1_oth", [1024, D], F32, kind="Internal").ap()
                ctx1 = nc.dram_tensor("ctx1", [256, D], F32, kind="Internal").ap()
                x2_own = nc.dram_tensor("x2_own", [1024, D], F32, kind="Internal").ap()
                ydst = self.outp("y", [1024, D])
                self.layer(0, False, {"own": x_own, "oth": x_oth, "ctx": ctx_in, "blend": None}, x1_own, ctx1, None,
                           mode="full", L="L0", xkey="x1own")
                self.layer(0, False, {"own": x_oth, "oth": x_own, "ctx": ctx_in, "blend": None}, x1_oth, None, None,
                           mode="B", L="L0B", xkey="x1oth")
                self.layer(1, True, {"own": x1_own, "oth": x1_oth, "ctx": ctx1, "blend": None, "own_key": "x1own",
                                     "oth_key": "x1oth"}, x2_own, None, ydst, mode="full", L="L1", xkey="x2own")
            P.add("sp", None, r=list(self.final_keys))
            P.emit()
        return nc

    def layer(self, l, last, src, xdst, cdst, ydst, mode="full", L=None, xkey="xdst"):
        nc, P, B = self.nc, self.P, self.B
        hxA = self.hxA
        lam_init = 0.8 - 0.6 * math.exp(-0.3 * l)
        has_ctx = (not last) and mode == "full"
        passB = mode == "B"
        mL, mR = (1, 0) if passB else (0, 1)
        qoff = 1024 if passB else 0
        with self.scope() as les:
            wmod = self.inp(f"wmod{l}", [D, 3 * D]).rearrange("(kc p) n -> p kc n", p=128)
            bmod = self.inp(f"bmod{l}", [1, 3 * D])
            ng_d = self.inp(f"ng{l}", [128, 16])
            win = self.inp(f"win{l}", [D, NIN]).rearrange("(kc p) n -> p kc n", p=128)
            lam_d = self.inp(f"lam{l}", [1, 256])
            subg_d = self.inp(f"subg{l}", [128, 1])
            wpool_d = self.inp(f"wpool{l}", [4, 128, 128])
            pscale_d = self.inp(f"pscale{l}", [128, 4])
            wdw_d = self.inp(f"wdw{l}", [128, 4, 31])
            cvec_d = self.inp(f"cvec{l}", [128, 12])
            wpw2_d = self.inp(f"wpw2{l}", [512, 512]).rearrange("(c p) n -> p c n", p=128)
            wout = self.inp(f"wout{l}", [D, D]).rearrange("(kc p) n -> p kc n", p=128)
            fg_d = self.inp("fg", [1, D]) if last else None

            sm = self.sb(les, [128, 64], F32)
            ng = self.sb(les, [128, 16], F32)
            lamb = self.sb(les, [128, 256], F32)
            subg = self.sb(les, [128, 1], F32)
            pscale = self.sb(les, [128, 4], F32)
            wdw = self.sb(les, [128, 4, 31], F32)
            cvec = self.sb(les, [128, 12], F32)
            wpool_b = self.sb(les, [128, 4, 128], BF16)
            wpw2_b = self.sb(les, [128, 4, 512], BF16)
            modcol, gs = self.modsave[l]
            ML = f"M{l}"
            junk128 = self.sb(les, [128, 128], F32)
            L = L or f"L{l}"
            self.dma("sp", ng[:], ng_d, r=[], w=[L + "ng"], sem="p0")
            self.dma("sp", lamb[:], lam_d.partition_broadcast(128), r=[], w=[L + "lamb"], sem="p1")
            self.dma("sp", subg[:], subg_d, r=[], w=[L + "subg"], sem="p2")
            self.dma("sp", pscale[:], pscale_d, r=[], w=[L + "pscale"], sem="p3")
            self.dma("sp", wdw[:], wdw_d, r=[], w=[L + "wdw"], sem="p4")
            self.dma("sp", cvec[:], cvec_d, r=[], w=[L + "cvec"], sem="p5")
            self.dma("pool", wpool_b[:], wpool_d.rearrange("g c d -> c g d"), r=[], w=[L + "wpool"], sem="p6")
            self.dma("pool", wpw2_b[:], wpw2_d, r=[], w=[L + "wpw2"], sem="p7")
            self.memset("dve", sm[:], 0.0, w=[L + "sm"])
            self.stt("dve", junk128[:, 0:64], lamb[:, 0:64], 1.0, lamb[:, 64:128], ALU.mult, ALU.mult,
                     r=[L + "lamb", L + "sm"], w=[L + "junk128", L + "sm0"], accum_out=sm[:, 0:1])
            self.stt("dve", junk128[:, 64:128], lamb[:, 128:192], 1.0, lamb[:, 192:256], ALU.mult, ALU.mult,
                     r=[L + "lamb", L + "sm"], w=[L + "junk128b", L + "sm1"], accum_out=sm[:, 1:2])
            self.act(sm[:, 2:4], sm[:, 0:2], AF.Exp, r=[L + "sm0", L + "sm1"], w=[L + "sm23"])
            self.tt("dve", sm[:, 4:5], sm[:, 2:3], sm[:, 3:4], ALU.subtract, r=[L + "sm23"], w=[L + "sm4"])
            self.ts("dve", sm[:, 5:6], sm[:, 4:5], lam_init, -1.0, ALU.add, ALU.mult, r=[L + "sm4"], w=[L + "neglam"])
            self.tsm("dve", sm[:, 6:7], subg[:], 1.0 - lam_init, r=[L + "subg", L + "sm"], w=[L + "subg2"])
            neglam = sm[:, 5:6]
            subg2 = sm[:, 6:7]

            pes_mod = self.scope()
            pes_mod.__enter__()
            pes = pes_mod
            if not passB:
                s_f = self.sb(pes, [128, 32], F32)
                srep = self.sb(pes, [128, 32, 128], BF16)
                bm = [self.sb(pes, [128, 512], F32) for _ in range(2)]
                rowb = [self.sb(pes, [128, 512], F32) for _ in range(2)]
                self.act(s_f[:], self.cT[:], AF.Silu, r=["cT"], w=[L + "s_f"])
                for j in range(32):
                    self.tsm("dve", srep[:, j, :], self.ones_b[:], s_f[:, j:j + 1], r=["ones_b", L + "s_f"],
                             w=[(L + "srep", j)])
                for r_ in range(2):
                    self.memset("dve", modcol[r_][:], 0.0, w=[(ML + "modcol", r_)])
                for g in range(8):
                    i = self.wload(wmod, g * 512)
                    self.dma("sp", bm[g % 2][:], bmod[0:1, g * 512:(g + 1) * 512].partition_broadcast(128),
                             r=[], w=[(L + "bm", g % 2)], sem=f"bm{g%2}")
                    for r_ in range(2):
                        bi = self.bank()
                        self.mm16(bi, 512, lambda kc: srep[:, r_ * 16 + kc, :], lambda kc: self.wb[i][:, kc, :],
                                  r=[("wb", i)] + [(L + "srep", r_ * 16 + kc) for kc in range(16)])
                        self.tt("dve", rowb[r_][:], B[bi][:], bm[g % 2][:], ALU.add,
                                r=[("ps", bi), (L + "bm", g % 2)], w=[(L + "rowb", r_)])
                        for j in range(4):
                            c = g * 4 + j
                            self.stt("dve", junk128[:], rowb[r_][:, j * 128:(j + 1) * 128], 1.0, self.ident_f[:],
                                     ALU.mult, ALU.mult, r=[(L + "rowb", r_), "ident_f", (ML + "modcol", r_)],
                                     w=[L + "junk128", (ML + "modcolc", r_, c)], accum_out=modcol[r_][:, c:c + 1])
                for r_ in range(2):
                    rk = [(ML + "modcolc", r_, c) for c in range(16, 32)]
                    self.ts("dve", gs[r_][:], modcol[r_][:, 16:32], 1.0, 1.0, ALU.add, ALU.mult,
                            r=rk, w=[(ML + "gs0", r_)])
                    self.tt("dve", gs[r_][:], gs[r_][:], ng[:], ALU.mult, r=[(ML + "gs0", r_), L + "ng"],
                            w=[(ML + "gs", r_)])
            shiftk = lambda r_: [(ML + "modcolc", r_, c) for c in range(16)]
            self.final_keys = set()
            if self.debug and not passB:
                dm = self.outp(f"dbg_mod{l}", [128, 96], F32)
                allmk = [(ML + "modcolc", r_, c) for r_ in range(2) for c in range(32)] + [(ML + "gs", 0), (ML + "gs", 1)]
                self.dma("sp", dm[:, 0:32], modcol[0][:], r=allmk, w=["dm0"], sem="dbgm0")
                self.dma("sp", dm[:, 32:64], modcol[1][:], r=allmk, w=["dm1"], sem="dbgm1")
                self.dma("sp", dm[:, 64:80], gs[0][:], r=allmk, w=["dm2"], sem="dbgm2")
                self.dma("sp", dm[:, 80:96], gs[1][:], r=allmk, w=["dm3"], sem="dbgm3")
                self.final_keys |= {"dm0", "dm1", "dm2", "dm3"}
                P.barrier()
            if self.stop == "MOD1":
                return

            with self.scope() as pes:
                xt = [self.sb(pes, [128, D], F32) for _ in range(2)]
                xt2 = [self.sb(pes, [128, D], F32) for _ in range(2)] if src["blend"] is not None else None
                xn = [self.sb(pes, [128, D], BF16) for _ in range(2)]
                junkb = self.sb(pes, [128, D], BF16)
                st = self.sb(pes, [128, 4, 18], F32)
                self.memset("dve", st[:], 0.0, w=[L + "st"])
                hx_tiles = [2, 3, 4, 5, 6, 7, 8, 9, 10, 17] if passB else list(range(18))

                def hx_s1(a):
                        r_ = 1 if a < 2 else 0
                        x = xt[a % 2]
                        xk = (L + "xt", a % 2)
                        if a < 2:
                            sap = src["ctx"][a * 128:(a + 1) * 128, :]
                            self.dma("sp", x[:], sap, r=[("cdst", a)], w=[xk], sem=f"xt{a%2}")
                        elif a < 10:
                            t = a - 2
                            self.dma("sp", x[:], src["own"][t * 128:(t + 1) * 128, :], r=[(src.get("own_key", "none"), t)], w=[xk],
                                     sem=f"xt{a%2}")
                        else:
                            t = a - 10
                            if src["blend"] is None:
                                self.dma("sp", x[:], src["oth"][t * 128:(t + 1) * 128, :], r=[(src.get("oth_key", "none"), t)], w=[xk], sem=f"xt{a%2}")
                            else:
                                x2 = xt2[a % 2]
                                x2k = (L + "xt2", a % 2)
                                self.dma("sp", x[:], src["blend"][t * 128:(t + 1) * 128, :], r=["recv"], w=[xk],
                                         sem=f"xt{a%2}")
                                self.dma("sp", x2[:], src["blend"][1024 + t * 128:1024 + (t + 1) * 128, :], r=["recv"],
                                         w=[x2k], sem=f"xtb{a%2}")
                                self.tsm("dve", x[:], x[:], self.msk[:, 2:3], r=[xk, "msk"], w=[xk])
                                self.stt("dve", x[:], x2[:], self.msk[:, 3:4], x[:], ALU.mult, ALU.add,
                                         r=[xk, x2k, "msk"], w=[xk])
                        self.act(junkb[:], x[:], AF.Square, r=[xk, L + "st"], w=[L + "junkb", (L + "ss", a)],
                                 accum_out=st[:, 0, a:a + 1])
                        self.ts("dve", st[:, 1, a:a + 1], st[:, 0, a:a + 1], 1.0 / D, EPS, ALU.mult, ALU.add,
                                r=[(L + "ss", a)], w=[(L + "ms", a)])

                def hx_s2(a):
                        xk = (L + "xt", a % 2)
                        x = xt[a % 2]
                        self.act(st[:, 2, a:a + 1], st[:, 1, a:a + 1], AF.Sqrt, r=[(L + "ms", a)], w=[(L + "sd", a)])
                        self.recip(st[:, 3, a:a + 1], st[:, 2, a:a + 1], r=[(L + "sd", a)], w=[(L + "rstd", a)])
                        xnk = (L + "xn", a % 2)
                        self.act(xn[a % 2][:], x[:], AF.Identity, r=[xk, (L + "rstd", a)], w=[xnk],
                                 scale=st[:, 3, a:a + 1])

                def hx_s3(a):
                        r_ = 1 if a < 2 else 0
                        xnk = (L + "xn", a % 2)
                        b0 = (a % 2) * 2
                        for kc in range(16):
                            bi = b0 + kc // 8
                            pv = B[bi][:].bitcast(BF16)
                            self.tr(pv[:, (kc % 8) * 128:(kc % 8 + 1) * 128], xn[a % 2][:, kc * 128:(kc + 1) * 128],
                                    self.ident_b[:], r=[xnk, "ident_b"], w=[("ps", bi)])
                        for kc in range(16):
                            bi = b0 + kc // 8
                            pv = B[bi][:].bitcast(BF16)
                            o = hxA[:, kc, a * 128:(a + 1) * 128]
                            i_ = pv[:, (kc % 8) * 128:(kc % 8 + 1) * 128]
                            rr = [("ps", bi), (ML + "gs", r_)] + shiftk(r_)
                            if True:
                                self.ts("dve", o, i_, gs[r_][:, kc:kc + 1], modcol[r_][:, kc:kc + 1], ALU.mult, ALU.add,
                                        r=rr, w=[("hx", a)])
                            else:
                                self.act(o, i_, AF.Identity, r=rr, w=[("hx", a)], scale=gs[r_][:, kc:kc + 1],
                                         bias=modcol[r_][:, kc:kc + 1])

                for i_t, a in enumerate(hx_tiles):
                    hx_s1(a)
                    if i_t >= 1:
                        hx_s2(hx_tiles[i_t - 1])
                        hx_s3(hx_tiles[i_t - 1])
                hx_s2(hx_tiles[-1])
                hx_s3(hx_tiles[-1])
                P.barrier()
            pes_mod.__exit__(None, None, None)
            if self.debug and not passB:
                dh = self.outp(f"dbg_hx{l}", [128, 16, 2304], BF16)
                self.dma("sp", dh, hxA[:], r=[("hx", a) for a in range(18)], w=[f"dbg_hx{l}"], sem="dbg")
                P.barrier()
                self.final_keys.add(f"dbg_hx{l}")
            if self.stop == "HX":
                return

            with self.scope() as pes:
              if not passB:
                cosk = self.sb(pes, [128, 2048], F32)
                sink = self.sb(pes, [128, 2048], F32)
                k_sb = [self.sb(pes, [128, 512], BF16) for _ in range(2)]
                t1 = [self.sb(pes, [128, 512], F32) for _ in range(2)]
                t2 = [self.sb(pes, [128, 512], F32) for _ in range(2)]
                kto = [self.sb(pes, [128, 512], BF16) for _ in range(2)]
                vst = [self.sb(pes, [128, 4, 132], BF16) for _ in range(2)]
                self.dma("sp", cosk[:], self.cosk_d, r=[], w=[L + "cosk"], sem="cosk")
                self.dma("sp", sink[:], self.sink_d, r=[], w=[L + "sink"], sem="sink")
                for j in range(2):
                    self.memset("dve", vst[j][:], 1.0, w=[(L + "vst", j)])
                cnt = 0
                for gk in (2, 3):
                    i = self.wload(win, gk * 512)
                    for hh in range(4):
                        h = (gk - 2) * 4 + hh
                        for (tok0, n, rope) in [(0, 256, False), (256, 512, True), (768, 512, True),
                                                (1280, 512, True), (1792, 512, True)]:
                            j = cnt % 2
                            cnt += 1
                            bi = self.bank()
                            self.mm16(bi, n, lambda kc: self.wb[i][:, kc, hh * 128:(hh + 1) * 128],
                                      lambda kc: hxA[:, kc, tok0:tok0 + n], r=[("wb", i)] + self.hxkeys(tok0, n))
                            if not rope or os.environ.get("KV_NOROPE"):
                                self.cp("act", kto[j][:, :n], B[bi][:, :n], r=[("ps", bi)], w=[(L + "kto", j)])
                            else:
                                RV = os.environ.get("ROPE_VAR", "")
                                self.cp("act", k_sb[j][:, :n], B[bi][:, :n], r=[("ps", bi)], w=[(L + "k_sb", j)])
                                br = self.bank()
                                if RV != "dve_only":
                                    if os.environ.get("ROPE_IDENT"):
                                        self.mm(B[br][:, :n], self.ident_b[:], k_sb[j][:, :n], True, True,
                                                r=["ident_b", (L + "k_sb", j)], w=[("ps", br)])
                                    else:
                                        self.mm(B[br][:, :n], self.rotm_b[:], k_sb[j][:, :n], True, True,
                                                r=["rotm_b", (L + "k_sb", j)], w=[("ps", br)])
                                else:
                                    br = bi
                                if RV == "mm_only":
                                    self.cp("act", kto[j][:, :n], B[br][:, :n], r=[("ps", br)], w=[(L + "kto", j)])
                                    continue
                                p0 = tok0 - 256
                                self.tt("dve", t1[j][:, :n], B[bi][:, :n], cosk[:, p0:p0 + n], ALU.mult,
                                        r=[("ps", bi), L + "cosk", (L + "k_sb", j)], w=[(L + "t1", j)])
                                self.tt("dve", t2[j][:, :n], B[br][:, :n], sink[:, p0:p0 + n], ALU.mult,
                                        r=[("ps", br), L + "sink"], w=[(L + "t2", j)])
                                self.tt("dve", kto[j][:, :n], t1[j][:, :n], t2[j][:, :n], ALU.add,
                                        r=[(L + "t1", j), (L + "t2", j)], w=[(L + "kto", j)])
                            if os.environ.get("KV_NOSTORE") and not (h == 7 and tok0 == 1792):
                                continue
                            self.dma("sp", self.kT_d[h, :, tok0:tok0 + n], kto[j][:, :n], r=[(L + "kto", j)],
                                     w=[("kT_d", h)], sem=f"kto{j}")
                if self.stop == "KVK":
                    self.final_keys |= {("kT_d", h) for h in range(8)}
                    P.barrier()
                    return
                cnt = 0
                for gv in (4, 5):
                    i = self.wload(win, gv * 512)
                    for a in range(18):
                        j = cnt % 2
                        cnt += 1
                        bi = self.bank()
                        self.mm16(bi, 512, lambda kc: hxA[:, kc, a * 128:(a + 1) * 128],
                                  lambda kc: self.wb[i][:, kc, :], r=[("wb", i), ("hx", a)])
                        self.cp("act", vst[j][:, :, 0:128], B[bi][:].rearrange("p (h e) -> p h e", h=4),
                                r=[("ps", bi)], w=[(L + "vst", j)])
                        h0 = (gv - 4) * 4
                        self.dma("sp", self.v_d[h0:h0 + 4, :, a, :].rearrange("h p e -> p h e"), vst[j][:],
                                 r=[(L + "vst", j)], w=[("v_d", h0 + q) for q in range(4)], sem=f"vst{j}")
                P.barrier()

            if self.debug:
                self.final_keys |= {("kT_d", h) for h in range(8)} | {("v_d", h) for h in range(8)}
            if self.stop == "KV":
                return
            nown = 1280 if has_ctx else 1024
            yTc = self.sb(les, [128, 16, 256], BF16) if has_ctx else None

            def ytv(kc, c0, n):
                if c0 < 1024:
                    return hxA[:, kc, 1280 + c0:1280 + c0 + n]
                return yTc[:, kc, c0 - 1024:c0 - 1024 + n]

            def ytk(kc, c0, n):
                return [("yT", kc, t) for t in range(c0 // 128, (c0 + n) // 128)]

            oblocks = [(256, 0, 512), (768, 512, 512)] + ([(0, 1024, 256)] if has_ctx else [])

            with self.scope() as cps:
                u_ext = self.sb(cps, [128, 4, 1056], BF16)
                uc_ext = self.sb(cps, [128, 4, 288], BF16) if has_ctx else None
                pps = self.scope()
                pps.__enter__()
                up_ext = self.sb(cps, [128, 4, 1056], F32)
                upc_ext = self.sb(cps, [128, 4, 288], F32) if has_ctx else None
                if has_ctx:
                    self.memset("dve", uc_ext[:], 0.0, w=[L + "uc_ext"])
                    self.memset("dve", upc_ext[:], 0.0, w=[L + "upc_ext"])
                with self.scope() as pes:
                    sig = [self.sb(pes, [128, 512], F32) for _ in range(2)]
                    ablocks = [(256, 512, False, 16, None), (768, 512, False, 528, None)]
                    if has_ctx:
                        ablocks.append((0, 256, True, 16, None))
                    ablocks += [(2288, 16, False, 0, mL), (1280, 16, False, 1040, mR)]
                    iA = self.wload(win, 5120)
                    iB = self.wload(win, 5632)
                    cnt = 0
                    for j in range(4):
                        for (tok0, n, isc, off, mc) in ablocks:
                            q = cnt % 2
                            cnt += 1
                            ba = self.bank()
                            self.mm16(ba, n, lambda kc: self.wb[iA][:, kc, j * 128:(j + 1) * 128],
                                      lambda kc: hxA[:, kc, tok0:tok0 + n], r=[("wb", iA)] + self.hxkeys(tok0, n))
                            bb = self.bank()
                            self.mm16(bb, n, lambda kc: self.wb[iB][:, kc, j * 128:(j + 1) * 128],
                                      lambda kc: hxA[:, kc, tok0:tok0 + n], r=[("wb", iB)] + self.hxkeys(tok0, n))
                            self.act(sig[q][:, :n], B[bb][:, :n], AF.Sigmoid, r=[("ps", bb)], w=[(L + "sig", q)])
                            dst = (uc_ext if isc else u_ext)[:, j, off:off + n]
                            dk = L + ("uc_ext" if isc else "u_ext")
                            if mc is None:
                                self.tt("dve", dst, B[ba][:, :n], sig[q][:, :n], ALU.mult,
                                        r=[("ps", ba), (L + "sig", q), dk], w=[dk])
                            else:
                                self.stt("dve", dst, B[ba][:, :n], self.msk[:, mc:mc + 1], sig[q][:, :n],
                                         ALU.mult, ALU.mult, r=[("ps", ba), (L + "sig", q), "msk", dk], w=[dk])
                    i8 = self.wload(win, 4096)
                    for j in range(4):
                        for (tok0, n, isc, off, mc) in ablocks:
                            ba = self.bank()
                            self.mm16(ba, n, lambda kc: self.wb[i8][:, kc, j * 128:(j + 1) * 128],
                                      lambda kc: hxA[:, kc, tok0:tok0 + n], r=[("wb", i8)] + self.hxkeys(tok0, n))
                            dst = (upc_ext if isc else up_ext)[:, j, off:off + n]
                            dk = L + ("upc_ext" if isc else "up_ext")
                            if mc is None:
                                self.cp("act", dst, B[ba][:, :n], r=[("ps", ba), dk], w=[dk])
                            else:
                                self.act(dst, B[ba][:, :n], AF.Identity, r=[("ps", ba), "msk", dk], w=[dk],
                                         scale=self.msk[:, mc:mc + 1])
                    P.barrier()

                with self.scope() as pes:
                    sa = [self.sb(pes, [128, 1056], F32) for _ in range(2)]
                    va = [self.sb(pes, [128, 1056], F32) for _ in range(3)]
                    dT = self.sb(pes, [128, 4, nown], BF16)
                    for q_ in range(2):
                        self.memset("dve", sa[q_][:], 0.0, w=[L + "sa" + str(q_)])
                        self.memset("dve", va[q_][:], 0.0, w=[L + "va" + str(q_)])
                    sgp = self.sb(pes, [128, 4, nown], BF16)
                    ig = self.wload(win, 4608)
                    for j in range(4):
                        for (tok0, c0, n) in oblocks:
                            ba = self.bank()
                            self.mm16(ba, n, lambda kc: self.wb[ig][:, kc, j * 128:(j + 1) * 128],
                                      lambda kc: hxA[:, kc, tok0:tok0 + n],
                                      r=[("wb", ig)] + self.hxkeys(tok0, n))
                            self.act(sgp[:, j, c0:c0 + n], B[ba][:, :n], AF.Silu, r=[("ps", ba), L + "sgp"],
                                     w=[L + "sgp"])
                    segs = [(up_ext, L + "up_ext", 1056, 1024, 0, True)]
                    if has_ctx:
                        segs.append((upc_ext, L + "upc_ext", 288, 256, 1024, False))
                    for (U, uk, E, N, c0, is_lat) in segs:
                        V = va[2]
                        self.memset("dve", V[:, :E], 1.0 if is_lat else 0.0, w=[L + "V"])
                        if is_lat:
                            self.tsm("dve", V[:, 0:16], V[:, 0:16], self.msk[:, mL:mL + 1], r=[L + "V", "msk"], w=[L + "V"])
                            self.tsm("dve", V[:, 1040:1056], V[:, 1040:1056], self.msk[:, mR:mR + 1], r=[L + "V", "msk"],
                                     w=[L + "V"])
                        else:
                            self.memset("dve", V[:, 16:16 + N], 1.0, w=[L + "V"])
                        for g in range(4):
                            def steps(src_ap, bufs, keyp, srck):
                                cur = src_ap
                                ck = srck
                                for i in range(g + 1):
                                    nb = bufs[i % 2]
                                    nk = keyp + str(i % 2)
                                    if i == 0:
                                        self.tt("dve", nb[:, 1:E], cur[:, 0:E - 1], cur[:, 1:E], ALU.add,
                                                r=[ck, nk], w=[nk])
                                    else:
                                        sh = 1 << (i - 1)
                                        self.tt("dve", nb[:, sh:E - sh], cur[:, 0:E - 2 * sh], cur[:, 2 * sh:E],
                                                ALU.add, r=[ck, nk], w=[nk])
                                    cur = nb
                                    ck = nk
                                return cur, ck
                            s_fin, sk_ = steps(U[:, g, :], sa, L + "sa", uk)
                            v_fin, vk_ = steps(V, va, L + "va", L + "V")
                            self.recip(v_fin[:, 16:16 + N], v_fin[:, 16:16 + N], r=[vk_], w=[vk_])
                            self.tt("dve", s_fin[:, 16:16 + N], s_fin[:, 16:16 + N], v_fin[:, 16:16 + N], ALU.mult,
                                    r=[sk_, vk_], w=[sk_])
                            self.tt("dve", dT[:, g, c0:c0 + N], s_fin[:, 16:16 + N], U[:, g, 16:16 + N],
                                    ALU.subtract, r=[sk_, uk, L + "dT"], w=[L + "dT"])
                    for g in range(4):
                        for (tok0, c0, n) in oblocks:
                            ba = self.bank()
                            self.mm(B[ba][:, :n], wpool_b[:, g, :], dT[:, g, c0:c0 + n], True, True,
                                    r=[L + "wpool", L + "dT"], w=[("ps", ba)])
                            self.stt("dve", ytv(8 + g, c0, n), B[ba][:, :n], pscale[:, g:g + 1], sgp[:, g, c0:c0 + n],
                                     ALU.mult, ALU.mult, r=[("ps", ba), L + "pscale", L + "sgp"],
                                     w=ytk(8 + g, c0, n))
                    P.barrier()
                pps.__exit__(None, None, None)

                with self.scope() as pes:
                    diag = self.sb(pes, [128, 4, 31, 128], BF16)
                    ybuf = self.sb(pes, [128, 4, 512], F32)
                    ysq = self.sb(pes, [128, 4, 512], F32)
                    mst = self.sb(pes, [128, 4, 512], F32)
                    sT = self.sb(pes, [128, 4, 512], BF16)
                    sgc = self.sb(pes, [128, 4, nown], BF16)
                    ig = self.wload(win, 6144)
                    for j in range(4):
                        for (tok0, c0, n) in oblocks:
                            ba = self.bank()
                            self.mm16(ba, n, lambda kc: self.wb[ig][:, kc, j * 128:(j + 1) * 128],
                                      lambda kc: hxA[:, kc, tok0:tok0 + n],
                                      r=[("wb", ig)] + self.hxkeys(tok0, n))
                            self.act(sgc[:, j, c0:c0 + n], B[ba][:, :n], AF.Silu, r=[("ps", ba), L + "sgc"],
                                     w=[L + "sgc"])
                    for c in range(4):
                        for k in range(31):
                            self.tsm("dve", diag[:, c, k, :], self.ident_b[:], wdw[:, c, k:k + 1],
                                     r=["ident_b", L + "wdw"], w=[(L + "diag", c)])
                    cb = [4, 5, 6, 7]
                    for (tok0, c0, n) in oblocks:
                        isc = c0 >= 1024
                        ue = uc_ext if isc else u_ext
                        uk = L + ("uc_ext" if isc else "u_ext")
                        e0 = (c0 - 1024) if isc else c0
                        for c in range(4):
                            bi = cb[c]
                            for k in range(31):
                                self.mm(B[bi][:, :n], diag[:, c, k, :], ue[:, c, e0 + k + 1:e0 + k + 1 + n],
                                        k == 0, k == 30, r=[(L + "diag", c), uk], w=[("ps", bi)])
                            self.act(ybuf[:, c, :n], B[bi][:, :n], AF.Identity, r=[("ps", bi), L + "cvec"],
                                     w=[(L + "ybuf", c)], bias=cvec[:, c:c + 1])
                            self.act(ysq[:, c, :n], B[bi][:, :n], AF.Square, r=[("ps", bi), L + "cvec"],
                                     w=[(L + "ysq", c)], bias=cvec[:, c:c + 1])
                        b1, b2 = 0, 1
                        for c in range(4):
                            self.mm(B[b1][:, :n], self.ones_f[:], ybuf[:, c, :n], c == 0, c == 3,
                                    r=["ones_f", (L + "ybuf", c)], w=[("ps", b1)])
                        for c in range(4):
                            self.mm(B[b2][:, :n], self.ones_f[:], ysq[:, c, :n], c == 0, c == 3,
                                    r=["ones_f", (L + "ysq", c)], w=[("ps", b2)])
                        self.ts("dve", mst[:, 0, :n], B[b1][:, :n], 1.0 / 512, 0.0, ALU.mult, ALU.add,
                                r=[("ps", b1)], w=[L + "m0"])
                        self.tt("dve", mst[:, 1, :n], mst[:, 0, :n], mst[:, 0, :n], ALU.mult, r=[L + "m0"], w=[L + "m1"])
                        self.stt("dve", mst[:, 2, :n], B[b2][:, :n], 1.0 / 512, mst[:, 1, :n], ALU.mult, ALU.subtract,
                                 r=[("ps", b2), L + "m1"], w=[L + "m2"])
                        self.ts("dve", mst[:, 2, :n], mst[:, 2, :n], EPS, 0.0, ALU.add, ALU.add,
                                r=[L + "m2"], w=[L + "m2"])
                        self.act(mst[:, 2, :n], mst[:, 2, :n], AF.Sqrt, r=[L + "m2"], w=[L + "m2"])
                        self.recip(mst[:, 3, :n], mst[:, 2, :n], r=[L + "m2"], w=[L + "m3"])
                        for c in range(4):
                            self.tt("dve", ybuf[:, c, :n], ybuf[:, c, :n], mst[:, 0, :n], ALU.subtract,
                                    r=[(L + "ybuf", c), L + "m0"], w=[(L + "ybuf", c)])
                            self.tt("dve", ybuf[:, c, :n], ybuf[:, c, :n], mst[:, 3, :n], ALU.mult,
                                    r=[(L + "ybuf", c), L + "m3"], w=[(L + "ybuf", c)])
                            self.act(sT[:, c, :n], ybuf[:, c, :n], AF.Silu, r=[(L + "ybuf", c), L + "cvec"],
                                     w=[(L + "sT", c)], scale=cvec[:, 4 + c:5 + c], bias=cvec[:, 8 + c:9 + c])
                        for j in range(4):
                            ba = 2 + (j % 2)
                            for c in range(4):
                                self.mm(B[ba][:, :n], wpw2_b[:, c, j * 128:(j + 1) * 128], sT[:, c, :n], c == 0, c == 3,
                                        r=[L + "wpw2", (L + "sT", c)], w=[("ps", ba)])
                            self.tt("dve", ytv(12 + j, c0, n), B[ba][:, :n], sgc[:, j, c0:c0 + n], ALU.mult,
                                    r=[("ps", ba), L + "sgc"], w=ytk(12 + j, c0, n))
                    P.barrier()

            with self.scope() as pes:
                cosq = self.sb(pes, [128, 1024], F32)
                sinq = self.sb(pes, [128, 1024], F32)
                qm = [[self.sb(pes, [128, nown], BF16) for _ in range(2)] for _ in range(2)]
                sgT = [self.sb(pes, [128, nown], BF16) for _ in range(2)]
                kTs = [self.sb(pes, [128, 2304], BF16) for _ in range(2)]
                Vhs = [self.sb(pes, [128, 18, 132], BF16) for _ in range(2)]
                k_sb = self.sb(pes, [128, 512], BF16)
                t1 = self.sb(pes, [128, 512], F32)
                t2 = self.sb(pes, [128, 512], F32)
                NPT = 6
                PT = [self.sb(pes, [128, 512], BF16) for _ in range(NPT)]
                Osb = self.sb(pes, [128, 2, 4, 132], F32)
                osm = self.sb(pes, [128, 8, 16], F32)
                o_sb = [self.sb(pes, [128, 128], F32) for _ in range(4)]
                on_b = [self.sb(pes, [128, 128], BF16) for _ in range(4)]
                junko4 = [self.sb(pes, [128, 128], F32) for _ in range(4)]
                self.dma("sp", cosq[:], self.cosk_d[:, qoff:qoff + 1024], r=[], w=[L + "cosq"], sem="cosq")
                self.dma("sp", sinq[:], self.sink_d[:, qoff:qoff + 1024], r=[], w=[L + "sinq"], sem="sinq")
                self.memset("dve", osm[:], 0.0, w=[L + "osm"])
                for hb_ in range(2):
                    for c_ in range(2):
                        self.memset("dve", qm[hb_][c_][:], 0.0, w=[(L + "qT", hb_, c0_) for c0_ in (0, 512, 1024)])
                gbs = [0, 1]
                gci = 0
                pendA = [None]
                pendB = [None]
                since = [0]
                ptc = 0
                sci = 0
                tcount = 0
                wl = {}

                def inproj(h):
                        nonlocal gci
                        hp, hh = divmod(h, 4)
                        hb = h % 2
                        if hh == 0:
                            wl[hp] = (self.wload(win, hp * 512), self.wload(win, 3072 + hp * 512))
                        iq, igt = wl[hp]
                        for (tok0, c0, n) in oblocks:
                            bi = gbs[gci % 2]; gci += 1
                            self.mm16(bi, n, lambda kc: self.wb[iq][:, kc, hh * 128:(hh + 1) * 128],
                                      lambda kc: hxA[:, kc, tok0:tok0 + n], r=[("wb", iq)] + self.hxkeys(tok0, n))
                            qk = (L + "qT", hb, c0)
                            if c0 >= 1024:
                                self.cp("act", qm[hb][0][0:64, c0:c0 + n], B[bi][0:64, :n], r=[("ps", bi)], w=[qk])
                                self.cp("act", qm[hb][1][64:128, c0:c0 + n], B[bi][64:128, :n], r=[("ps", bi)], w=[qk])
                            else:
                                self.cp("act", k_sb[:, :n], B[bi][:, :n], r=[("ps", bi)], w=[L + "qk_sb"])
                                br = gbs[gci % 2]; gci += 1
                                self.mm(B[br][:, :n], self.rotm_b[:], k_sb[:, :n], True, True,
                                        r=["rotm_b", L + "qk_sb"], w=[("ps", br)])
                                self.tt("dve", t1[:, :n], B[bi][:, :n], cosq[:, c0:c0 + n], ALU.mult,
                                        r=[("ps", bi), L + "cosq", L + "qk_sb"], w=[L + "qt1"])
                                self.tt("dve", t2[:, :n], B[br][:, :n], sinq[:, c0:c0 + n], ALU.mult,
                                        r=[("ps", br), L + "sinq"], w=[L + "qt2"])
                                self.tt("dve", qm[hb][0][0:64, c0:c0 + n], t1[0:64, :n], t2[0:64, :n], ALU.add,
                                        r=[L + "qt1", L + "qt2"], w=[qk])
                                self.tt("dve", qm[hb][1][64:128, c0:c0 + n], t1[64:128, :n], t2[64:128, :n], ALU.add,
                                        r=[L + "qt1", L + "qt2"], w=[qk])
                            bi = gbs[gci % 2]; gci += 1
                            self.mm16(bi, n, lambda kc: self.wb[igt][:, kc, hh * 128:(hh + 1) * 128],
                                      lambda kc: hxA[:, kc, tok0:tok0 + n], r=[("wb", igt)] + self.hxkeys(tok0, n))
                            self.act(sgT[hb][:, c0:c0 + n], B[bi][:, :n], AF.Silu, r=[("ps", bi)],
                                     w=[(L + "sgT", hb, c0)])
                        self.dma("sp", kTs[hb][:], self.kT_d[h], r=[("kT_d", h)], w=[(L + "kT", hb)], sem=f"kT{hb}")
                        self.dma("sp", Vhs[hb][:], self.v_d[h], r=[("v_d", h)], w=[(L + "Vh", hb)], sem=f"Vh{hb}")

                inproj(0)
                for h in range(8):
                        hb = h % 2
                        kT = kTs[hb]
                        Vh = Vhs[hb]
                        qblocks = [(0, 512, list(range(18))), (512, 512, list(range(18)))]
                        if has_ctx:
                            qblocks.append((1024, 256, [0, 1]))
                        for qbi, (c0, n, kts) in enumerate(qblocks):
                            if qbi == 1 and h + 1 < 8:
                                inproj(h + 1)
                            nqs = n // 128
                            items = [(c, ki, kt) for c in range(2) for ki, kt in enumerate(kts)]

                            def emit_pv(it, pi):
                                c, ki, kt = it
                                for qs in range(nqs):
                                    self.mm(B[4 + qs][:, 0:129], PT[pi][:, qs * 128:(qs + 1) * 128], Vh[:, kt, 0:129],
                                            ki == 0, ki == len(kts) - 1, r=[(L + "PT", pi), (L + "Vh", hb)],
                                            w=[("ps", 4 + qs)])
                                if ki == len(kts) - 1:
                                    if pendA[0] is not None:
                                        pendA[0](); pendA[0] = None
                                    for qs in range(nqs):
                                        self.cp("dve", Osb[:, c, qs, 0:129], B[4 + qs][:, 0:129], r=[("ps", 4 + qs)],
                                                w=[(L + "Osb", c, qs)])
                            pendq = []
                            for it in items:
                                c, ki, kt = it
                                sb_ = 1 + (sci % 3); sci += 1
                                self.mm(B[sb_][:, :n], kT[:, kt * 128:(kt + 1) * 128],
                                        qm[hb][c][:, c0:c0 + n], True, True,
                                        r=[(L + "kT", hb), (L + "qT", hb, c0)], w=[("ps", sb_)])
                                pi = ptc % NPT; ptc += 1
                                self.act(PT[pi][:, :n], B[sb_][:, :n], AF.Exp, r=[("ps", sb_)], w=[(L + "PT", pi)],
                                         scale=0.125)
                                if len(pendq) == 2:
                                    emit_pv(*pendq.pop(0))
                                pendq.append((it, pi))
                                since[0] += 1
                                if since[0] == 3 and pendA[0] is not None:
                                    pendA[0](); pendA[0] = None
                                if since[0] == 10 and pendB[0] is not None:
                                    if pendA[0] is not None:
                                        pendA[0](); pendA[0] = None
                                    pendB[0](); pendB[0] = None
                            while pendq:
                                emit_pv(*pendq.pop(0))
                            def make_epi(h=h, hb=hb, c0=c0, nqs=nqs):
                                def epiA():
                                    Q = range(nqs)
                                    okr = lambda qs: [(L + "Osb", 0, qs), (L + "Osb", 1, qs)]
                                    k = lambda nm, qs: (L + "osm_" + nm, qs)
                                    for qs in Q:
                                        self.recip(osm[:, qs, 0:1], Osb[:, 0, qs, 128:129], r=okr(qs) + [L + "osm"], w=[k("rz0", qs)])
                                    for qs in Q:
                                        self.recip(osm[:, qs, 1:2], Osb[:, 1, qs, 128:129], r=okr(qs) + [L + "osm"], w=[k("rz1", qs)])
                                    for qs in Q:
                                        self.tt("dve", osm[:, qs, 2:3], osm[:, qs, 1:2], neglam, ALU.mult,
                                                r=[k("rz1", qs), L + "neglam"], w=[k("nl", qs)])
                                    for qs in Q:
                                        self.tsm("dve", o_sb[qs][:], Osb[:, 0, qs, 0:128], osm[:, qs, 0:1], r=okr(qs) + [k("rz0", qs)],
                                                 w=[(L + "o_sb", qs)])
                                    for qs in Q:
                                        self.stt("dve", o_sb[qs][:], Osb[:, 1, qs, 0:128], osm[:, qs, 2:3], o_sb[qs][:],
                                                 ALU.mult, ALU.add, r=okr(qs) + [k("nl", qs), (L + "o_sb", qs)], w=[(L + "o_sb", qs)])
                                    for qs in Q:
                                        self.stt("dve", junko[:, qs * 32:qs * 32 + 32].bitcast(F32) if False else junko4[qs][:], o_sb[qs][:], 1.0, o_sb[qs][:], ALU.mult, ALU.mult,
                                                 r=[(L + "o_sb", qs), L + "osm"], w=[(L + "junko", qs), k("ss", qs)], accum_out=osm[:, qs, 3:4])
                                    for qs in Q:
                                        self.ts("dve", osm[:, qs, 4:5], osm[:, qs, 3:4], 1.0 / 128, EPS, ALU.mult, ALU.add,
                                                r=[k("ss", qs)], w=[k("ms", qs)])
                                    for qs in Q:
                                        self.act(osm[:, qs, 5:6], osm[:, qs, 4:5], AF.Ln, r=[k("ms", qs)], w=[k("ln", qs)])
                                    for qs in Q:
                                        self.act(osm[:, qs, 6:7], osm[:, qs, 5:6], AF.Exp, r=[k("ln", qs)], w=[k("rstd", qs)], scale=-0.5)
                                    for qs in Q:
                                        self.tsm("dve", on_b[qs][:], o_sb[qs][:], osm[:, qs, 6:7], r=[(L + "o_sb", qs), k("rstd", qs)],
                                                 w=[(L + "on_b", qs)])

                                def epiB():
                                    nonlocal gci
                                    for qs in range(nqs):
                                        tcol = c0 + qs * 128
                                        bt = 0
                                        pv = B[bt][:].bitcast(BF16)
                                        self.tr(pv[:, (qs % 4) * 128:(qs % 4 + 1) * 128], on_b[qs][:], self.ident_b[:], r=[(L + "on_b", qs), "ident_b"],
                                                w=[("ps", bt)])
                                        self.stt("dve", ytv(h, tcol, 128), pv[:, (qs % 4) * 128:(qs % 4 + 1) * 128], subg2, sgT[hb][:, tcol:tcol + 128],
                                                 ALU.mult, ALU.mult,
                                                 r=[("ps", bt), L + "subg2", (L + "sgT", hb, (tcol // 512) * 512 if tcol < 1024 else 1024)],
                                                 w=ytk(h, tcol, 128))
                                return epiA, epiB
                            if pendA[0] is not None:
                                pendA[0](); pendA[0] = None
                            if pendB[0] is not None:
                                pendB[0](); pendB[0] = None
                            eA, eB = make_epi()
                            pendA[0] = eA
                            pendB[0] = eB
                            since[0] = 0
                if pendA[0] is not None:
                    pendA[0](); pendA[0] = None
                if pendB[0] is not None:
                    pendB[0](); pendB[0] = None
                P.barrier()
            if self.debug and not passB:
                dy = self.outp(f"dbg_yT{l}", [128, 16, 1024], BF16)
                self.dma("sp", dy, hxA[:, :, 1280:2304], r=[("yT", kc, t) for kc in range(16) for t in range(8)],
                         w=[f"dbg_yT{l}"], sem="dbg2")
                P.barrier()
                self.final_keys.add(f"dbg_yT{l}")
            if self.stop == "ATT":
                return

            with self.scope() as pes:
                nr = 2 if has_ctx else 1
                gate = [self.sb(pes, [128, D], F32) for _ in range(nr)]
                bm = [self.sb(pes, [128, 512], F32) for _ in range(2)]
                s_f = self.sb(pes, [128, 32], F32)
                srep = self.sb(pes, [128, 32, 128], BF16)
                xs = [self.sb(pes, [128, 512], F32) for _ in range(3)]
                xo = [self.sb(pes, [128, 512], F32) for _ in range(3)]
                self.act(s_f[:], self.cT[:], AF.Silu, r=["cT"], w=[L + "s_f2"])
                for j in range(16 * nr):
                    self.tsm("dve", srep[:, j, :], self.ones_b[:], s_f[:, j:j + 1], r=["ones_b", L + "s_f2"],
                             w=[(L + "srep2", j)])
                for g in range(8, 12):
                    i = self.wload(wmod, g * 512)
                    self.dma("sp", bm[g % 2][:], bmod[0:1, g * 512:(g + 1) * 512].partition_broadcast(128),
                             r=[], w=[(L + "bm2", g % 2)], sem=f"bmo{g%2}")
                    for r_ in range(nr):
                        bi = self.bank()
                        self.mm16(bi, 512, lambda kc: srep[:, r_ * 16 + kc, :], lambda kc: self.wb[i][:, kc, :],
                                  r=[("wb", i)] + [(L + "srep2", r_ * 16 + kc) for kc in range(16)])
                        self.tt("dve", gate[r_][:, (g - 8) * 512:(g - 7) * 512], B[bi][:], bm[g % 2][:], ALU.add,
                                r=[("ps", bi), (L + "bm2", g % 2)], w=[(L + "gate", r_, g - 8)])
                tiles = [(t, False) for t in range(8)] + ([(8, True), (9, True)] if has_ctx else [])
                cnt = 0
                for gw in range(4):
                    i = self.wload(wout, gw * 512)
                    for (t, isc) in tiles:
                        q = cnt % 3
                        cnt += 1
                        if isc:
                            tt_ = t - 8
                            sap = src["ctx"][tt_ * 128:(tt_ + 1) * 128, gw * 512:(gw + 1) * 512]
                            dap = cdst[tt_ * 128:(tt_ + 1) * 128, gw * 512:(gw + 1) * 512]
                            dk = ("cdst", tt_)
                            r_ = 1
                        else:
                            sap = src["own"][t * 128:(t + 1) * 128, gw * 512:(gw + 1) * 512]
                            dap = xdst[t * 128:(t + 1) * 128, gw * 512:(gw + 1) * 512]
                            dk = (xkey, t)
                            r_ = 0
                        self.dma("act", xs[q][:], sap, r=[], w=[(L + "xs", q)], sem=f"xs{q}")
                        bi = self.bank()
                        c0 = t * 128
                        self.mm16(bi, 512, lambda kc: ytv(kc, c0, 128), lambda kc: self.wb[i][:, kc, :],
                                  r=[("wb", i)] + [("yT", kc, t) for kc in range(16)])
                        self.tt("dve", xo[q][:], B[bi][:], gate[r_][:, gw * 512:(gw + 1) * 512], ALU.mult,
                                r=[("ps", bi), (L + "gate", r_, gw)], w=[(L + "xo", q)])
                        self.tt("dve", xo[q][:], xo[q][:], xs[q][:], ALU.add, r=[(L + "xo", q), (L + "xs", q)],
                                w=[(L + "xo", q)])
                        self.dma("sp", dap, xo[q][:], r=[(L + "xo", q)], w=[dk], sem=f"xo{q}")
                P.barrier()
            if not last:
                if not self.fused:
                    self.final_keys |= {("xdst", t) for t in range(8)} | {("cdst", t) for t in range(2)}
            else:
                with self.scope() as pes:
                    fg = self.sb(pes, [128, D], F32)
                    xt = [self.sb(pes, [128, D], F32) for _ in range(2)]
                    yo = [self.sb(pes, [128, D], F32) for _ in range(2)]
                    junkb = self.sb(pes, [128, D], BF16)
                    st = self.sb(pes, [128, 4, 8], F32)
                    self.memset("dve", st[:], 0.0, w=[L + "fst"])
                    self.dma("sp", fg[:], fg_d.partition_broadcast(128), r=[], w=[L + "fg"], sem="fg")
                    for t in range(8):
                        q = t % 2
                        self.dma("sp", xt[q][:], xdst[t * 128:(t + 1) * 128, :], r=[(xkey, t)], w=[(L + "fxt", q)],
                                 sem=f"fxt{q}")
                        self.act(junkb[:], xt[q][:], AF.Square, r=[(L + "fxt", q), L + "fst"],
                                 w=[L + "fjunk", (L + "fss", t)], accum_out=st[:, 0, t:t + 1])
                        self.ts("dve", st[:, 1, t:t + 1], st[:, 0, t:t + 1], 1.0 / D, EPS, ALU.mult, ALU.add,
                                r=[(L + "fss", t)], w=[(L + "fms", t)])
                        self.act(st[:, 2, t:t + 1], st[:, 1, t:t + 1], AF.Sqrt, r=[(L + "fms", t)], w=[(L + "fsd", t)])
                        self.recip(st[:, 3, t:t + 1], st[:, 2, t:t + 1], r=[(L + "fsd", t)], w=[(L + "frs", t)])
                        self.stt("dve", yo[q][:], xt[q][:], st[:, 3, t:t + 1], fg[:], ALU.mult, ALU.mult,
                                 r=[(L + "fxt", q), (L + "frs", t), L + "fg"], w=[(L + "yo", q)])
                        self.dma("sp", ydst[t * 128:(t + 1) * 128, :], yo[q][:], r=[(L + "yo", q)], w=[("ydst", t)],
                                 sem=f"yo{q}")
                    self.final_keys |= {("ydst", t) for t in range(8)}
                    P.barrier()


_CACHE = {}


def _get(layers, fused, debug=False, stop=None):
    key = (tuple(layers), fused, debug, stop)
    if key not in _CACHE:
        b = Builder(list(layers), fused, debug, stop)
        nc = b.build()
        _CACHE[key] = (b, nc)
    return _CACHE[key]


def _col(v, n):
    return np.ascontiguousarray(np.asarray(v, np.float32).reshape(n, 128).T)


def _rope_tables():
    n_freq = 16
    inv = (np.float32(10000.0) ** (-np.arange(n_freq, dtype=np.float32) / np.float32(n_freq))).astype(np.float32)
    t = np.arange(2048)
    row = (t // 64).astype(np.float32)
    col = (t % 64).astype(np.float32)
    cos = np.zeros((128, 2048), np.float32)
    sin = np.zeros((128, 2048), np.float32)
    for p in range(128):
        d = p % 64
        pos = row if d < 32 else col
        j = (d % 32) % 16
        ang = (pos * inv[j]).astype(np.float32)
        cos[p] = np.cos(ang)
        sin[p] = np.sin(ang)
    return cos, sin


def _rotm():
    m = np.zeros((128, 128), np.float32)
    for mm_ in range(128):
        if (mm_ % 32) < 16:
            m[mm_ + 16, mm_] = -1.0
        else:
            m[mm_ - 16, mm_] = 1.0
    return m


def _layer_inputs(l, inp):
    d = {}
    d[f"wmod{l}"] = np.ascontiguousarray(inp["w_mod"][l], np.float32)
    d[f"bmod{l}"] = np.ascontiguousarray(inp["b_mod"][l], np.float32).reshape(1, -1)
    d[f"ng{l}"] = _col(inp["norm_g"][l], 16)
    d[f"win{l}"] = np.ascontiguousarray(inp["w_in"][l], np.float32)
    d[f"lam{l}"] = np.concatenate([inp["lambda_q1"][l], inp["lambda_k1"][l], inp["lambda_q2"][l],
                                   inp["lambda_k2"][l]]).astype(np.float32).reshape(1, 256)
    d[f"subg{l}"] = np.ascontiguousarray(np.asarray(inp["subln_g"][l], np.float32).reshape(128, 1))
    d[f"wpool{l}"] = np.ascontiguousarray(inp["w_pool"][l], np.float32)
    d[f"pscale{l}"] = _col(inp["pool_scale"][l], 4)
    wdw = np.asarray(inp["w_dw"][l], np.float32)
    d[f"wdw{l}"] = np.ascontiguousarray(wdw.T.reshape(4, 128, 31).transpose(1, 0, 2))
    d[f"cvec{l}"] = np.ascontiguousarray(np.concatenate(
        [_col(inp["b_dw"][l], 4), _col(inp["conv_ln_g"][l], 4), _col(inp["conv_ln_b"][l], 4)], axis=1))
    d[f"wpw2{l}"] = np.ascontiguousarray(inp["w_pw2"][l], np.float32)
    d[f"wout{l}"] = np.ascontiguousarray(inp["w_out"][l], np.float32)
    if l == 1:
        d["fg"] = np.asarray(inp["final_g"], np.float32).reshape(1, -1)
    return d


def _core_consts(core, inp):
    b, h = core // 2, core % 2
    cos, sin = _rope_tables()
    order = np.concatenate([np.arange(h * 1024, (h + 1) * 1024), np.arange((1 - h) * 1024, (2 - h) * 1024)])
    d = {}
    d["ident"] = np.eye(128, dtype=np.float32)
    d["rotm"] = _rotm()
    d["cosk"] = np.ascontiguousarray(cos[:, order])
    d["sink"] = np.ascontiguousarray(sin[:, order])
    m = np.zeros((128, 4), np.float32)
    m[:, 0] = 1.0 if h == 1 else 0.0
    m[:, 1] = 1.0 if h == 0 else 0.0
    m[:, 2] = 1.0 if h == 1 else 0.0
    m[:, 3] = 1.0 if h == 0 else 0.0
    d["msk"] = m
    cT = np.concatenate([_col(inp["c"][b], 16), _col(inp["c_ctx"], 16)], axis=1)
    d["cT"] = np.ascontiguousarray(cT)
    return d


def _run(layers, fused, inp, x, ctx, debug=False, stop=None):
    b_, nc = _get(layers, fused, debug, stop)
    in_maps = []
    ncr = int(os.environ.get("DBG_NCORES", NCORES)) if debug else NCORES
    for core in range(ncr):
        b, h = core // 2, core % 2
        d = _core_consts(core, inp)
        for l in layers:
            d.update(_layer_inputs(l, inp))
        d["x_own"] = np.ascontiguousarray(x[b, h * 1024:(h + 1) * 1024])
        d["x_oth"] = np.ascontiguousarray(x[b, (1 - h) * 1024:(2 - h) * 1024])
        d["ctx_in"] = np.ascontiguousarray(ctx[b])
        in_maps.append(d)
    res = run_bass_kernel_spmd(nc, in_maps, core_ids=list(range(ncr)))
    return res.results


FUSED = True


def kernel(**inp):
    inp = {k: np.asarray(v) for k, v in inp.items()}
    x = np.asarray(inp["x"], np.float32)
    ctx = np.asarray(inp["ctx"], np.float32)
    if FUSED:
        res = _run([0, 1], True, inp, x, ctx)
    else:
        r0 = _run([0], False, inp, x, ctx)
        x1 = np.empty_like(x)
        ctx1 = np.empty_like(ctx)
        for core in range(NCORES):
            b, h = core // 2, core % 2
            x1[b, h * 1024:(h + 1) * 1024] = r0[core]["x1_own"]
            if h == 0:
                ctx1[b] = r0[core]["ctx1"]
        res = _run([1], False, inp, x1, ctx1)
    out = np.empty((4, 2048, 2048), np.float32)
    for core in range(NCORES):
        b, h = core // 2, core % 2
        out[b, h * 1024:(h + 1) * 1024] = res[core]["y"]
    return out
```
